# Optimizing a Trainium2 kernel written in Bass

```python
import math
import jax, jax.numpy as jnp
from jax import lax
import numpy as np

D_MODEL = 2048
BATCH = 1
SEQ = 8192
DEPTH = 2

N_BRANCH = 4
BR_WIDTH = 512
LRU_BLOCKS = 8
LRU_BLOCK = BR_WIDTH // LRU_BLOCKS
LRU_CONV = 4
LRU_C = 8.0
ATT_GROUPS = ((128, 1), (512, 4), (2048, 16))
ATT_HEADS_PER_GROUP = 4
ATT_HEAD_DIM = BR_WIDTH // ATT_HEADS_PER_GROUP
ATT_HEADS = len(ATT_GROUPS) * ATT_HEADS_PER_GROUP
ATT_QKV = ATT_HEADS * ATT_HEAD_DIM
ATT_SPAN = 128
N_BUCKETS = 32
MAX_DISTANCE = 2048
NEG_INF = -1e30
RWKV_HEAD = 64
RWKV_HEADS = BR_WIDTH // RWKV_HEAD
DECAY_RANK = 64
ICLR_RANK = 64
RWKV_GN_EPS = 64e-5
RWKV_SHIFT_W = 3 * BR_WIDTH + DECAY_RANK + ICLR_RANK
CONF_KERNEL = 31
LN_EPS = 1e-5
ALPHA = (2.0 * DEPTH) ** 0.25
BETA = (8.0 * DEPTH) ** -0.25

IN_SPLITS = (
    BR_WIDTH, BR_WIDTH,
    ATT_QKV, ATT_QKV, ATT_QKV, BR_WIDTH,
    BR_WIDTH, BR_WIDTH, BR_WIDTH, DECAY_RANK, ICLR_RANK, BR_WIDTH,
    BR_WIDTH, BR_WIDTH, BR_WIDTH,
    N_BRANCH * D_MODEL,
)
D_IN = sum(IN_SPLITS)

kernel_name = 'hybrid_rglru_dilattn_rwkv7_conformer_deepnorm'


def _layernorm(x, g, b, eps=LN_EPS):
    xf = x.astype(jnp.float32)
    mu = xf.mean(-1, keepdims=True)
    var = jnp.square(xf - mu).mean(-1, keepdims=True)
    return ((xf - mu) * lax.rsqrt(var + eps) * g + b).astype(x.dtype)


def _causal_depthwise_conv(x, w, b):
    width, ch = w.shape
    y = lax.conv_general_dilated(x, w[:, None, :], window_strides=(1,), padding=[(width - 1, 0)],
                                 dimension_numbers=('NWC', 'WIO', 'NWC'), feature_group_count=ch)
    return y + b


def _linear_scan(a, b):
    def op(c1, c2):
        a1, b1 = c1
        a2, b2 = c2
        return a1 * a2, a2 * b1 + b2
    _, h = lax.associative_scan(op, (a, b), axis=1)
    return h


def _rwkv7_scan(r, w, k, v, a, b):
    bsz, _, nh, n = r.shape
    def step(state, inp):
        r_t, w_t, k_t, v_t, a_t, b_t = inp
        sa = jnp.einsum('bhij,bhj->bhi', state, a_t)
        state = state * w_t[:, :, None, :] + sa[..., :, None] * b_t[:, :, None, :] + v_t[..., :, None] * k_t[:, :, None, :]
        return state, jnp.einsum('bhij,bhj->bhi', state, r_t)
    xs = tuple(jnp.moveaxis(t, 1, 0) for t in (r, w, k, v, a, b))
    s0 = jnp.zeros((bsz, nh, n, n), jnp.float32)
    _, ys = lax.scan(step, s0, xs)
    return jnp.moveaxis(ys, 0, 1)


def _t5_bucket(dist):
    max_exact = N_BUCKETS // 2
    large = max_exact + (np.log(np.maximum(dist, 1) / max_exact) / math.log(MAX_DISTANCE / max_exact)
                         * (N_BUCKETS - max_exact)).astype(np.int32)
    large = np.minimum(large, N_BUCKETS - 1)
    return np.where(dist < max_exact, dist, large).astype(np.int32)


def _group_bias(table, dilation):
    qi = np.arange(ATT_SPAN)[:, None]
    kj = np.arange(2 * ATT_SPAN)[None, :]
    dist = qi + ATT_SPAN - kj
    valid = (dist >= 0) & (dist <= ATT_SPAN)
    bucket = _t5_bucket(np.clip(dist, 0, ATT_SPAN) * dilation)
    bias = jnp.where(valid[..., None], table[bucket].astype(jnp.float32), NEG_INF)
    return jnp.transpose(bias, (0, 2, 1))


def _dilated_window_attention(q, k, v, bias, dilation):
    bsz, s, h, dh = q.shape
    m_rows = s // dilation
    nb = -(-m_rows // ATT_SPAN)
    mp = nb * ATT_SPAN

    def to_res(t):
        t = jnp.transpose(t.reshape(bsz, m_rows, dilation, h, dh), (0, 2, 1, 3, 4))
        t = jnp.pad(t, ((0, 0), (0, 0), (0, mp - m_rows), (0, 0), (0, 0)))
        return t.reshape(bsz, dilation, nb, ATT_SPAN, h, dh)

    def with_prev(t):
        prev = jnp.pad(t, ((0, 0), (0, 0), (1, 0), (0, 0), (0, 0), (0, 0)))[:, :, :-1]
        return jnp.concatenate([prev, t], axis=3)

    qb = to_res(q).astype(jnp.float32)
    kw = with_prev(to_res(k)).astype(jnp.float32)
    vw = with_prev(to_res(v)).astype(jnp.float32)
    logits = jnp.einsum('brnqhd,brnkhd->brnqhk', qb, kw) * (dh ** -0.5) + bias
    first = (np.arange(nb)[:, None] == 0) & (np.arange(2 * ATT_SPAN)[None, :] < ATT_SPAN)
    logits = jnp.where(first[:, None, None, :], NEG_INF, logits)
    m = logits.max(-1)
    p = jnp.exp(logits - m[..., None])
    den = p.sum(-1)
    o = jnp.einsum('brnqhk,brnkhd->brnqhd', p, vw) / den[..., None]

    def from_res(t):
        t = t.reshape((bsz, dilation, mp) + t.shape[4:])[:, :, :m_rows]
        t = jnp.moveaxis(t, 1, 2)
        return t.reshape((bsz, s) + t.shape[3:])

    return from_res(o), from_res(m), from_res(den)


def _hybrid_layer(x, rel_bias, w_in, b_in, lru_conv_w, lru_conv_b, lru_gate_a_w, lru_gate_a_b,
                  lru_gate_x_w, lru_gate_x_b, lru_lambda, rwkv_mu, rwkv_w0, rwkv_w_up, rwkv_a0, rwkv_a_up,
                  rwkv_k_k, rwkv_k_a, rwkv_r_k, rwkv_gn_g, rwkv_gn_b, conf_dw_w, conf_dw_b, conf_ln_g,
                  conf_ln_b, w_br, w_out, ln_g, ln_b):
    bsz, s, _ = x.shape
    dt = x.dtype
    f32 = jnp.float32
    split_points = np.cumsum(IN_SPLITS)[:-1].tolist()
    h = jnp.einsum('bsd,de->bse', x, w_in) + b_in
    (a_x, a_gate, q, k, v, b_gate, c_r, c_k, c_v, c_wd, c_ad, c_gate,
     d_val, d_glu, d_gate, merge_logits) = jnp.split(h, split_points, axis=-1)

    u = _causal_depthwise_conv(a_x, lru_conv_w, lru_conv_b)
    ub = u.reshape(bsz, s, LRU_BLOCKS, LRU_BLOCK)
    gate_r = jax.nn.sigmoid(jnp.einsum('bsgi,gij->bsgj', ub, lru_gate_a_w).reshape(bsz, s, BR_WIDTH) + lru_gate_a_b)
    gate_i = jax.nn.sigmoid(jnp.einsum('bsgi,gij->bsgj', ub, lru_gate_x_w).reshape(bsz, s, BR_WIDTH) + lru_gate_x_b)
    log_a = -LRU_C * gate_r.astype(f32) * jax.nn.softplus(-lru_lambda.astype(f32))
    a_t = jnp.exp(log_a)
    mult = jnp.sqrt(-jnp.expm1(2.0 * log_a))
    y_a = _linear_scan(a_t, mult * (gate_i * u).astype(f32)).astype(dt)

    qh = q.reshape(bsz, s, ATT_HEADS, ATT_HEAD_DIM)
    kh = k.reshape(bsz, s, ATT_HEADS, ATT_HEAD_DIM)
    vh = v.reshape(bsz, s, ATT_HEADS, ATT_HEAD_DIM)
    outs, maxes, dens = [], [], []
    for g, (window, dil) in enumerate(ATT_GROUPS):
        hs = slice(g * ATT_HEADS_PER_GROUP, (g + 1) * ATT_HEADS_PER_GROUP)
        bias = _group_bias(rel_bias[:, hs], dil)
        o_g, m_g, s_g = _dilated_window_attention(qh[:, :, hs], kh[:, :, hs], vh[:, :, hs], bias, dil)
        outs.append(o_g)
        maxes.append(m_g)
        dens.append(s_g)
    m_all = jnp.stack(maxes)
    wts = jnp.exp(m_all - m_all.max(0)) * jnp.stack(dens)
    y_b = (wts[..., None] * jnp.stack(outs)).sum(0) / wts.sum(0)[..., None]
    y_b = y_b.reshape(bsz, s, BR_WIDTH).astype(dt)

    c_in = jnp.concatenate([c_r, c_k, c_v, c_wd, c_ad], axis=-1)
    c_prev = jnp.pad(c_in, ((0, 0), (1, 0), (0, 0)))[:, :-1]
    c_in = c_in + rwkv_mu * (c_prev - c_in)
    r, kx, vv, wd, ad = jnp.split(c_in, [BR_WIDTH, 2 * BR_WIDTH, 3 * BR_WIDTH, 3 * BR_WIDTH + DECAY_RANK], axis=-1)
    w_log = -jax.nn.softplus(-(rwkv_w0 + jnp.tanh(wd) @ rwkv_w_up).astype(f32)) - 0.5
    decay = jnp.exp(-jnp.exp(w_log))
    a_icl = jax.nn.sigmoid((rwkv_a0 + ad @ rwkv_a_up).astype(f32))

    def heads(t):
        return t.reshape(bsz, s, RWKV_HEADS, RWKV_HEAD)

    kxf = kx.astype(f32)
    kk = heads(kxf * rwkv_k_k)
    kk = kk / jnp.maximum(jnp.sqrt(jnp.sum(kk * kk, -1, keepdims=True)), 1e-12)
    kc = heads(kxf * (1.0 + (a_icl - 1.0) * rwkv_k_a))
    rh = heads(r.astype(f32))
    vf = heads(vv.astype(f32))
    wy = _rwkv7_scan(rh, heads(decay), kc, vf, -kk, kk * heads(a_icl))
    mu = wy.mean(-1, keepdims=True)
    var = jnp.square(wy - mu).mean(-1, keepdims=True)
    wy = ((wy - mu) * lax.rsqrt(var + RWKV_GN_EPS)).reshape(bsz, s, BR_WIDTH) * rwkv_gn_g + rwkv_gn_b
    bonus = jnp.sum(rh * kc * rwkv_r_k, -1, keepdims=True) * vf
    y_c = (wy + bonus.reshape(bsz, s, BR_WIDTH)).astype(dt)

    cu = d_val * jax.nn.sigmoid(d_glu)
    cu = _causal_depthwise_conv(cu, conf_dw_w, conf_dw_b)
    y_d = jax.nn.silu(_layernorm(cu, conf_ln_g, conf_ln_b))

    merge_g = jax.nn.sigmoid(merge_logits).reshape(bsz, s, N_BRANCH, D_MODEL)
    ys = (y_a, y_b, y_c, y_d)
    gates = (a_gate, b_gate, c_gate, d_gate)
    mixed = jnp.zeros_like(x)
    for n in range(N_BRANCH):
        mixed = mixed + merge_g[:, :, n] * jnp.einsum('bsc,cd->bsd', ys[n] * jax.nn.silu(gates[n]), w_br[n])
    out = jnp.einsum('bsd,de->bse', mixed, w_out)
    return _layernorm(ALPHA * x + out, ln_g, ln_b)


def setup_inputs(seed: int = 0) -> dict:
    key = jax.random.key(seed)
    ks = jax.random.split(key, 32)
    f32 = jnp.float32

    def nrm(k, shape, scale):
        return jax.random.normal(k, shape, f32) * scale

    def unif(k, shape, lo, hi):
        return jax.random.uniform(k, shape, f32, lo, hi)

    a_target = unif(ks[10], (DEPTH, BR_WIDTH), 0.9, 0.999)
    s_lam = a_target ** (1.0 / LRU_C)
    return {
        'x': nrm(ks[0], (BATCH, SEQ, D_MODEL), 1.0),
        'att_rel_bias': nrm(ks[1], (N_BUCKETS, ATT_HEADS), 0.1),
        'w_in': nrm(ks[2], (DEPTH, D_MODEL, D_IN), D_MODEL ** -0.5),
        'b_in': nrm(ks[3], (DEPTH, D_IN), 0.02),
        'lru_conv_w': nrm(ks[4], (DEPTH, LRU_CONV, BR_WIDTH), LRU_CONV ** -0.5),
        'lru_conv_b': nrm(ks[5], (DEPTH, BR_WIDTH), 0.02),
        'lru_gate_a_w': nrm(ks[6], (DEPTH, LRU_BLOCKS, LRU_BLOCK, LRU_BLOCK), LRU_BLOCK ** -0.5),
        'lru_gate_a_b': nrm(ks[7], (DEPTH, BR_WIDTH), 0.02),
        'lru_gate_x_w': nrm(ks[8], (DEPTH, LRU_BLOCKS, LRU_BLOCK, LRU_BLOCK), LRU_BLOCK ** -0.5),
        'lru_gate_x_b': nrm(ks[9], (DEPTH, BR_WIDTH), 0.02),
        'lru_lambda': jnp.log(s_lam) - jnp.log1p(-s_lam),
        'rwkv_mu': unif(ks[11], (DEPTH, RWKV_SHIFT_W), 0.0, 1.0),
        'rwkv_w0': jnp.linspace(-6.5, -1.5, BR_WIDTH, dtype=f32)[None, :] + nrm(ks[12], (DEPTH, BR_WIDTH), 0.1),
        'rwkv_w_up': nrm(ks[13], (DEPTH, DECAY_RANK, BR_WIDTH), 0.5 * DECAY_RANK ** -0.5),
        'rwkv_a0': nrm(ks[14], (DEPTH, BR_WIDTH), 0.1),
        'rwkv_a_up': nrm(ks[15], (DEPTH, ICLR_RANK, BR_WIDTH), 0.5 * ICLR_RANK ** -0.5),
        'rwkv_k_k': 0.85 + nrm(ks[16], (DEPTH, BR_WIDTH), 0.02),
        'rwkv_k_a': 1.0 + nrm(ks[17], (DEPTH, BR_WIDTH), 0.02),
        'rwkv_r_k': nrm(ks[18], (DEPTH, RWKV_HEADS, RWKV_HEAD), 0.1),
        'rwkv_gn_g': 1.0 + nrm(ks[19], (DEPTH, BR_WIDTH), 0.02),
        'rwkv_gn_b': nrm(ks[20], (DEPTH, BR_WIDTH), 0.02),
        'conf_dw_w': nrm(ks[21], (DEPTH, CONF_KERNEL, BR_WIDTH), CONF_KERNEL ** -0.5),
        'conf_dw_b': nrm(ks[22], (DEPTH, BR_WIDTH), 0.02),
        'conf_ln_g': 1.0 + nrm(ks[23], (DEPTH, BR_WIDTH), 0.02),
        'conf_ln_b': nrm(ks[24], (DEPTH, BR_WIDTH), 0.02),
        'w_br': nrm(ks[25], (DEPTH, N_BRANCH, BR_WIDTH, D_MODEL), BR_WIDTH ** -0.5),
        'w_out': nrm(ks[26], (DEPTH, D_MODEL, D_MODEL), BETA * D_MODEL ** -0.5),
        'ln_g': 1.0 + nrm(ks[27], (DEPTH, D_MODEL), 0.02),
        'ln_b': nrm(ks[28], (DEPTH, D_MODEL), 0.02),
    }


def reference(x, att_rel_bias, w_in, b_in, lru_conv_w, lru_conv_b, lru_gate_a_w, lru_gate_a_b,
              lru_gate_x_w, lru_gate_x_b, lru_lambda, rwkv_mu, rwkv_w0, rwkv_w_up, rwkv_a0, rwkv_a_up,
              rwkv_k_k, rwkv_k_a, rwkv_r_k, rwkv_gn_g, rwkv_gn_b, conf_dw_w, conf_dw_b, conf_ln_g,
              conf_ln_b, w_br, w_out, ln_g, ln_b):
    for l in range(DEPTH):
        x = _hybrid_layer(x, att_rel_bias, w_in[l], b_in[l], lru_conv_w[l], lru_conv_b[l], lru_gate_a_w[l],
                          lru_gate_a_b[l], lru_gate_x_w[l], lru_gate_x_b[l], lru_lambda[l], rwkv_mu[l],
                          rwkv_w0[l], rwkv_w_up[l], rwkv_a0[l], rwkv_a_up[l], rwkv_k_k[l], rwkv_k_a[l],
                          rwkv_r_k[l], rwkv_gn_g[l], rwkv_gn_b[l], conf_dw_w[l], conf_dw_b[l], conf_ln_g[l],
                          conf_ln_b[l], w_br[l], w_out[l], ln_g[l], ln_b[l])
    return x
```

```python
import contextlib
import numpy as np
import concourse.bass as bass
import concourse.mybir as mybir

F32 = mybir.dt.float32
BF16 = mybir.dt.bfloat16
AF = mybir.ActivationFunctionType
ALU = mybir.AluOpType
AX = mybir.AxisListType


class Sched:
    ENGS = ("pe", "dve", "act", "pool", "sp")

    def __init__(self, nc, stack, n_lanes=24, n_sw_lanes=16):
        self.nc = nc
        self.h = {"pe": nc.tensor, "dve": nc.vector, "act": nc.scalar,
                  "pool": nc.gpsimd, "sp": nc.sync}
        self.sem = {e: stack.enter_context(nc.semaphore("s_" + e)) for e in self.ENGS}
        self.cnt = {e: 0 for e in self.ENGS}
        self.n_hw = n_lanes
        self.lanes = [stack.enter_context(nc.semaphore("l%d" % i)) for i in range(n_lanes)]
        self.lanes += [stack.enter_context(nc.semaphore("w%d" % i)) for i in range(n_sw_lanes)]
        self.lane_tot = [0] * (n_lanes + n_sw_lanes)
        self.lane_next = 0
        self.sw_next = 0
        self.prog = {e: [] for e in self.ENGS}
        self.seen = {e: {} for e in self.ENGS}
        self.lastw = {}
        self.readers = {}
        self.semobj = {}
        self.unread = set()

    def _need(self, e, tok, waits):
        if tok is None:
            return
        k, v = tok
        if e == "pe" and k == ("e", "pe"):
            return
        if self.seen[e].get(k, 0) >= v:
            return
        self.seen[e][k] = v
        waits.append((k, v))

    def _semof(self, k):
        return self.sem[k[1]] if k[0] == "e" else self.lanes[k[1]]

    def _deps(self, e, reads, writes):
        waits = []
        for r in reads:
            self._need(e, self.lastw.get(r), waits)
        for w in writes:
            self._need(e, self.lastw.get(w), waits)
            for t in self.readers.get(w, ()):
                self._need(e, t, waits)
        return waits

    def _commit(self, tok, reads, writes):
        for r in reads:
            self.unread.discard(r)
            self.readers.setdefault(r, []).append(tok)
        for w in writes:
            self.lastw[w] = tok
            self.readers[w] = []

    def op(self, e, fn, reads=(), writes=()):
        waits = self._deps(e, reads, writes)
        self.cnt[e] += 1
        tok = (("e", e), self.cnt[e])
        self.prog[e].append((waits, fn, (self.sem[e], 1)))
        self._commit(tok, reads, writes)
        return tok

    def take_lane(self, q):
        if q == "pool":
            lane = self.n_hw + self.sw_next
            self.sw_next = (self.sw_next + 1) % (len(self.lanes) - self.n_hw)
        else:
            lane = self.lane_next
            self.lane_next = (self.lane_next + 1) % self.n_hw
        return lane

    def dma(self, q, out, in_, reads=(), writes=(), **kw):
        lane = self.take_lane(q)
        waits = self._deps(q, reads, writes)
        self._need(q, (("l", lane), self.lane_tot[lane]) if self.lane_tot[lane] else None, waits)
        self.lane_tot[lane] += 16
        tok = (("l", lane), self.lane_tot[lane])

        def fn(eng, out=out, in_=in_, kw=kw):
            return eng.dma_start(out=out, in_=in_, **kw)
        self.prog[q].append((waits, fn, (self.lanes[lane], 16)))
        self._commit(tok, reads, writes)
        return tok

    def wait_all(self, e, toks):
        waits = []
        for t in toks:
            self._need(e, t, waits)
        self.prog[e].append((waits, None, None))

    def barrier(self):
        toks = [(("e", e), self.cnt[e]) for e in self.ENGS if self.cnt[e]]
        toks += [(("l", i), t) for i, t in enumerate(self.lane_tot) if t]
        for e in self.ENGS:
            self.wait_all(e, toks)

    def emit(self, block):
        def mk(e):
            def body(eng):
                for waits, fn, inc in self.prog[e]:
                    for k, v in waits:
                        eng.wait_ge(self._semof(k), v)
                    if fn is not None:
                        ins = fn(eng)
                        ins.then_inc(inc[0], inc[1])
            return body
        block.tensor(mk("pe"))
        block.vector(mk("dve"))
        block.scalar(mk("act"))
        block.gpsimd(mk("pool"))
        block.sync(mk("sp"))


import contextlib
import numpy as np
import concourse.bass as bass
import concourse.mybir as mybir
from concourse.bass_utils import run_bass_kernel_spmd

F32 = mybir.dt.float32
BF16 = mybir.dt.bfloat16
AF = mybir.ActivationFunctionType
ALU = mybir.AluOpType

D_MODEL = 2048
SEQ = 8192
KCH = 16


class B:
    def __init__(self):
        self.nc = bass.Bass("TRN2", target_bir_lowering=False, num_devices=8)
        self.st = contextlib.ExitStack()
        self.S = Sched(self.nc, self.st)
        self.ps = [self.st.enter_context(self.nc.psum_tensor("ps%d" % i, [128, 512], F32)) for i in range(8)]
        self.ps_i = 0
        self.uid = 0
        self.outs = []
        self.pfx = ""

    def din(self, name, shape, dt=F32):
        return self.nc.dram_tensor(name, list(shape), dt, kind="ExternalInput").ap()

    def dout(self, name, shape, dt=F32):
        self.outs.append(name)
        return self.nc.dram_tensor(name, list(shape), dt, kind="ExternalOutput").ap()

    def sb(self, name, shape, dt=F32, st=None):
        return (st or self.st).enter_context(self.nc.sbuf_tensor("sb_" + self.pfx + name, list(shape), dt))

    def psum(self):
        i = self.ps_i
        self.ps_i = (i + 1) % 8
        if ("ps", i) in self.S.unread:
            raise RuntimeError("PSUM bank %d handed out again before its previous contents were read" % i)
        self.S.unread.add(("ps", i))
        return self.ps[i], ("ps", i)

    def key(self, p="t"):
        self.uid += 1
        return (p, self.uid)

    def finish(self, out_keys, extra=()):
        S = self.S
        S.wait_all("sp", [S.lastw[k] for k in out_keys])
        with self.nc.Block() as block:
            S.emit(block)
        for e in extra:
            e.close()
        self.st.close()
        return self.nc

    def mm(self, out, lhsT, rhs, start, stop, reads, writes):
        return self.S.op("pe", lambda e: e.matmul(out, lhsT=lhsT, rhs=rhs, start=start, stop=stop), reads, writes)

    def tr(self, out, in_, ident, reads, writes):
        return self.S.op("pe", lambda e: e.transpose(out=out, in_=in_, identity=ident), reads, writes)

    def act(self, out, in_, func, reads, writes, bias=None, scale=None, eng="act"):
        kw = {}
        if bias is not None:
            kw["bias"] = bias
        if scale is not None:
            kw["scale"] = scale
        return self.S.op("act", lambda e: e.activation(out=out, in_=in_, func=func, **kw), reads, writes)

    def tt(self, out, in0, in1, op, reads, writes, eng="dve"):
        return self.S.op(eng, lambda e: e.tensor_tensor(out=out, in0=in0, in1=in1, op=op), reads, writes)

    def ts(self, out, in0, s1, s2, op0, op1, reads, writes, eng="dve"):
        if op1 is None:
            return self.S.op(eng, lambda e: e.tensor_scalar(out=out, in0=in0, scalar1=s1, scalar2=None, op0=op0), reads, writes)
        return self.S.op(eng, lambda e: e.tensor_scalar(out=out, in0=in0, scalar1=s1, scalar2=s2, op0=op0, op1=op1), reads, writes)

    def stt(self, out, in0, scalar, in1, op0, op1, reads, writes):
        return self.S.op("dve", lambda e: e.scalar_tensor_tensor(out=out, in0=in0, scalar=scalar, in1=in1, op0=op0, op1=op1), reads, writes)

    def cp(self, out, in_, reads, writes, eng="dve"):
        if eng == "act":
            return self.S.op("act", lambda e: e.copy(out=out, in_=in_), reads, writes)
        return self.S.op(eng, lambda e: e.tensor_copy(out=out, in_=in_), reads, writes)

    def memset(self, ap, val, writes, eng="pool"):
        return self.S.op(eng, lambda e: e.memset(ap, val), (), writes)

    def load(self, sb_ap, dram_ap, key, q="sp", **kw):
        return self.S.dma(q, sb_ap, dram_ap, writes=[key], **kw)

    def _lane_op(self, q, fn, reads, writes):
        S = self.S
        lane = S.take_lane(q)
        waits = S._deps(q, reads, writes)
        S._need(q, (("l", lane), S.lane_tot[lane]) if S.lane_tot[lane] else None, waits)
        S.lane_tot[lane] += 16
        tok = (("l", lane), S.lane_tot[lane])
        S.prog[q].append((waits, fn, (S.lanes[lane], 16)))
        S._commit(tok, reads, writes)
        return tok

    def coll(self, kind, src, dst, reads, writes):
        return self._lane_op("pool", lambda e: e.collective_compute(kind, ALU.bypass, replica_groups=[list(range(8))],
                                                                    ins=[src], outs=[dst]), reads, writes)

    def gather(self, out, in_, idx, reads, writes):
        return self._lane_op("pool", lambda e: e.indirect_dma_start(out=out, out_offset=None, in_=in_,
                                                                    in_offset=bass.IndirectOffsetOnAxis(ap=idx, axis=0)), reads, writes)

    def ident(self, n=128):
        ones = self.sb("c_ones", [128, 512], F32)
        ident = self.sb("c_ident", [128, 128], F32)
        self.memset(ones[:], 1.0, ["c_ones"])
        self.S.op("pool", lambda e: e.affine_select(out=ident[:], in_=ones[:, 0:128], pattern=[[1, 128]],
                                                    compare_op=ALU.is_equal, fill=0.0, base=0, channel_multiplier=-1),
                  ["c_ones"], ["c_ident"])
        return ones, ident


def xT_tile_ap(xT, t0, n):
    return xT[:, t0:t0 + n].rearrange("(k p) t -> p k t", p=128)


def w_ap(w):
    return w.rearrange("(k p) c -> p k c", p=128)


def emit_p1a(b, st, xT, wa_d, pa_d, gw_d, yg_d, T=SEQ, SEG=512):
    S = b.S
    b.pfx = "a_"
    nseg = T // SEG

    wa = b.sb("wa", [128, KCH, 128], BF16, st=st)
    pa = b.sb("pa", [64, 16], st=st)
    gw = b.sb("gw", [64, 128], st=st)
    c1 = b.sb("c1", [64, 4], st=st)
    SUP = 4
    xb = [b.sb("xb%d" % i, [128, KCH, SUP * SEG], BF16, st=st) for i in range(2)]
    sgs = b.sb("sgs", [64, SUP * SEG], st=st)
    axb = b.sb("axb", [64, 4 * SEG + 3], st=st)
    u = b.sb("u", [64, SEG], st=st)
    gr = b.sb("gr", [64, SEG], st=st)
    gi = b.sb("gi", [64, SEG], st=st)
    at = b.sb("at", [64, SEG], st=st)
    a2 = b.sb("a2", [64, SEG], st=st)
    bt = b.sb("bt", [64, SEG], st=st)
    hb = [b.sb("hb%d" % i, [64, SEG], st=st) for i in range(2)]
    sg = b.sb("sg", [64, SEG], st=st)
    yo = [b.sb("yo%d" % i, [64, SEG], st=st) for i in range(2)]

    S.dma("pool", wa[:], w_ap(wa_d), writes=["wa"])
    b.load(pa[:], pa_d, "pa")
    b.load(gw[:], gw_d, "gw")
    b.memset(axb[:, 0:3], 0.0, ["axb"])
    b.act(c1[:, 2:3], pa[:, 9:10], AF.Exp, ["pa"], ["c1"], scale=-1.0)
    b.act(c1[:, 3:4], c1[:, 2:3], AF.Ln, ["c1"], ["c1"], bias=1.0)
    b.ts(c1[:, 0:1], c1[:, 3:4], -8.0, None, ALU.mult, None, ["c1"], ["c1"])
    b.ts(c1[:, 1:2], c1[:, 3:4], -16.0, None, ALU.mult, None, ["c1"], ["c1"])

    for s in range(nseg):
        sup, sub = s // SUP, s % SUP
        xs = xb[sup % 2]
        xk = ("xb", sup % 2)
        if sub == 0:
            S.dma("pool", xs[:], xT_tile_ap(xT, sup * SUP * SEG, SUP * SEG), writes=[xk])
            nsub = min(SUP, nseg - s)
            pa_ps = [b.psum() for _ in range(nsub)]
            for k in range(KCH):
                for q in range(nsub):
                    b.mm(pa_ps[q][0][0:64, :SEG], wa[:, k, 0:64], xs[:, k, q * SEG:(q + 1) * SEG], k == 0, k == KCH - 1, ["wa", xk], [pa_ps[q][1]])
            for q in range(nsub):
                b.act(axb[:, 3 + q * SEG:3 + (q + 1) * SEG], pa_ps[q][0][0:64, :SEG], AF.Identity, [pa_ps[q][1], "pa"], ["axb"], bias=pa[:, 0:1])
            pg_ps = [b.psum() for _ in range(nsub)]
            for k in range(KCH):
                for q in range(nsub):
                    b.mm(pg_ps[q][0][0:64, :SEG], wa[:, k, 64:128], xs[:, k, q * SEG:(q + 1) * SEG], k == 0, k == KCH - 1, ["wa", xk], [pg_ps[q][1]])
            for q in range(nsub):
                b.act(sgs[:, q * SEG:(q + 1) * SEG], pg_ps[q][0][0:64, :SEG], AF.Silu, [pg_ps[q][1], "pa"], [("sgs", q)], bias=pa[:, 1:2])
        o0 = sub * SEG
        b.ts(u[:], axb[:, o0:o0 + SEG], pa[:, 3:4], pa[:, 2:3], ALU.mult, ALU.add, ["axb", "pa"], ["u"])
        for j in range(1, 4):
            b.stt(u[:], axb[:, o0 + j:o0 + j + SEG], pa[:, 3 + j:4 + j], u[:], ALU.mult, ALU.add, ["axb", "pa", "u"], ["u"])
        if sub == SUP - 1 or s == nseg - 1:
            b.cp(axb[:, 0:3], axb[:, (sub + 1) * SEG:(sub + 1) * SEG + 3], ["axb", "u"], ["axb"], eng="pool")
        ps3, pk3 = b.psum()
        b.mm(ps3[0:64, :SEG], gw[:, 0:64], u[:], True, True, ["gw", "u"], [pk3])
        b.act(gr[:], ps3[0:64, :SEG], AF.Sigmoid, [pk3, "pa"], ["gr"], bias=pa[:, 7:8])
        ps4, pk4 = b.psum()
        b.mm(ps4[0:64, :SEG], gw[:, 64:128], u[:], True, True, ["gw", "u"], [pk4])
        b.act(gi[:], ps4[0:64, :SEG], AF.Sigmoid, [pk4, "pa"], ["gi"], bias=pa[:, 8:9])
        b.act(at[:], gr[:], AF.Exp, ["gr", "c1"], ["at"], scale=c1[:, 0:1])
        b.act(a2[:], gr[:], AF.Exp, ["gr", "c1"], ["a2"], scale=c1[:, 1:2])
        b.ts(a2[:], a2[:], -1.0, 1.0, ALU.mult, ALU.add, ["a2"], ["a2"])
        b.act(a2[:], a2[:], AF.Sqrt, ["a2"], ["a2"])
        b.tt(bt[:], gi[:], u[:], ALU.mult, ["gi", "u"], ["bt"])
        b.tt(bt[:], bt[:], a2[:], ALU.mult, ["bt", "a2"], ["bt"])
        hcur = hb[s % 2]
        hprev = hb[(s + 1) % 2]
        init = 0.0 if s == 0 else hprev[:, SEG - 1:SEG]
        S.op("dve", lambda e, hcur=hcur, init=init: e.tensor_tensor_scan(out=hcur[:], data0=at[:], data1=bt[:], initial=init,
                                                                        op0=ALU.mult, op1=ALU.add),
             ["at", "bt", ("hb", (s + 1) % 2)], [("hb", s % 2)])
        yt = yo[s % 2]
        b.tt(yt[:], hcur[:], sgs[:, sub * SEG:(sub + 1) * SEG], ALU.mult, [("hb", s % 2), ("sgs", sub)], [("yo", s % 2)], eng="pool")
        S.dma("sp", yg_d[:, s * SEG:(s + 1) * SEG], yt[:], reads=[("yo", s % 2)], writes=[("ygd", s)])
    return [("ygd", s) for s in range(nseg)]


def build_p1a(T=SEQ, SEG=512):
    b = B()
    st = contextlib.ExitStack()
    keys = emit_p1a(b, st, b.din("xT", [D_MODEL, T]), b.din("wa", [D_MODEL, 128]), b.din("pa", [64, 16]),
                    b.din("gw", [64, 128]), b.dout("ygA", [64, T]), T, SEG)
    return b.finish(keys, extra=[st]), b.outs


def host_p1a(inp, l, xT):
    sp = np.cumsum([0, 512, 512])
    w_in = inp["w_in"][l]
    b_in = inp["b_in"][l]
    maps = []
    for c in range(8):
        cs = slice(64 * c, 64 * c + 64)
        wa = np.concatenate([w_in[:, 0:512][:, cs], w_in[:, 512:1024][:, cs]], axis=1)
        pa = np.zeros((64, 16), np.float32)
        pa[:, 0] = b_in[0:512][cs]
        pa[:, 1] = b_in[512:1024][cs]
        pa[:, 2] = inp["lru_conv_b"][l][cs]
        pa[:, 3:7] = inp["lru_conv_w"][l][:, cs].T
        pa[:, 7] = inp["lru_gate_a_b"][l][cs]
        pa[:, 8] = inp["lru_gate_x_b"][l][cs]
        pa[:, 9] = inp["lru_lambda"][l][cs]
        gw = np.concatenate([inp["lru_gate_a_w"][l][c], inp["lru_gate_x_w"][l][c]], axis=1)
        maps.append({"xT": xT, "wa": np.ascontiguousarray(wa), "pa": pa, "gw": np.ascontiguousarray(gw)})
    return maps


NQ = 4096
NKV = 6144
HAL = 2048
DILS = (1, 4, 16)
N_BUCKETS = 32
MAX_DISTANCE = 2048


def emit_p1b(b, st, ones, ident, xTh, wq_d, wk_d, wv_d, wg_d, pb_d, bias_d, yg_d):
    S = b.S
    b.pfx = "b_"
    onesb = b.sb("onesb", [128, 128], BF16, st=st)
    b.cp(onesb[:], ones[:, 0:128], ["c_ones"], ["onesb"])
    pb = b.sb("pb", [128, 16], st=st); b.load(pb[:], pb_d, "pb")
    eb = b.sb("eb", [128, 3, 2, 128], st=st); ebf = b.sb("ebf", [128, 3, 2, 128], st=st)
    b.load(eb[:], bias_d.rearrange("p (g k q) -> p g k q", g=3, k=2), "eb")
    b.act(eb[:], eb[:], AF.Exp, ["eb"], ["eb"])
    b.cp(ebf[:], eb[:], ["eb"], ["ebf"])
    for g in range(3):
        b.ts(ebf[:, g, 0, :], ebf[:, g, 0, :], pb[:, 10:11], None, ALU.mult, None, ["ebf", "pb"], ["ebf"])
    wq = b.sb("wq", [128, KCH, 384], BF16, st=st); wk = b.sb("wk", [128, KCH, 384], BF16, st=st); wv = b.sb("wv", [128, KCH, 384], BF16, st=st)
    wg = b.sb("wg", [128, KCH, 128], BF16, st=st)
    S.dma("pool", wk[:], w_ap(wk_d), writes=["wk"]); S.dma("pool", wv[:], w_ap(wv_d), writes=["wv"])
    S.dma("pool", wq[:], w_ap(wq_d), writes=["wq"]); S.dma("pool", wg[:], w_ap(wg_d), writes=["wg"])
    xb = [b.sb("xb%d" % i, [128, KCH, 512], BF16, st=st) for i in range(2)]
    KT = b.sb("KT", [128, NKV], BF16, st=st)
    VT = b.sb("VT", [128, NKV], F32, st=st)
    QT = b.sb("QT", [128, NQ], BF16, st=st)
    bg = b.sb("bg", [128, NQ], F32, st=st)
    ND = b.sb("ND", [128, 2, NQ], F32, st=st)
    esb = [b.sb("esb%d" % i, [128, 256], F32, st=st) for i in range(2)]
    PT = [b.sb("PT%d" % i, [128, 2, 128], BF16, st=st) for i in range(2)]
    Vt = [b.sb("Vt%d" % i, [128, 2, 128], BF16, st=st) for i in range(2)]
    scale = 128.0 ** -0.5
    xi = 0
    bi = 0
    for g, dil in enumerate(DILS):
        start = HAL - max(512, 128 * dil)
        tl_all = list(range(start, NKV, 512))
        for pi in range(0, len(tl_all), 1):
            pair = tl_all[pi:pi + 1]
            xss = []
            for t0 in pair:
                xs = xb[xi % 2]; xk = ("xb", xi % 2); xi += 1
                S.dma("pool", xs[:], xT_tile_ap(xTh, t0, 512), writes=[xk])
                xss.append((xs, xk, t0))
            secs = [("k", wk, "wk"), ("v", wv, "wv"), ("q", wq, "wq")] + ([("g", wg, "wg")] if g == 0 else [])
            for nm, wt, wkey in secs:
                xsel = xss if nm in ("k", "v") else [t for t in xss if t[2] >= HAL]
                if not xsel:
                    continue
                pss = [b.psum() for _ in xsel]
                for k in range(KCH):
                    for (ps_, pk_), (xs, xk, t0) in zip(pss, xsel):
                        lhs = wt[:, k, :] if nm == "g" else wt[:, k, g * 128:(g + 1) * 128]
                        b.mm(ps_[:, :], lhs, xs[:, k, :], k == 0, k == KCH - 1, [wkey, xk], [pk_])
                for (ps_, pk_), (xs, xk, t0) in zip(pss, xsel):
                    if nm == "k":
                        b.act(KT[:, t0:t0 + 512], ps_[:, :], AF.Identity, [pk_, "pb"], [("KT", t0)], bias=pb[:, 3 + g:4 + g])
                    elif nm == "v":
                        b.ts(VT[:, t0:t0 + 512], ps_[:, :], pb[:, 6 + g:7 + g], None, ALU.add, None, [pk_, "pb"], [("VT", t0)])
                    elif nm == "q":
                        b.act(QT[:, t0 - HAL:t0 - HAL + 512], ps_[:, :], AF.Identity, [pk_, "pb"], [("QT", t0 - HAL)], bias=pb[:, g:g + 1])
                    else:
                        b.act(bg[:, t0 - HAL:t0 - HAL + 512], ps_[:, :], AF.Silu, [pk_, "pb"], ["bg"], bias=pb[:, 9:10])
        span = 128 * dil
        allK = [("KT", t) for t in range(start, NKV, 512)]
        allV = [("VT", t) for t in range(start, NKV, 512)]
        allQ = [("QT", t) for t in range(0, NQ, 512)]

        def tiles_of(keys, base, lo, hi):
            return [(keys, t) for t in range(base, 99999, 512) if t < hi and t + 512 > lo]
        for r in range(dil):
            for nl in range(32 // dil):
                qs = r + span * nl
                ssl = lambda a: slice(a, a + 127 * dil + 1, dil)
                qsl = ssl(qs)
                kc_ = ssl(HAL + qs)
                kp_ = ssl(HAL + qs - span)
                qk = [("QT", t) for t in range(0, NQ, 512) if t < qs + span and t + 512 > qs]
                kk_ = [("KT", t) for t in range(start, NKV, 512) if t < HAL + qs + span and t + 512 > HAL + qs - span]
                vk_ = [("VT", t) for t in range(start, NKV, 512) if t < HAL + qs + span and t + 512 > HAL + qs - span]
                psl, kl = b.psum()
                b.mm(psl[:, 0:128], KT[:, kp_], QT[:, qsl], True, True, kk_ + qk, [kl])
                b.mm(psl[:, 128:256], KT[:, kc_], QT[:, qsl], True, True, kk_ + qk, [kl])
                e_ = esb[bi % 2]; ek = ("esb", bi % 2)
                b.act(e_[:], psl[:, 0:256], AF.Exp, [kl], [ek], scale=scale)
                p_ = PT[bi % 2]; pk_ = ("PT", bi % 2)
                ebsel = ebf if nl == 0 else eb
                b.tt(p_[:].rearrange("p a q -> p (a q)"), e_[:], ebsel[:, g].rearrange("p a q -> p (a q)"), ALU.mult, [ek, "eb", "ebf"], [pk_])
                psv, kv = b.psum()
                b.tr(psv[:, 0:128], VT[:, kp_], ident[:], vk_ + ["c_ident"], [kv])
                b.tr(psv[:, 128:256], VT[:, kc_], ident[:], vk_ + ["c_ident"], [kv])
                v_ = Vt[bi % 2]; vk2 = ("Vt", bi % 2)
                b.cp(v_[:].rearrange("p a q -> p (a q)"), psv[:, 0:256], [kv], [vk2], eng="act")
                pso, ko = b.psum()
                b.mm(pso[:, 0:128], v_[:, 0, :], p_[:, 0, :], True, False, [vk2, pk_], [ko])
                b.mm(pso[:, 0:128], v_[:, 1, :], p_[:, 1, :], False, True, [vk2, pk_], [ko])
                b.mm(pso[:, 128:256], onesb[:], p_[:, 0, :], True, False, ["onesb", pk_], [ko])
                b.mm(pso[:, 128:256], onesb[:], p_[:, 1, :], False, True, ["onesb", pk_], [ko])
                src = pso[:, 0:256].rearrange("p (a q) -> p a q", a=2)
                if g == 0:
                    b.cp(ND[:, :, qsl], src, [ko], ["ND"], eng="act")
                else:
                    b.tt(ND[:, :, qsl], ND[:, :, qsl], src, ALU.add, ["ND", ko], ["ND"])
                bi += 1
    S.op("dve", lambda e: e.reciprocal(out=ND[:, 1, :], in_=ND[:, 1, :]), ["ND"], ["ND"])
    b.tt(ND[:, 0, :], ND[:, 0, :], ND[:, 1, :], ALU.mult, ["ND"], ["ND"])
    b.tt(ND[:, 0, :], ND[:, 0, :], bg[:], ALU.mult, ["ND", "bg"], ["ND"], eng="pool")
    S.dma("sp", yg_d, ND[:, 0, :], reads=["ND"], writes=["outB"])
    return ["outB"]


def p1b_dram(b):
    return (b.din("xTh", [D_MODEL, NKV]), b.din("wq", [D_MODEL, 384]), b.din("wk", [D_MODEL, 384]), b.din("wv", [D_MODEL, 384]),
            b.din("wg", [D_MODEL, 128]), b.din("pb", [128, 16]), b.din("biasT", [128, 3 * 2 * 128]), b.dout("ygB", [128, NQ]))


def build_p1b():
    b = B()
    ones, ident = b.ident()
    st = contextlib.ExitStack()
    keys = emit_p1b(b, st, ones, ident, *p1b_dram(b))
    return b.finish(keys, extra=[st]), b.outs


def t5_bucket(dist):
    import math
    max_exact = N_BUCKETS // 2
    large = max_exact + (np.log(np.maximum(dist, 1) / max_exact) / math.log(MAX_DISTANCE / max_exact)
                         * (N_BUCKETS - max_exact)).astype(np.int32)
    large = np.minimum(large, N_BUCKETS - 1)
    return np.where(dist < max_exact, dist, large).astype(np.int32)


def host_p1b(inp, l, xT):
    w_in = inp["w_in"][l]; b_in = inp["b_in"][l]
    Q0 = 1024; K0 = Q0 + 1536; V0 = K0 + 1536; G0 = V0 + 1536
    table = inp["att_rel_bias"]
    ki = np.arange(128)[:, None, None]; kb = np.arange(2)[None, :, None]; qi = np.arange(128)[None, None, :]
    dist = qi + 128 - (kb * 128 + ki)
    valid = (dist >= 0) & (dist <= 128)
    maps = []
    for c in range(8):
        hm, half = c // 2, c % 2
        cols = lambda base: np.concatenate([w_in[:, base + (g * 4 + hm) * 128: base + (g * 4 + hm + 1) * 128] for g in range(3)], axis=1)
        pb = np.zeros((128, 16), np.float32)
        for g in range(3):
            hh = (g * 4 + hm) * 128
            pb[:, g] = b_in[Q0 + hh:Q0 + hh + 128]
            pb[:, 3 + g] = b_in[K0 + hh:K0 + hh + 128]
            pb[:, 6 + g] = b_in[V0 + hh:V0 + hh + 128]
        pb[:, 9] = b_in[G0 + hm * 128:G0 + hm * 128 + 128]
        pb[:, 10] = float(half)
        bT = np.zeros((128, 3, 2, 128), np.float32)
        for g, dil in enumerate(DILS):
            bucket = t5_bucket(np.clip(dist, 0, 128) * dil)
            bT[:, g] = np.where(valid, table[bucket, g * 4 + hm], np.float32(-100.0))
        xh = np.zeros((2048, NKV), np.float32)
        base = half * NQ
        if half == 0:
            xh[:, HAL:] = xT[:, 0:NQ]
        else:
            xh[:] = xT[:, base - HAL:base + NQ]
        maps.append({"xTh": xh, "wq": np.ascontiguousarray(cols(Q0)), "wk": np.ascontiguousarray(cols(K0)),
                     "wv": np.ascontiguousarray(cols(V0)), "wg": np.ascontiguousarray(w_in[:, G0 + hm * 128:G0 + (hm + 1) * 128]),
                     "pb": pb, "biasT": bT.reshape(128, 768)})
    return maps


def gather_p1b(results):
    yg = np.zeros((512, 8192), np.float32)
    for c, r in enumerate(results):
        hm, half = c // 2, c % 2
        yg[hm * 128:(hm + 1) * 128, half * NQ:(half + 1) * NQ] = r["ygB"]
    return yg


SEGC = 1024
GN_EPS = 64e-5
DEC_C = float(np.exp(-0.5))


def emit_p1c(b, st, ones, ident, xT, wc_d, pc_d, wup_d, gnb_d, yg_d, T=SEQ):
    S = b.S
    b.pfx = "c_"
    nseg = T // SEGC
    NCH = SEGC // 128
    pc = b.sb("pc", [64, 32], st=st); b.load(pc[:], pc_d, "pc")
    wup = b.sb("wup", [64, 128], st=st); b.load(wup[:], wup_d, "wup")
    gnb = b.sb("gnb", [128, 192], st=st); b.load(gnb[:], gnb_d, "gnb")
    wc = b.sb("wc", [128, KCH, 384], BF16, st=st)
    S.dma("pool", wc[:], w_ap(wc_d), writes=["wc"])
    pcx = b.sb("pcx", [64, 4], st=st)
    b.ts(pcx[:, 0:1], pc[:, 13:14], -1.0, 1.0, ALU.mult, ALU.add, ["pc"], ["pcx"])
    mask2 = b.sb("mask2", [128, 2, 128], st=st); maskL = b.sb("maskL", [128, 128], st=st); maskc = b.sb("maskc", [64, SEGC], st=st)
    S.op("pool", lambda e: e.affine_select(out=mask2[:, 0, :], in_=ones[:, 0:128], pattern=[[1, 128]], compare_op=ALU.is_gt,
                                           fill=0.0, base=0, channel_multiplier=-1), ["c_ones"], ["mask2"])
    S.op("pool", lambda e: e.affine_select(out=mask2[:, 1, :], in_=ones[:, 0:128], pattern=[[1, 128]], compare_op=ALU.is_ge,
                                           fill=0.0, base=0, channel_multiplier=-1), ["c_ones"], ["mask2"])
    S.op("pool", lambda e: e.affine_select(out=maskL[:], in_=ones[:, 0:128], pattern=[[-1, 128]], compare_op=ALU.is_gt,
                                           fill=0.0, base=0, channel_multiplier=1), ["c_ones"], ["maskL"])
    b.memset(maskc[:], 1.0, ["maskc"])
    b.memset(maskc[:, 0:SEGC - 127:128], 0.0, ["maskc"])

    xb = [b.sb("xb%d" % i, [128, KCH, 512], BF16, st=st) for i in range(2)]
    raw = [b.sb("raw%d" % i, [64, SEGC + 1], st=st) for i in range(5)]
    sh = [b.sb("sh%d" % i, [64, SEGC], st=st) for i in range(5)]
    names = ["lw", "aicl", "kk", "kc", "bb", "cum", "tmp1", "tmp2", "bt", "kt", "bh", "kh", "rkr"]
    A = {n: b.sb(n, [64, SEGC], st=st) for n in names}
    atrt = b.sb("atrt", [64, 2, SEGC], st=st)
    wl = b.sb("wl", [64, NCH], st=st)
    sgc = b.sb("sgc", [64, SEGC], st=st)
    obuf = [b.sb("obuf%d" % i, [64, SEGC], st=st) for i in range(2)]
    ST = [b.sb("ST%d" % i, [64, 64], st=st) for i in range(2)]
    tok4 = [b.sb("tok4_%d" % i, [128, 5, 64], st=st) for i in range(2)]
    AB = [b.sb("AB%d" % i, [128, 2, 128], st=st) for i in range(2)]
    AK = [b.sb("AK%d" % i, [128, 2, 128], st=st) for i in range(2)]
    Mb = [[b.sb("Mb%d_%d" % (q, i), [128, 128], st=st) for i in range(2)] for q in range(2)]
    MTb = [[b.sb("MTb%d_%d" % (q, i), [128, 128], st=st) for i in range(2)] for q in range(2)]
    TTb = [[b.sb("TTb%d_%d" % (q, i), [128, 128], st=st) for i in range(2)] for q in range(2)]
    Xsb = [b.sb("Xsb%d" % q, [128, 64], st=st) for q in range(2)]
    PQ = [b.sb("PQ%d" % i, [128, 2, 64], st=st) for i in range(2)]
    GT = [b.sb("GT%d" % q, [64, 64], st=st) for q in range(2)]; Hs = [b.sb("Hs%d" % q, [64, 64], st=st) for q in range(2)]
    R2T = [b.sb("R2T%d" % q, [64, 128], st=st) for q in range(2)]
    bst = [b.sb("bst%d" % q, [128, 6], st=st) for q in range(2)]; mv = [b.sb("mv%d" % q, [128, 4], st=st) for q in range(2)]
    bsc = [b.sb("bsc%d" % q, [128, 1], st=st) for q in range(2)]
    yn = [b.sb("yn%d" % i, [128, 64], st=st) for i in range(2)]

    b.memset(ST[0][:], 0.0, [("ST", 0)])
    for g in range(5):
        b.memset(raw[g][:, 0:1], 0.0, [("raw", g)])
    xi = 0
    stc = 0
    outk = []
    for s in range(nseg):
        NTL = SEGC // 512

        def load_x(seg):
            nonlocal xi
            out = []
            for tl in range(NTL):
                xs = xb[xi % 2]; xk = ("xb", xi % 2); xi += 1
                S.dma("pool", xs[:], xT_tile_ap(xT, seg * SEGC + tl * 512, 512), writes=[xk])
                out.append((xs, xk))
            return out
        xss = x_next if s > 0 else load_x(0)
        for g in range(6):
            pss = [b.psum() for _ in range(NTL)]
            for k in range(KCH):
                for tl in range(NTL):
                    b.mm(pss[tl][0][0:64, :], wc[:, k, g * 64:(g + 1) * 64], xss[tl][0][:, k, :], k == 0, k == KCH - 1, ["wc", xss[tl][1]], [pss[tl][1]])
            for tl in range(NTL):
                ps, pk = pss[tl]
                if g < 5:
                    b.act(raw[g][:, 1 + tl * 512:1 + (tl + 1) * 512], ps[0:64, :], AF.Identity, [pk, "pc"], [("raw", g)], bias=pc[:, g:g + 1])
                else:
                    b.act(sgc[:, tl * 512:(tl + 1) * 512], ps[0:64, :], AF.Silu, [pk, "pc"], ["sgc"], bias=pc[:, 16:17])
        x_next = load_x(s + 1) if s + 1 < nseg else None
        for g in range(5):
            b.tt(sh[g][:], raw[g][:, 0:SEGC], raw[g][:, 1:SEGC + 1], ALU.subtract, [("raw", g)], [("sh", g)])
            b.stt(sh[g][:], sh[g][:], pc[:, 5 + g:6 + g], raw[g][:, 1:SEGC + 1], ALU.mult, ALU.add, [("sh", g), "pc", ("raw", g)], [("sh", g)])
            b.cp(raw[g][:, 0:1], raw[g][:, SEGC:SEGC + 1], [("raw", g)], [("raw", g)], eng="pool")
        s_r, s_k, s_v, s_w, s_a = sh
        kr, kk_, kv_, kw_, ka_ = [("sh", g) for g in range(5)]
        b.act(s_w[:], s_w[:], AF.Tanh, [kw_], [kw_])
        for tl in range(2):
            sl = slice(tl * 512, (tl + 1) * 512)
            ps, pk = b.psum()
            b.mm(ps[0:64, :], wup[:, 0:64], s_w[:, sl], True, True, ["wup", kw_], [pk])
            b.act(A["lw"][:, sl], ps[0:64, :], AF.Sigmoid, [pk, "pc"], ["lw"], bias=pc[:, 10:11])
            ps, pk = b.psum()
            b.mm(ps[0:64, :], wup[:, 64:128], s_a[:, sl], True, True, ["wup", ka_], [pk])
            b.act(A["aicl"][:, sl], ps[0:64, :], AF.Sigmoid, [pk, "pc"], ["aicl"], bias=pc[:, 11:12])
        b.ts(A["lw"][:], A["lw"][:], -DEC_C, None, ALU.mult, None, ["lw"], ["lw"])
        b.ts(A["kk"][:], s_k[:], pc[:, 12:13], None, ALU.mult, None, [kk_, "pc"], ["kk"])
        b.tt(A["tmp1"][:], A["kk"][:], A["kk"][:], ALU.mult, ["kk"], ["tmp1"])
        for tl in range(2):
            sl = slice(tl * 512, (tl + 1) * 512)
            ps, pk = b.psum()
            b.mm(ps[0:64, :], ones[0:64, 0:64], A["tmp1"][:, sl], True, True, ["c_ones", "tmp1"], [pk])
            b.act(A["tmp2"][:, sl], ps[0:64, :], AF.Sqrt, [pk], ["tmp2"])
        b.ts(A["tmp2"][:], A["tmp2"][:], 1e-12, None, ALU.max, None, ["tmp2"], ["tmp2"])
        S.op("dve", lambda e: e.reciprocal(out=A["tmp2"][:], in_=A["tmp2"][:]), ["tmp2"], ["tmp2"])
        b.tt(A["kk"][:], A["kk"][:], A["tmp2"][:], ALU.mult, ["kk", "tmp2"], ["kk"])
        b.ts(A["tmp1"][:], A["aicl"][:], pc[:, 13:14], pcx[:, 0:1], ALU.mult, ALU.add, ["aicl", "pc", "pcx"], ["tmp1"])
        b.tt(A["kc"][:], s_k[:], A["tmp1"][:], ALU.mult, [kk_, "tmp1"], ["kc"])
        b.tt(A["bb"][:], A["kk"][:], A["aicl"][:], ALU.mult, ["kk", "aicl"], ["bb"], eng="pool")
        S.op("dve", lambda e: e.tensor_tensor_scan(out=A["cum"][:], data0=maskc[:], data1=A["lw"][:], initial=0.0, op0=ALU.mult, op1=ALU.add),
             ["maskc", "lw"], ["cum"])
        b.tt(A["tmp1"][:], A["cum"][:], A["lw"][:], ALU.subtract, ["cum", "lw"], ["tmp1"])
        b.act(A["tmp1"][:], A["tmp1"][:], AF.Exp, ["tmp1"], ["tmp1"])
        b.stt(atrt[:, 0, :], A["kk"][:], -1.0, A["tmp1"][:], ALU.mult, ALU.mult, ["kk", "tmp1"], ["atrt"])
        b.act(A["tmp2"][:], A["cum"][:], AF.Exp, ["cum"], ["tmp2"])
        b.tt(atrt[:, 1, :], s_r[:], A["tmp2"][:], ALU.mult, [kr, "tmp2"], ["atrt"])
        b.act(A["tmp2"][:], A["cum"][:], AF.Exp, ["cum", "atrt"], ["tmp2"], scale=-1.0)
        b.tt(A["bt"][:], A["bb"][:], A["tmp2"][:], ALU.mult, ["bb", "tmp2"], ["bt"])
        b.tt(A["kt"][:], A["kc"][:], A["tmp2"][:], ALU.mult, ["kc", "tmp2"], ["kt"], eng="pool")
        for c in range(NCH):
            cs = slice(c * 128, (c + 1) * 128)
            b.act(A["tmp1"][:, cs], A["cum"][:, cs], AF.Exp, ["cum", "atrt"], ["tmp1"], scale=-1.0, bias=A["cum"][:, c * 128 + 127:c * 128 + 128])
        b.tt(A["bh"][:], A["bb"][:], A["tmp1"][:], ALU.mult, ["bb", "tmp1"], ["bh"])
        b.tt(A["kh"][:], A["kc"][:], A["tmp1"][:], ALU.mult, ["kc", "tmp1"], ["kh"], eng="pool")
        b.act(wl[:], A["cum"][:, 127:SEGC:128], AF.Exp, ["cum"], ["wl"])
        b.stt(A["rkr"][:], s_r[:], pc[:, 15:16], A["kc"][:], ALU.mult, ALU.mult, [kr, "pc", "kc"], ["rkr"])

        ob = obuf[s % 2]; obk = ("obuf", s % 2)
        def chunk_steps(c, p):
            nonlocal stc
            cs = slice(c * 128, (c + 1) * 128)
            t4 = tok4[p]; t4k = ("tok4", p)
            psT, kT = b.psum()
            b.tr(psT[:, 0:64], atrt[:, 0, cs], ident[0:64, 0:64], ["atrt", "c_ident"], [kT])
            b.tr(psT[:, 64:128], s_v[:, cs], ident[0:64, 0:64], [kv_, "c_ident"], [kT])
            b.tr(psT[:, 128:192], A["bh"][:, cs], ident[0:64, 0:64], ["bh", "c_ident"], [kT])
            b.tr(psT[:, 192:256], A["kh"][:, cs], ident[0:64, 0:64], ["kh", "c_ident"], [kT])
            b.tr(psT[:, 256:320], sgc[:, cs], ident[0:64, 0:64], ["sgc", "c_ident"], [kT])
            b.cp(t4[:].rearrange("p a q -> p (a q)"), psT[:, 0:320], [kT], [t4k], eng="act")
            yield
            ab = AB[p]; abk = ("AB", p); ak = AK[p]; akk = ("AK", p)
            ps1, k1 = b.psum()
            b.mm(ps1[:, 0:256], A["bt"][:, cs], atrt[:, :, cs], True, True, ["bt", "atrt"], [k1])
            b.tt(ab[:].rearrange("p a q -> p (a q)"), ps1[:, 0:256], mask2[:].rearrange("p a q -> p (a q)"), ALU.mult, [k1, "mask2"], [abk])
            yield
            ps2, k2 = b.psum()
            b.mm(ps2[:, 0:256], A["kt"][:, cs], atrt[:, :, cs], True, True, ["kt", "atrt"], [k2])
            b.tt(ak[:].rearrange("p a q -> p (a q)"), ps2[:, 0:256], mask2[:].rearrange("p a q -> p (a q)"), ALU.mult, [k2, "mask2"], [akk])
            yield
            ps3, k3 = b.psum()
            b.mm(ps3[:, 0:128], atrt[:, 0, cs], A["bt"][:, cs], True, True, ["bt", "atrt"], [k3])
            b.tt(Mb[p][0][:], ps3[:, 0:128], maskL[:], ALU.mult, [k3, "maskL"], [("Mb", p, 0)])
            b.tt(TTb[p][0][:], ab[:, 0, :], ident[:], ALU.add, [abk, "c_ident"], [("TTb", p, 0)], eng="pool")
            yield
            Mc, Mck = Mb[p][0], ("Mb", p, 0)
            MTc, MTck = ab[:, 0, :], abk
            for lev in range(1, 7):
                Mn, Mnk = Mb[p][lev % 2], ("Mb", p, lev % 2)
                psm, km = b.psum()
                b.mm(psm[:, 0:128], MTc, Mc[:], True, True, [MTck, Mck], [km])
                if lev < 6:
                    MTn, MTnk = MTb[p][lev % 2], ("MTb", p, lev % 2)
                    psn, kn = b.psum()
                    b.mm(psn[:, 0:128], Mc[:], MTc, True, True, [MTck, Mck], [kn])
                b.cp(Mn[:], psm[:, 0:128], [km], [Mnk], eng="act")
                if lev < 6:
                    b.cp(MTn[:], psn[:, 0:128], [kn], [MTnk], eng="dve")
                yield
                pst, kt_ = b.psum()
                b.mm(pst[:, 0:128], Mn[:], TTb[p][(lev - 1) % 2][:], True, True, [Mnk, ("TTb", p, (lev - 1) % 2)], [kt_])
                b.tt(TTb[p][lev % 2][:], TTb[p][(lev - 1) % 2][:], pst[:, 0:128], ALU.add, [("TTb", p, (lev - 1) % 2), kt_], [("TTb", p, lev % 2)])
                yield
                Mc, Mck = Mn, Mnk
                if lev < 6:
                    MTc, MTck = MTn[:], MTnk
            TT, TTk = TTb[p][0], ("TTb", p, 0)
            pq = PQ[p]; pqk = ("PQ", p)
            psP, kP = b.psum()
            b.mm(psP[:, 0:64], TT[:], t4[:, 0, :], True, True, [TTk, t4k], [kP])
            psX, kX = b.psum()
            b.mm(psX[:, 0:64], ak[:, 0, :], t4[:, 1, :], True, True, [akk, t4k], [kX])
            b.cp(Xsb[p][:], psX[:, 0:64], [kX], [("Xsb", p)], eng="act")
            yield
            b.mm(psP[:, 64:128], TT[:], Xsb[p][:], True, True, [TTk, ("Xsb", p)], [kP])
            b.cp(pq[:].rearrange("p a q -> p (a q)"), psP[:, 0:128], [kP], [pqk], eng="dve")
            yield
            psg, kg = b.psum()
            b.mm(psg[0:64, 0:64], pq[:, 0, :], t4[:, 2, :], True, True, [pqk, t4k], [kg])
            b.stt(GT[p][:], ident[0:64, 0:64], wl[:, c:c + 1], psg[0:64, 0:64], ALU.mult, ALU.add, ["c_ident", "wl", kg], [("GT", p)])
            psh, kh_ = b.psum()
            b.mm(psh[0:64, 0:64], t4[:, 2, :], pq[:, 1, :], True, False, [pqk, t4k], [kh_])
            b.mm(psh[0:64, 0:64], t4[:, 3, :], t4[:, 1, :], False, True, [t4k], [kh_])
            b.cp(Hs[p][:], psh[0:64, 0:64], [kh_], [("Hs", p)], eng="act")
            yield
            psr, kr_ = b.psum()
            b.mm(psr[0:64, 0:128], pq[:, 0, :], ab[:, 1, :], True, True, [pqk, abk], [kr_])
            b.tt(R2T[p][:], psr[0:64, 0:128], atrt[:, 1, cs], ALU.add, [kr_, "atrt"], [("R2T", p)])
            psb, kb = b.psum()
            b.mm(psb[:, 0:1], A["rkr"][:, cs], ones[0:64, 0:1], True, True, ["rkr", "c_ones"], [kb])
            b.cp(bsc[p][:], psb[:, 0:1], [kb], [("bsc", p)], eng="act")
            yield
            stcur, stk = ST[stc % 2], ("ST", stc % 2)
            stnew, stnk = ST[(stc + 1) % 2], ("ST", (stc + 1) % 2)
            stc += 1
            psy, ky = b.psum()
            b.mm(psy[:, 0:64], R2T[p][:], stcur[:], True, False, [("R2T", p), stk], [ky])
            b.mm(psy[:, 0:64], ab[:, 1, :], pq[:, 1, :], False, False, [abk, pqk], [ky])
            b.mm(psy[:, 0:64], ak[:, 1, :], t4[:, 1, :], False, True, [akk, t4k], [ky])
            pss, ks = b.psum()
            b.mm(pss[0:64, 0:64], GT[p][:], stcur[:], True, True, [("GT", p), stk], [ks])
            b.tt(stnew[:], pss[0:64, 0:64], Hs[p][:], ALU.add, [ks, ("Hs", p)], [stnk])
            S.op("dve", lambda e, psy=psy: e.bn_stats(out=bst[p][:], in_=psy[:, 0:64]), [ky], [("bst", p)])
            S.op("dve", lambda e: e.bn_aggr(out=mv[p][:, 0:2], in_=bst[p][:]), [("bst", p)], [("mv", p)])
            yield
            b.act(mv[p][:, 2:3], mv[p][:, 1:2], AF.Sqrt, [("mv", p)], [("mv2", p)], bias=GN_EPS)
            S.op("dve", lambda e: e.reciprocal(out=mv[p][:, 2:3], in_=mv[p][:, 2:3]), [("mv2", p)], [("mv2", p)])
            b.stt(mv[p][:, 3:4], mv[p][:, 0:1], -1.0, mv[p][:, 2:3], ALU.mult, ALU.mult, [("mv", p), ("mv2", p)], [("mv3", p)])
            y_ = yn[p]; ynk = ("yn", p)
            b.act(y_[:], psy[:, 0:64], AF.Identity, [ky, ("mv2", p), ("mv3", p)], [ynk], bias=mv[p][:, 3:4], scale=mv[p][:, 2:3])
            yield
            b.tt(y_[:], y_[:], gnb[:, 0:64], ALU.mult, [ynk, "gnb"], [ynk])
            b.tt(y_[:], y_[:], gnb[:, 64:128], ALU.add, [ynk, "gnb"], [ynk], eng="pool")
            b.stt(y_[:], t4[:, 1, :], bsc[p][:, 0:1], y_[:], ALU.mult, ALU.add, [t4k, ("bsc", p), ynk], [ynk])
            b.tt(y_[:], y_[:], t4[:, 4, :], ALU.mult, [ynk, t4k], [ynk], eng="pool")
            pso, ko = b.psum()
            b.tr(pso[0:64, 0:128], y_[:], ident[:], [ynk, "c_ident"], [ko])
            b.cp(ob[:, cs], pso[0:64, 0:128], [ko], [obk], eng="act")
            yield

        import itertools
        for c0 in range(0, NCH, 2):
            gens = [chunk_steps(c0 + q, q) for q in range(min(2, NCH - c0))]
            for _ in itertools.zip_longest(*gens):
                pass
        S.dma("sp", yg_d[:, s * SEGC:(s + 1) * SEGC], ob[:], reads=[obk], writes=[("outC", s)])
        outk.append(("outC", s))
    return outk


def p1c_dram(b, T=SEQ):
    return (b.din("wc", [D_MODEL, 384]), b.din("pc", [64, 32]), b.din("wup", [64, 128]), b.din("gnb", [128, 192]), b.dout("ygC", [64, T]))


def build_p1c(T=SEQ):
    b = B()
    ones, ident = b.ident()
    st = contextlib.ExitStack()
    keys = emit_p1c(b, st, ones, ident, b.din("xT", [D_MODEL, T]), *p1c_dram(b, T), T=T)
    return b.finish(keys, extra=[st]), b.outs


def host_p1c(inp, l, xT):
    w_in = inp["w_in"][l]; b_in = inp["b_in"][l]
    C0 = 1024 + 5120
    mu = inp["rwkv_mu"][l]
    maps = []
    for c in range(8):
        hs = slice(64 * c, 64 * c + 64)
        secs = [(C0, hs), (C0 + 512, hs), (C0 + 1024, hs), (C0 + 1536, slice(0, 64)), (C0 + 1600, slice(0, 64)), (C0 + 1664, hs)]
        wc = np.concatenate([w_in[:, o:o + 512][:, sl_] if sl_ is hs else w_in[:, o:o + 64] for o, sl_ in secs], axis=1)
        pc = np.zeros((64, 32), np.float32)
        pc[:, 0] = b_in[C0:C0 + 512][hs]; pc[:, 1] = b_in[C0 + 512:C0 + 1024][hs]; pc[:, 2] = b_in[C0 + 1024:C0 + 1536][hs]
        pc[:, 3] = b_in[C0 + 1536:C0 + 1600]; pc[:, 4] = b_in[C0 + 1600:C0 + 1664]
        pc[:, 5] = mu[0:512][hs]; pc[:, 6] = mu[512:1024][hs]; pc[:, 7] = mu[1024:1536][hs]
        pc[:, 8] = mu[1536:1600]; pc[:, 9] = mu[1600:1664]
        pc[:, 10] = inp["rwkv_w0"][l][hs]; pc[:, 11] = inp["rwkv_a0"][l][hs]
        pc[:, 12] = inp["rwkv_k_k"][l][hs]; pc[:, 13] = inp["rwkv_k_a"][l][hs]
        pc[:, 15] = inp["rwkv_r_k"][l][c]
        pc[:, 16] = b_in[C0 + 1664:C0 + 2176][hs]
        wup = np.concatenate([inp["rwkv_w_up"][l][:, hs], inp["rwkv_a_up"][l][:, hs]], axis=1)
        gn = np.concatenate([inp["rwkv_gn_g"][l][hs], inp["rwkv_gn_b"][l][hs], b_in[C0 + 1664:C0 + 2176][hs]])
        gnb = np.ascontiguousarray(np.broadcast_to(gn[None, :], (128, 192)))
        maps.append({"xT": xT, "wc": np.ascontiguousarray(wc), "pc": pc, "wup": np.ascontiguousarray(wup), "gnb": gnb})
    return maps


def build_p1():
    b = B()
    S = b.S
    ones, ident = b.ident()
    xT = b.din("xT", [D_MODEL, SEQ])
    a_dram = (b.din("wa", [D_MODEL, 128]), b.din("pa", [64, 16]), b.din("gw", [64, 128]), b.dout("ygA", [64, SEQ]))
    b_dram = p1b_dram(b)
    c_dram = p1c_dram(b)
    keys = []
    st = contextlib.ExitStack()
    keys += emit_p1a(b, st, xT, *a_dram)
    S.barrier(); st.close()
    st = contextlib.ExitStack()
    keys += emit_p1b(b, st, ones, ident, *b_dram)
    S.barrier(); st.close()
    st = contextlib.ExitStack()
    keys += emit_p1c(b, st, ones, ident, xT, *c_dram)
    return b.finish(keys, extra=[st]), b.outs


def host_p1(inp, l, xT):
    ma, mb, mc = host_p1a(inp, l, xT), host_p1b(inp, l, xT), host_p1c(inp, l, xT)
    maps = []
    for c in range(8):
        m = dict(ma[c]); m.update(mb[c]); m.update(mc[c])
        maps.append(m)
    return maps


ALPHA = (2.0 * 2) ** 0.25
LN_EPS = 1e-5
NT = 1024
HALO = 32


def build_p2():
    b = B()
    S = b.S
    xTh = b.din("xTh", [D_MODEL, NT + HALO])
    xown = b.din("xown", [NT, D_MODEL])
    ygT = b.din("ygT", [1536, NT])
    wm_d = b.din("wm", [D_MODEL, 8192])
    wd_d = b.din("wd", [D_MODEL, 1536])
    wbr_d = b.din("wbr", [2048, D_MODEL])
    wout_d = b.din("wout", [D_MODEL, D_MODEL])
    pm_d = b.din("pm", [128, 64])
    pd_d = b.din("pd", [128, 64])
    dww_d = b.din("dww", [128, 4 * 31])
    lng_d = b.din("lng", [128, D_MODEL])
    lnb_d = b.din("lnb", [128, D_MODEL])
    xo_d = b.dout("xo", [NT, D_MODEL])

    ones, ident = b.ident()
    pm = b.sb("pm", [128, 64]); pd = b.sb("pd", [128, 64]); dww = b.sb("dww", [128, 4, 31])
    b.load(pm[:], pm_d, "pm"); b.load(pd[:], pd_d, "pd")
    b.load(dww[:], dww_d.rearrange("p (c j) -> p c j", j=31), "dww")
    mixT = b.sb("mixT", [128, KCH, NT], BF16)

    stDM = contextlib.ExitStack()
    xb = b.sb("xb", [128, KCH, NT + HALO], BF16, st=stDM)
    ygd = b.sb("ygd", [128, 4, NT], BF16, st=stDM)
    S.dma("pool", xb[:], xT_tile_ap(xTh, 0, NT + HALO), writes=["xb"])

    stD = contextlib.ExitStack()
    wdb = [b.sb("wdb%d" % i, [128, KCH, 3, 128], BF16, st=stD) for i in range(2)]
    cu = b.sb("cu", [128, NT + HALO], F32, st=stD)
    t1 = b.sb("t1", [128, NT + HALO], F32, st=stD)
    t2 = b.sb("t2", [128, NT + HALO], F32, st=stD)
    cv = b.sb("cv", [128, 4, NT], F32, st=stD)
    dg = b.sb("dg", [128, 4, NT], F32, st=stD)
    sq = b.sb("sq", [128, NT], F32, st=stD)
    mean = b.sb("mean", [128, NT], F32, st=stD)
    rstd = b.sb("rstd", [128, NT], F32, st=stD)
    tiles = [(0, HALO), (HALO, 512), (HALO + 512, 512)]
    for cc in range(4):
        w = wdb[cc % 2]; wk = ("wdb", cc % 2)
        for j in range(3):
            S.dma("pool", w[:, :, j, :], w_ap(wd_d[:, j * 512 + cc * 128: j * 512 + cc * 128 + 128]), writes=[wk])
        groups = [[tiles[0]], [tiles[1], tiles[2]]]
        for grp in groups:
            for sec in range(3):
                if sec == 2 and grp[0][0] < HALO:
                    continue
                pss = [b.psum() for _ in grp]
                for k in range(KCH):
                    for (ps_, pk_), (t0, n) in zip(pss, grp):
                        b.mm(ps_[:, :n], w[:, k, sec, :], xb[:, k, t0:t0 + n], k == 0, k == KCH - 1, [wk, "xb"], [pk_])
                for (ps_, pk_), (t0, n) in zip(pss, grp):
                    if sec == 0:
                        b.act(t1[:, t0:t0 + n], ps_[:, :n], AF.Identity, [pk_, "pd"], [("t1", t0)], bias=pd[:, cc:cc + 1])
                    elif sec == 1:
                        b.act(t2[:, t0:t0 + n], ps_[:, :n], AF.Sigmoid, [pk_, "pd"], [("t2", t0)], bias=pd[:, 4 + cc:5 + cc])
                        b.tt(cu[:, t0:t0 + n], t1[:, t0:t0 + n], t2[:, t0:t0 + n], ALU.mult, [("t1", t0), ("t2", t0)], ["cu"])
                    else:
                        b.act(dg[:, cc, t0 - HALO:t0 - HALO + n], ps_[:, :n], AF.Silu, [pk_, "pd"], [("dg", cc)], bias=pd[:, 8 + cc:9 + cc])
        b.ts(cu[:, 0:HALO], cu[:, 0:HALO], pd[:, 24:25], None, ALU.mult, None, ["cu", "pd"], ["cu"])
        ck = ("cv", cc)
        b.ts(cv[:, cc, :], cu[:, 2:2 + NT], dww[:, cc, 0:1], pd[:, 12 + cc:13 + cc], ALU.mult, ALU.add, ["cu", "dww", "pd"], [ck])
        for j in range(1, 31):
            b.stt(cv[:, cc, :], cu[:, 2 + j:2 + j + NT], dww[:, cc, j:j + 1], cv[:, cc, :], ALU.mult, ALU.add, ["cu", "dww", ck], [ck])
    for tt_ in range(2):
        sl = slice(tt_ * 512, (tt_ + 1) * 512)
        ps1, k1 = b.psum()
        for cc in range(4):
            b.mm(ps1[:, :], ones[:, 0:128], cv[:, cc, sl], cc == 0, cc == 3, ["c_ones", ("cv", cc)], [k1])
        b.act(mean[:, sl], ps1[:, :], AF.Copy, [k1], ["mean"], scale=1.0 / 512)
        ps2, k2 = b.psum()
        for cc in range(4):
            b.act(sq[:, sl], cv[:, cc, sl], AF.Square, [("cv", cc)], ["sq"])
            b.mm(ps2[:, :], ones[:, 0:128], sq[:, sl], cc == 0, cc == 3, ["c_ones", "sq"], [k2])
        b.act(rstd[:, sl], ps2[:, :], AF.Copy, [k2], ["rstd"], scale=1.0 / 512)
    b.tt(sq[:], mean[:], mean[:], ALU.mult, ["mean"], ["sq"])
    b.tt(rstd[:], rstd[:], sq[:], ALU.subtract, ["rstd", "sq"], ["rstd"])
    b.act(rstd[:], rstd[:], AF.Sqrt, ["rstd"], ["rstd"], bias=LN_EPS)
    S.op("dve", lambda e: e.reciprocal(out=rstd[:], in_=rstd[:]), ["rstd"], ["rstd"])
    for cc in range(4):
        b.tt(sq[:], cv[:, cc, :], mean[:], ALU.subtract, [("cv", cc), "mean"], ["sq"])
        b.tt(sq[:], sq[:], rstd[:], ALU.mult, ["sq", "rstd"], ["sq"], eng="pool")
        b.act(sq[:], sq[:], AF.Silu, ["sq", "pd"], ["sq"], bias=pd[:, 20 + cc:21 + cc], scale=pd[:, 16 + cc:17 + cc])
        b.tt(ygd[:, cc, :], sq[:], dg[:, cc, :], ALU.mult, ["sq", ("dg", cc)], ["ygd"])
    S.barrier()
    stD.close()

    stM = contextlib.ExitStack()
    ygb = b.sb("ygb", [128, 12, NT], BF16, st=stM)
    S.dma("pool", ygb[:], ygT.rearrange("(c p) t -> p c t", p=128), writes=["ygb"])
    wmb = [b.sb("wmb%d" % i, [128, KCH, 512], BF16, st=stM) for i in range(3)]
    wbb = [b.sb("wbb%d" % i, [128, 4, 512], BF16, st=stM) for i in range(3)]
    macc = b.sb("macc", [128, 4, NT], F32, st=stM)
    mg = [b.sb("mg%d" % i, [128, 512], F32, st=stM) for i in range(2)]
    tmp = [b.sb("tmp%d" % i, [128, 512], F32, st=stM) for i in range(2)]
    it = 0
    for dcg in range(4):
        for n in range(4):
            wi = (dcg * 4 + n) % 3
            wm_, wmk = wmb[wi], ("wmb", wi)
            wb_, wbk = wbb[wi], ("wbb", wi)
            c0 = n * 2048 + dcg * 512
            S.dma("pool", wm_[:], w_ap(wm_d[:, c0:c0 + 512]), writes=[wmk])
            S.dma("pool", wb_[:], wbr_d[n * 512:(n + 1) * 512, dcg * 512:(dcg + 1) * 512].rearrange("(c p) d -> p c d", p=128), writes=[wbk])
            for j in range(4):
                psm2 = [b.psum() for _ in range(2)]
                for k in range(KCH):
                    for tt_ in range(2):
                        b.mm(psm2[tt_][0][:, :], wm_[:, k, j * 128:(j + 1) * 128], xb[:, k, HALO + tt_ * 512:HALO + (tt_ + 1) * 512],
                             k == 0, k == KCH - 1, [wmk, "xb"], [psm2[tt_][1]])
                psb2 = [b.psum() for _ in range(2)]
                for c4 in range(4):
                    for tt_ in range(2):
                        sl = slice(tt_ * 512, (tt_ + 1) * 512)
                        rhs = ygb[:, n * 4 + c4, sl] if n < 3 else ygd[:, c4, sl]
                        b.mm(psb2[tt_][0][:, :], wb_[:, c4, j * 128:(j + 1) * 128], rhs, c4 == 0, c4 == 3, [wbk, "ygb", "ygd"], [psb2[tt_][1]])
                for tt_ in range(2):
                    sl = slice(tt_ * 512, (tt_ + 1) * 512)
                    psm, km = psm2[tt_]
                    psb, kb = psb2[tt_]
                    m_ = mg[it % 2]; mk = ("mg", it % 2)
                    col = n * 16 + dcg * 4 + j
                    b.act(m_[:], psm[:, :], AF.Sigmoid, [km, "pm"], [mk], bias=pm[:, col:col + 1])
                    ak = ("macc", j, tt_)
                    if n == 0:
                        b.tt(macc[:, j, sl], m_[:], psb[:, :], ALU.mult, [mk, kb], [ak])
                    else:
                        t_ = tmp[it % 2]; tk = ("tmp", it % 2)
                        b.tt(t_[:], m_[:], psb[:, :], ALU.mult, [mk, kb], [tk])
                        b.tt(macc[:, j, sl], macc[:, j, sl], t_[:], ALU.add, [ak, tk], [ak])
                    it += 1
                    if n == 3:
                        b.cp(mixT[:, dcg * 4 + j, sl], macc[:, j, sl], [ak], ["mixT"], eng="act")
    S.barrier()
    stM.close()
    stDM.close()

    stO = contextlib.ExitStack()
    wob = b.sb("wob", [128, KCH, D_MODEL], BF16, st=stO)
    for eg in range(4):
        S.dma("pool", wob[:, :, eg * 512:(eg + 1) * 512], w_ap(wout_d[:, eg * 512:(eg + 1) * 512]), writes=[("wob", eg)])
    lng = b.sb("lng", [128, D_MODEL], F32, st=stO); lnb = b.sb("lnb", [128, D_MODEL], F32, st=stO)
    b.load(lng[:], lng_d, "lng"); b.load(lnb[:], lnb_d, "lnb")
    xt = [b.sb("xt%d" % i, [128, D_MODEL], F32, st=stO) for i in range(2)]
    z = [b.sb("z%d" % i, [128, D_MODEL], F32, st=stO) for i in range(2)]
    bst = b.sb("bst", [128, 4, 6], F32, st=stO)
    mv = b.sb("mv", [128, 4], F32, st=stO)
    outk = []
    for tt_ in range(NT // 128):
        x_ = xt[tt_ % 2]; xk = ("xt", tt_ % 2)
        z_ = z[tt_ % 2]; zk = ("z", tt_ % 2)
        b.load(x_[:], xown[tt_ * 128:(tt_ + 1) * 128, :], xk)
        pso = [b.psum() for _ in range(4)]
        for dc in range(KCH):
            for eg in range(4):
                b.mm(pso[eg][0][:, :], mixT[:, dc, tt_ * 128:(tt_ + 1) * 128], wob[:, dc, eg * 512:(eg + 1) * 512], dc == 0, dc == KCH - 1,
                     ["mixT", ("wob", eg)], [pso[eg][1]])
        for eg in range(4):
            es = slice(eg * 512, (eg + 1) * 512)
            ps, pk = pso[eg]
            b.stt(z_[:, es], x_[:, es], ALPHA, ps[:, :], ALU.mult, ALU.add, [xk, pk], [(zk, eg)])
            S.op("dve", lambda e, eg=eg, z_=z_, es=es: e.bn_stats(out=bst[:, eg, :], in_=z_[:, es]), [(zk, eg)], [("bst", eg)])
        S.op("dve", lambda e: e.bn_aggr(out=mv[:, 0:2], in_=bst[:].rearrange("p a b -> p (a b)")), [("bst", eg) for eg in range(4)], ["mv"])
        b.act(mv[:, 2:3], mv[:, 1:2], AF.Sqrt, ["mv"], ["mv2"], bias=LN_EPS)
        S.op("dve", lambda e: e.reciprocal(out=mv[:, 2:3], in_=mv[:, 2:3]), ["mv2"], ["mv2"])
        b.stt(mv[:, 3:4], mv[:, 0:1], -1.0, mv[:, 2:3], ALU.mult, ALU.mult, ["mv", "mv2"], ["mv3"])
        b.act(z_[:], z_[:], AF.Identity, [(zk, eg) for eg in range(4)] + ["mv2", "mv3"], [zk], bias=mv[:, 3:4], scale=mv[:, 2:3])
        b.tt(z_[:], z_[:], lng[:], ALU.mult, [zk, "lng"], [zk])
        b.tt(z_[:], z_[:], lnb[:], ALU.add, [zk, "lnb"], [zk] + [(zk, eg) for eg in range(4)], eng="pool")
        S.dma("sp", xo_d[tt_ * 128:(tt_ + 1) * 128, :], z_[:], reads=[zk], writes=[zk, ("xo", tt_)] + [(zk, eg) for eg in range(4)])
        outk.append(("xo", tt_))
    nc = b.finish(outk, extra=[stO])
    return nc, b.outs


def host_p2(inp, l, x_tok, ygT_full):
    w_in = inp["w_in"][l]; b_in = inp["b_in"][l]
    D0 = 1024 + 5120 + 2176
    M0 = D0 + 1536
    wm = np.ascontiguousarray(w_in[:, M0:M0 + 8192])
    wd = np.ascontiguousarray(w_in[:, D0:D0 + 1536])
    wbr = np.ascontiguousarray(inp["w_br"][l].reshape(2048, 2048))
    wout = np.ascontiguousarray(inp["w_out"][l])
    pm = np.ascontiguousarray(b_in[M0:M0 + 8192].reshape(64, 128).T)
    lng = np.ascontiguousarray(np.broadcast_to(inp["ln_g"][l][None, :], (128, 2048)))
    lnb = np.ascontiguousarray(np.broadcast_to(inp["ln_b"][l][None, :], (128, 2048)))
    dww = np.ascontiguousarray(inp["conf_dw_w"][l].T.reshape(4, 128, 31).transpose(1, 0, 2).reshape(128, 124))
    xT = x_tok.T
    maps = []
    for c in range(8):
        pd = np.zeros((128, 64), np.float32)
        bd = b_in[D0:D0 + 1536]
        for cc in range(4):
            pd[:, cc] = bd[cc * 128:(cc + 1) * 128]
            pd[:, 4 + cc] = bd[512 + cc * 128:512 + (cc + 1) * 128]
            pd[:, 8 + cc] = bd[1024 + cc * 128:1024 + (cc + 1) * 128]
            pd[:, 12 + cc] = inp["conf_dw_b"][l][cc * 128:(cc + 1) * 128]
            pd[:, 16 + cc] = inp["conf_ln_g"][l][cc * 128:(cc + 1) * 128]
            pd[:, 20 + cc] = inp["conf_ln_b"][l][cc * 128:(cc + 1) * 128]
        pd[:, 24] = 0.0 if c == 0 else 1.0
        t0 = c * NT
        xTh = np.zeros((2048, NT + HALO), np.float32)
        if c == 0:
            xTh[:, HALO:] = xT[:, 0:NT]
        else:
            xTh[:] = xT[:, t0 - HALO:t0 + NT]
        maps.append({"xTh": xTh, "xown": np.ascontiguousarray(x_tok[t0:t0 + NT]), "ygT": np.ascontiguousarray(ygT_full[:, t0:t0 + NT]),
                     "wm": wm, "wd": wd, "wbr": wbr, "wout": wout, "pm": pm, "pd": pd, "dww": dww, "lng": lng, "lnb": lnb})
    return maps


CORES = list(range(8))


def kernel(**inputs):
    inp = {k: np.asarray(v) for k, v in inputs.items()}
    x = np.ascontiguousarray(inp["x"][0], dtype=np.float32)
    for l in range(2):
        xT = np.ascontiguousarray(x.T)
        nc, _ = build_p1()
        r1 = run_bass_kernel_spmd(nc, host_p1(inp, l, xT), core_ids=CORES)
        ygA = np.concatenate([r["ygA"] for r in r1.results], axis=0)
        ygB = gather_p1b(r1.results)
        ygC = np.concatenate([r["ygC"] for r in r1.results], axis=0)
        ygT = np.ascontiguousarray(np.concatenate([ygA, ygB, ygC], axis=0))
        nc, _ = build_p2()
        r2 = run_bass_kernel_spmd(nc, host_p2(inp, l, x, ygT), core_ids=CORES)
        x = np.ascontiguousarray(np.concatenate([r["xo"] for r in r2.results], axis=0))
    return x[None].astype(np.float32)
```

```python
import contextlib
import numpy as np
import concourse.bass as bass
import concourse.mybir as mybir

F32 = mybir.dt.float32
BF16 = mybir.dt.bfloat16
AF = mybir.ActivationFunctionType
ALU = mybir.AluOpType
AX = mybir.AxisListType


class Sched:
    ENGS = ("pe", "dve", "act", "pool", "sp")

    def __init__(self, nc, stack, n_lanes=24, n_sw_lanes=16):
        self.nc = nc
        self.h = {"pe": nc.tensor, "dve": nc.vector, "act": nc.scalar,
                  "pool": nc.gpsimd, "sp": nc.sync}
        self.sem = {e: stack.enter_context(nc.semaphore("s_" + e)) for e in self.ENGS}
        self.cnt = {e: 0 for e in self.ENGS}
        self.n_hw = n_lanes
        self.lanes = [stack.enter_context(nc.semaphore("l%d" % i)) for i in range(n_lanes)]
        self.lanes += [stack.enter_context(nc.semaphore("w%d" % i)) for i in range(n_sw_lanes)]
        self.lane_tot = [0] * (n_lanes + n_sw_lanes)
        self.lane_next = 0
        self.sw_next = 0
        self.prog = {e: [] for e in self.ENGS}
        self.seen = {e: {} for e in self.ENGS}
        self.lastw = {}
        self.readers = {}
        self.semobj = {}
        self.unread = set()

    def _need(self, e, tok, waits):
        if tok is None:
            return
        k, v = tok
        if e == "pe" and k == ("e", "pe"):
            return
        if self.seen[e].get(k, 0) >= v:
            return
        self.seen[e][k] = v
        waits.append((k, v))

    def _semof(self, k):
        return self.sem[k[1]] if k[0] == "e" else self.lanes[k[1]]

    def _deps(self, e, reads, writes):
        waits = []
        for r in reads:
            self._need(e, self.lastw.get(r), waits)
        for w in writes:
            self._need(e, self.lastw.get(w), waits)
            for t in self.readers.get(w, ()):
                self._need(e, t, waits)
        return waits

    def _commit(self, tok, reads, writes):
        for r in reads:
            self.unread.discard(r)
            self.readers.setdefault(r, []).append(tok)
        for w in writes:
            self.lastw[w] = tok
            self.readers[w] = []

    def op(self, e, fn, reads=(), writes=()):
        waits = self._deps(e, reads, writes)
        self.cnt[e] += 1
        tok = (("e", e), self.cnt[e])
        self.prog[e].append((waits, fn, (self.sem[e], 1)))
        self._commit(tok, reads, writes)
        return tok

    def take_lane(self, q):
        if q == "pool":
            lane = self.n_hw + self.sw_next
            self.sw_next = (self.sw_next + 1) % (len(self.lanes) - self.n_hw)
        else:
            lane = self.lane_next
            self.lane_next = (self.lane_next + 1) % self.n_hw
        return lane

    def dma(self, q, out, in_, reads=(), writes=(), **kw):
        lane = self.take_lane(q)
        waits = self._deps(q, reads, writes)
        self._need(q, (("l", lane), self.lane_tot[lane]) if self.lane_tot[lane] else None, waits)
        self.lane_tot[lane] += 16
        tok = (("l", lane), self.lane_tot[lane])

        def fn(eng, out=out, in_=in_, kw=kw):
            return eng.dma_start(out=out, in_=in_, **kw)
        self.prog[q].append((waits, fn, (self.lanes[lane], 16)))
        self._commit(tok, reads, writes)
        return tok

    def wait_all(self, e, toks):
        waits = []
        for t in toks:
            self._need(e, t, waits)
        self.prog[e].append((waits, None, None))

    def barrier(self):
        toks = [(("e", e), self.cnt[e]) for e in self.ENGS if self.cnt[e]]
        toks += [(("l", i), t) for i, t in enumerate(self.lane_tot) if t]
        for e in self.ENGS:
            self.wait_all(e, toks)

    def emit(self, block):
        def mk(e):
            def body(eng):
                for waits, fn, inc in self.prog[e]:
                    for k, v in waits:
                        eng.wait_ge(self._semof(k), v)
                    if fn is not None:
                        ins = fn(eng)
                        ins.then_inc(inc[0], inc[1])
            return body
        block.tensor(mk("pe"))
        block.vector(mk("dve"))
        block.scalar(mk("act"))
        block.gpsimd(mk("pool"))
        block.sync(mk("sp"))


import contextlib
import numpy as np
import concourse.bass as bass
import concourse.mybir as mybir
from concourse.bass_utils import run_bass_kernel_spmd

F32 = mybir.dt.float32
BF16 = mybir.dt.bfloat16
AF = mybir.ActivationFunctionType
ALU = mybir.AluOpType

D_MODEL = 2048
SEQ = 8192
KCH = 16


class B:
    def __init__(self):
        self.nc = bass.Bass("TRN2", target_bir_lowering=False, num_devices=8)
        self.st = contextlib.ExitStack()
        self.S = Sched(self.nc, self.st)
        self.ps = [self.st.enter_context(self.nc.psum_tensor("ps%d" % i, [128, 512], F32)) for i in range(8)]
        self.ps_i = 0
        self.uid = 0
        self.outs = []
        self.pfx = ""

    def din(self, name, shape, dt=F32):
        return self.nc.dram_tensor(name, list(shape), dt, kind="ExternalInput").ap()

    def dout(self, name, shape, dt=F32):
        self.outs.append(name)
        return self.nc.dram_tensor(name, list(shape), dt, kind="ExternalOutput").ap()

    def sb(self, name, shape, dt=F32, st=None):
        return (st or self.st).enter_context(self.nc.sbuf_tensor("sb_" + self.pfx + name, list(shape), dt))

    def psum(self):
        i = self.ps_i
        self.ps_i = (i + 1) % 8
        if ("ps", i) in self.S.unread:
            raise RuntimeError("PSUM bank %d handed out again before its previous contents were read" % i)
        self.S.unread.add(("ps", i))
        return self.ps[i], ("ps", i)

    def key(self, p="t"):
        self.uid += 1
        return (p, self.uid)

    def finish(self, out_keys, extra=()):
        S = self.S
        S.wait_all("sp", [S.lastw[k] for k in out_keys])
        with self.nc.Block() as block:
            S.emit(block)
        for e in extra:
            e.close()
        self.st.close()
        return self.nc

    def mm(self, out, lhsT, rhs, start, stop, reads, writes):
        return self.S.op("pe", lambda e: e.matmul(out, lhsT=lhsT, rhs=rhs, start=start, stop=stop), reads, writes)

    def tr(self, out, in_, ident, reads, writes):
        return self.S.op("pe", lambda e: e.transpose(out=out, in_=in_, identity=ident), reads, writes)

    def act(self, out, in_, func, reads, writes, bias=None, scale=None, eng="act"):
        kw = {}
        if bias is not None:
            kw["bias"] = bias
        if scale is not None:
            kw["scale"] = scale
        return self.S.op("act", lambda e: e.activation(out=out, in_=in_, func=func, **kw), reads, writes)

    def tt(self, out, in0, in1, op, reads, writes, eng="dve"):
        return self.S.op(eng, lambda e: e.tensor_tensor(out=out, in0=in0, in1=in1, op=op), reads, writes)

    def ts(self, out, in0, s1, s2, op0, op1, reads, writes, eng="dve"):
        if op1 is None:
            return self.S.op(eng, lambda e: e.tensor_scalar(out=out, in0=in0, scalar1=s1, scalar2=None, op0=op0), reads, writes)
        return self.S.op(eng, lambda e: e.tensor_scalar(out=out, in0=in0, scalar1=s1, scalar2=s2, op0=op0, op1=op1), reads, writes)

    def stt(self, out, in0, scalar, in1, op0, op1, reads, writes):
        return self.S.op("dve", lambda e: e.scalar_tensor_tensor(out=out, in0=in0, scalar=scalar, in1=in1, op0=op0, op1=op1), reads, writes)

    def cp(self, out, in_, reads, writes, eng="dve"):
        if eng == "act":
            return self.S.op("act", lambda e: e.copy(out=out, in_=in_), reads, writes)
        return self.S.op(eng, lambda e: e.tensor_copy(out=out, in_=in_), reads, writes)

    def memset(self, ap, val, writes, eng="pool"):
        return self.S.op(eng, lambda e: e.memset(ap, val), (), writes)

    def load(self, sb_ap, dram_ap, key, q="sp", **kw):
        return self.S.dma(q, sb_ap, dram_ap, writes=[key], **kw)

    def _lane_op(self, q, fn, reads, writes):
        S = self.S
        lane = S.take_lane(q)
        waits = S._deps(q, reads, writes)
        S._need(q, (("l", lane), S.lane_tot[lane]) if S.lane_tot[lane] else None, waits)
        S.lane_tot[lane] += 16
        tok = (("l", lane), S.lane_tot[lane])
        S.prog[q].append((waits, fn, (S.lanes[lane], 16)))
        S._commit(tok, reads, writes)
        return tok

    def coll(self, kind, src, dst, reads, writes):
        return self._lane_op("pool", lambda e: e.collective_compute(kind, ALU.bypass, replica_groups=[list(range(8))],
                                                                    ins=[src], outs=[dst]), reads, writes)

    def gather(self, out, in_, idx, reads, writes):
        return self._lane_op("pool", lambda e: e.indirect_dma_start(out=out, out_offset=None, in_=in_,
                                                                    in_offset=bass.IndirectOffsetOnAxis(ap=idx, axis=0)), reads, writes)

    def ident(self, n=128):
        ones = self.sb("c_ones", [128, 512], F32)
        ident = self.sb("c_ident", [128, 128], F32)
        self.memset(ones[:], 1.0, ["c_ones"])
        self.S.op("pool", lambda e: e.affine_select(out=ident[:], in_=ones[:, 0:128], pattern=[[1, 128]],
                                                    compare_op=ALU.is_equal, fill=0.0, base=0, channel_multiplier=-1),
                  ["c_ones"], ["c_ident"])
        return ones, ident


def xT_tile_ap(xT, t0, n):
    return xT[:, t0:t0 + n].rearrange("(k p) t -> p k t", p=128)


def w_ap(w):
    return w.rearrange("(k p) c -> p k c", p=128)


def emit_p1a(b, st, xT, wa_d, pa_d, gw_d, yg_d, T=SEQ, SEG=512):
    S = b.S
    b.pfx = "a_"
    nseg = T // SEG

    wa = b.sb("wa", [128, KCH, 128], BF16, st=st)
    pa = b.sb("pa", [64, 16], st=st)
    gw = b.sb("gw", [64, 128], st=st)
    c1 = b.sb("c1", [64, 4], st=st)
    SUP = 4
    xb = [b.sb("xb%d" % i, [128, KCH, SUP * SEG], BF16, st=st) for i in range(2)]
    sgs = b.sb("sgs", [64, SUP * SEG], st=st)
    axb = b.sb("axb", [64, 4 * SEG + 3], st=st)
    u = b.sb("u", [64, SEG], st=st)
    gr = b.sb("gr", [64, SEG], st=st)
    gi = b.sb("gi", [64, SEG], st=st)
    at = b.sb("at", [64, SEG], st=st)
    a2 = b.sb("a2", [64, SEG], st=st)
    bt = b.sb("bt", [64, SEG], st=st)
    hb = [b.sb("hb%d" % i, [64, SEG], st=st) for i in range(2)]
    sg = b.sb("sg", [64, SEG], st=st)
    yo = [b.sb("yo%d" % i, [64, SEG], st=st) for i in range(2)]

    S.dma("pool", wa[:], w_ap(wa_d), writes=["wa"])
    b.load(pa[:], pa_d, "pa")
    b.load(gw[:], gw_d, "gw")
    b.memset(axb[:, 0:3], 0.0, ["axb"])
    b.act(c1[:, 2:3], pa[:, 9:10], AF.Exp, ["pa"], ["c1"], scale=-1.0)
    b.act(c1[:, 3:4], c1[:, 2:3], AF.Ln, ["c1"], ["c1"], bias=1.0)
    b.ts(c1[:, 0:1], c1[:, 3:4], -8.0, None, ALU.mult, None, ["c1"], ["c1"])
    b.ts(c1[:, 1:2], c1[:, 3:4], -16.0, None, ALU.mult, None, ["c1"], ["c1"])

    for s in range(nseg):
        sup, sub = s // SUP, s % SUP
        xs = xb[sup % 2]
        xk = ("xb", sup % 2)
        if sub == 0:
            S.dma("pool", xs[:], xT_tile_ap(xT, sup * SUP * SEG, SUP * SEG), writes=[xk])
            nsub = min(SUP, nseg - s)
            pa_ps = [b.psum() for _ in range(nsub)]
            for k in range(KCH):
                for q in range(nsub):
                    b.mm(pa_ps[q][0][0:64, :SEG], wa[:, k, 0:64], xs[:, k, q * SEG:(q + 1) * SEG], k == 0, k == KCH - 1, ["wa", xk], [pa_ps[q][1]])
            for q in range(nsub):
                b.act(axb[:, 3 + q * SEG:3 + (q + 1) * SEG], pa_ps[q][0][0:64, :SEG], AF.Identity, [pa_ps[q][1], "pa"], ["axb"], bias=pa[:, 0:1])
            pg_ps = [b.psum() for _ in range(nsub)]
            for k in range(KCH):
                for q in range(nsub):
                    b.mm(pg_ps[q][0][0:64, :SEG], wa[:, k, 64:128], xs[:, k, q * SEG:(q + 1) * SEG], k == 0, k == KCH - 1, ["wa", xk], [pg_ps[q][1]])
            for q in range(nsub):
                b.act(sgs[:, q * SEG:(q + 1) * SEG], pg_ps[q][0][0:64, :SEG], AF.Silu, [pg_ps[q][1], "pa"], [("sgs", q)], bias=pa[:, 1:2])
        o0 = sub * SEG
        b.ts(u[:], axb[:, o0:o0 + SEG], pa[:, 3:4], pa[:, 2:3], ALU.mult, ALU.add, ["axb", "pa"], ["u"])
        for j in range(1, 4):
            b.stt(u[:], axb[:, o0 + j:o0 + j + SEG], pa[:, 3 + j:4 + j], u[:], ALU.mult, ALU.add, ["axb", "pa", "u"], ["u"])
        if sub == SUP - 1 or s == nseg - 1:
            b.cp(axb[:, 0:3], axb[:, (sub + 1) * SEG:(sub + 1) * SEG + 3], ["axb", "u"], ["axb"], eng="pool")
        ps3, pk3 = b.psum()
        b.mm(ps3[0:64, :SEG], gw[:, 0:64], u[:], True, True, ["gw", "u"], [pk3])
        b.act(gr[:], ps3[0:64, :SEG], AF.Sigmoid, [pk3, "pa"], ["gr"], bias=pa[:, 7:8])
        ps4, pk4 = b.psum()
        b.mm(ps4[0:64, :SEG], gw[:, 64:128], u[:], True, True, ["gw", "u"], [pk4])
        b.act(gi[:], ps4[0:64, :SEG], AF.Sigmoid, [pk4, "pa"], ["gi"], bias=pa[:, 8:9])
        b.act(at[:], gr[:], AF.Exp, ["gr", "c1"], ["at"], scale=c1[:, 0:1])
        b.act(a2[:], gr[:], AF.Exp, ["gr", "c1"], ["a2"], scale=c1[:, 1:2])
        b.ts(a2[:], a2[:], -1.0, 1.0, ALU.mult, ALU.add, ["a2"], ["a2"])
        b.act(a2[:], a2[:], AF.Sqrt, ["a2"], ["a2"])
        b.tt(bt[:], gi[:], u[:], ALU.mult, ["gi", "u"], ["bt"])
        b.tt(bt[:], bt[:], a2[:], ALU.mult, ["bt", "a2"], ["bt"])
        hcur = hb[s % 2]
        hprev = hb[(s + 1) % 2]
        init = 0.0 if s == 0 else hprev[:, SEG - 1:SEG]
        S.op("dve", lambda e, hcur=hcur, init=init: e.tensor_tensor_scan(out=hcur[:], data0=at[:], data1=bt[:], initial=init,
                                                                        op0=ALU.mult, op1=ALU.add),
             ["at", "bt", ("hb", (s + 1) % 2)], [("hb", s % 2)])
        yt = yo[s % 2]
        b.tt(yt[:], hcur[:], sgs[:, sub * SEG:(sub + 1) * SEG], ALU.mult, [("hb", s % 2), ("sgs", sub)], [("yo", s % 2)], eng="pool")
        S.dma("sp", yg_d[:, s * SEG:(s + 1) * SEG], yt[:], reads=[("yo", s % 2)], writes=[("ygd", s)])
    return [("ygd", s) for s in range(nseg)]


def build_p1a(T=SEQ, SEG=512):
    b = B()
    st = contextlib.ExitStack()
    keys = emit_p1a(b, st, b.din("xT", [D_MODEL, T]), b.din("wa", [D_MODEL, 128]), b.din("pa", [64, 16]),
                    b.din("gw", [64, 128]), b.dout("ygA", [64, T]), T, SEG)
    return b.finish(keys, extra=[st]), b.outs


def host_p1a(inp, l, xT):
    sp = np.cumsum([0, 512, 512])
    w_in = inp["w_in"][l]
    b_in = inp["b_in"][l]
    maps = []
    for c in range(8):
        cs = slice(64 * c, 64 * c + 64)
        wa = np.concatenate([w_in[:, 0:512][:, cs], w_in[:, 512:1024][:, cs]], axis=1)
        pa = np.zeros((64, 16), np.float32)
        pa[:, 0] = b_in[0:512][cs]
        pa[:, 1] = b_in[512:1024][cs]
        pa[:, 2] = inp["lru_conv_b"][l][cs]
        pa[:, 3:7] = inp["lru_conv_w"][l][:, cs].T
        pa[:, 7] = inp["lru_gate_a_b"][l][cs]
        pa[:, 8] = inp["lru_gate_x_b"][l][cs]
        pa[:, 9] = inp["lru_lambda"][l][cs]
        gw = np.concatenate([inp["lru_gate_a_w"][l][c], inp["lru_gate_x_w"][l][c]], axis=1)
        maps.append({"xT": xT, "wa": np.ascontiguousarray(wa), "pa": pa, "gw": np.ascontiguousarray(gw)})
    return maps


NQ = 4096
NKV = 6144
HAL = 2048
DILS = (1, 4, 16)
N_BUCKETS = 32
MAX_DISTANCE = 2048


def emit_p1b(b, st, ones, ident, xTh, wq_d, wk_d, wv_d, wg_d, pb_d, bias_d, yg_d):
    S = b.S
    b.pfx = "b_"
    onesb = b.sb("onesb", [128, 128], BF16, st=st)
    b.cp(onesb[:], ones[:, 0:128], ["c_ones"], ["onesb"])
    pb = b.sb("pb", [128, 16], st=st); b.load(pb[:], pb_d, "pb")
    eb = b.sb("eb", [128, 3, 2, 128], st=st); ebf = b.sb("ebf", [128, 3, 2, 128], st=st)
    b.load(eb[:], bias_d.rearrange("p (g k q) -> p g k q", g=3, k=2), "eb")
    b.act(eb[:], eb[:], AF.Exp, ["eb"], ["eb"])
    b.cp(ebf[:], eb[:], ["eb"], ["ebf"])
    for g in range(3):
        b.ts(ebf[:, g, 0, :], ebf[:, g, 0, :], pb[:, 10:11], None, ALU.mult, None, ["ebf", "pb"], ["ebf"])
    wq = b.sb("wq", [128, KCH, 384], BF16, st=st); wk = b.sb("wk", [128, KCH, 384], BF16, st=st); wv = b.sb("wv", [128, KCH, 384], BF16, st=st)
    wg = b.sb("wg", [128, KCH, 128], BF16, st=st)
    S.dma("pool", wk[:], w_ap(wk_d), writes=["wk"]); S.dma("pool", wv[:], w_ap(wv_d), writes=["wv"])
    S.dma("pool", wq[:], w_ap(wq_d), writes=["wq"]); S.dma("pool", wg[:], w_ap(wg_d), writes=["wg"])
    xb = [b.sb("xb%d" % i, [128, KCH, 512], BF16, st=st) for i in range(2)]
    KT = b.sb("KT", [128, NKV], BF16, st=st)
    VT = b.sb("VT", [128, NKV], F32, st=st)
    QT = b.sb("QT", [128, NQ], BF16, st=st)
    bg = b.sb("bg", [128, NQ], F32, st=st)
    ND = b.sb("ND", [128, 2, NQ], F32, st=st)
    esb = [b.sb("esb%d" % i, [128, 256], F32, st=st) for i in range(2)]
    PT = [b.sb("PT%d" % i, [128, 2, 128], BF16, st=st) for i in range(2)]
    Vt = [b.sb("Vt%d" % i, [128, 2, 128], BF16, st=st) for i in range(2)]
    scale = 128.0 ** -0.5
    xi = 0
    bi = 0
    for g, dil in enumerate(DILS):
        start = HAL - max(512, 128 * dil)
        tl_all = list(range(start, NKV, 512))
        for pi in range(0, len(tl_all), 1):
            pair = tl_all[pi:pi + 1]
            xss = []
            for t0 in pair:
                xs = xb[xi % 2]; xk = ("xb", xi % 2); xi += 1
                S.dma("pool", xs[:], xT_tile_ap(xTh, t0, 512), writes=[xk])
                xss.append((xs, xk, t0))
            secs = [("k", wk, "wk"), ("v", wv, "wv"), ("q", wq, "wq")] + ([("g", wg, "wg")] if g == 0 else [])
            for nm, wt, wkey in secs:
                xsel = xss if nm in ("k", "v") else [t for t in xss if t[2] >= HAL]
                if not xsel:
                    continue
                pss = [b.psum() for _ in xsel]
                for k in range(KCH):
                    for (ps_, pk_), (xs, xk, t0) in zip(pss, xsel):
                        lhs = wt[:, k, :] if nm == "g" else wt[:, k, g * 128:(g + 1) * 128]
                        b.mm(ps_[:, :], lhs, xs[:, k, :], k == 0, k == KCH - 1, [wkey, xk], [pk_])
                for (ps_, pk_), (xs, xk, t0) in zip(pss, xsel):
                    if nm == "k":
                        b.act(KT[:, t0:t0 + 512], ps_[:, :], AF.Identity, [pk_, "pb"], [("KT", t0)], bias=pb[:, 3 + g:4 + g])
                    elif nm == "v":
                        b.ts(VT[:, t0:t0 + 512], ps_[:, :], pb[:, 6 + g:7 + g], None, ALU.add, None, [pk_, "pb"], [("VT", t0)])
                    elif nm == "q":
                        b.act(QT[:, t0 - HAL:t0 - HAL + 512], ps_[:, :], AF.Identity, [pk_, "pb"], [("QT", t0 - HAL)], bias=pb[:, g:g + 1])
                    else:
                        b.act(bg[:, t0 - HAL:t0 - HAL + 512], ps_[:, :], AF.Silu, [pk_, "pb"], ["bg"], bias=pb[:, 9:10])
        span = 128 * dil
        allK = [("KT", t) for t in range(start, NKV, 512)]
        allV = [("VT", t) for t in range(start, NKV, 512)]
        allQ = [("QT", t) for t in range(0, NQ, 512)]

        def tiles_of(keys, base, lo, hi):
            return [(keys, t) for t in range(base, 99999, 512) if t < hi and t + 512 > lo]
        for r in range(dil):
            for nl in range(32 // dil):
                qs = r + span * nl
                ssl = lambda a: slice(a, a + 127 * dil + 1, dil)
                qsl = ssl(qs)
                kc_ = ssl(HAL + qs)
                kp_ = ssl(HAL + qs - span)
                qk = [("QT", t) for t in range(0, NQ, 512) if t < qs + span and t + 512 > qs]
                kk_ = [("KT", t) for t in range(start, NKV, 512) if t < HAL + qs + span and t + 512 > HAL + qs - span]
                vk_ = [("VT", t) for t in range(start, NKV, 512) if t < HAL + qs + span and t + 512 > HAL + qs - span]
                psl, kl = b.psum()
                b.mm(psl[:, 0:128], KT[:, kp_], QT[:, qsl], True, True, kk_ + qk, [kl])
                b.mm(psl[:, 128:256], KT[:, kc_], QT[:, qsl], True, True, kk_ + qk, [kl])
                e_ = esb[bi % 2]; ek = ("esb", bi % 2)
                b.act(e_[:], psl[:, 0:256], AF.Exp, [kl], [ek], scale=scale)
                p_ = PT[bi % 2]; pk_ = ("PT", bi % 2)
                ebsel = ebf if nl == 0 else eb
                b.tt(p_[:].rearrange("p a q -> p (a q)"), e_[:], ebsel[:, g].rearrange("p a q -> p (a q)"), ALU.mult, [ek, "eb", "ebf"], [pk_])
                psv, kv = b.psum()
                b.tr(psv[:, 0:128], VT[:, kp_], ident[:], vk_ + ["c_ident"], [kv])
                b.tr(psv[:, 128:256], VT[:, kc_], ident[:], vk_ + ["c_ident"], [kv])
                v_ = Vt[bi % 2]; vk2 = ("Vt", bi % 2)
                b.cp(v_[:].rearrange("p a q -> p (a q)"), psv[:, 0:256], [kv], [vk2], eng="act")
                pso, ko = b.psum()
                b.mm(pso[:, 0:128], v_[:, 0, :], p_[:, 0, :], True, False, [vk2, pk_], [ko])
                b.mm(pso[:, 0:128], v_[:, 1, :], p_[:, 1, :], False, True, [vk2, pk_], [ko])
                b.mm(pso[:, 128:256], onesb[:], p_[:, 0, :], True, False, ["onesb", pk_], [ko])
                b.mm(pso[:, 128:256], onesb[:], p_[:, 1, :], False, True, ["onesb", pk_], [ko])
                src = pso[:, 0:256].rearrange("p (a q) -> p a q", a=2)
                if g == 0:
                    b.cp(ND[:, :, qsl], src, [ko], ["ND"], eng="act")
                else:
                    b.tt(ND[:, :, qsl], ND[:, :, qsl], src, ALU.add, ["ND", ko], ["ND"])
                bi += 1
    S.op("dve", lambda e: e.reciprocal(out=ND[:, 1, :], in_=ND[:, 1, :]), ["ND"], ["ND"])
    b.tt(ND[:, 0, :], ND[:, 0, :], ND[:, 1, :], ALU.mult, ["ND"], ["ND"])
    b.tt(ND[:, 0, :], ND[:, 0, :], bg[:], ALU.mult, ["ND", "bg"], ["ND"], eng="pool")
    S.dma("sp", yg_d, ND[:, 0, :], reads=["ND"], writes=["outB"])
    return ["outB"]


def p1b_dram(b):
    return (b.din("xTh", [D_MODEL, NKV]), b.din("wq", [D_MODEL, 384]), b.din("wk", [D_MODEL, 384]), b.din("wv", [D_MODEL, 384]),
            b.din("wg", [D_MODEL, 128]), b.din("pb", [128, 16]), b.din("biasT", [128, 3 * 2 * 128]), b.dout("ygB", [128, NQ]))


def build_p1b():
    b = B()
    ones, ident = b.ident()
    st = contextlib.ExitStack()
    keys = emit_p1b(b, st, ones, ident, *p1b_dram(b))
    return b.finish(keys, extra=[st]), b.outs


def t5_bucket(dist):
    import math
    max_exact = N_BUCKETS // 2
    large = max_exact + (np.log(np.maximum(dist, 1) / max_exact) / math.log(MAX_DISTANCE / max_exact)
                         * (N_BUCKETS - max_exact)).astype(np.int32)
    large = np.minimum(large, N_BUCKETS - 1)
    return np.where(dist < max_exact, dist, large).astype(np.int32)


def host_p1b(inp, l, xT):
    w_in = inp["w_in"][l]; b_in = inp["b_in"][l]
    Q0 = 1024; K0 = Q0 + 1536; V0 = K0 + 1536; G0 = V0 + 1536
    table = inp["att_rel_bias"]
    ki = np.arange(128)[:, None, None]; kb = np.arange(2)[None, :, None]; qi = np.arange(128)[None, None, :]
    dist = qi + 128 - (kb * 128 + ki)
    valid = (dist >= 0) & (dist <= 128)
    maps = []
    for c in range(8):
        hm, half = c // 2, c % 2
        cols = lambda base: np.concatenate([w_in[:, base + (g * 4 + hm) * 128: base + (g * 4 + hm + 1) * 128] for g in range(3)], axis=1)
        pb = np.zeros((128, 16), np.float32)
        for g in range(3):
            hh = (g * 4 + hm) * 128
            pb[:, g] = b_in[Q0 + hh:Q0 + hh + 128]
            pb[:, 3 + g] = b_in[K0 + hh:K0 + hh + 128]
            pb[:, 6 + g] = b_in[V0 + hh:V0 + hh + 128]
        pb[:, 9] = b_in[G0 + hm * 128:G0 + hm * 128 + 128]
        pb[:, 10] = float(half)
        bT = np.zeros((128, 3, 2, 128), np.float32)
        for g, dil in enumerate(DILS):
            bucket = t5_bucket(np.clip(dist, 0, 128) * dil)
            bT[:, g] = np.where(valid, table[bucket, g * 4 + hm], np.float32(-100.0))
        xh = np.zeros((2048, NKV), np.float32)
        base = half * NQ
        if half == 0:
            xh[:, HAL:] = xT[:, 0:NQ]
        else:
            xh[:] = xT[:, base - HAL:base + NQ]
        maps.append({"xTh": xh, "wq": np.ascontiguousarray(cols(Q0)), "wk": np.ascontiguousarray(cols(K0)),
                     "wv": np.ascontiguousarray(cols(V0)), "wg": np.ascontiguousarray(w_in[:, G0 + hm * 128:G0 + (hm + 1) * 128]),
                     "pb": pb, "biasT": bT.reshape(128, 768)})
    return maps


def gather_p1b(results):
    yg = np.zeros((512, 8192), np.float32)
    for c, r in enumerate(results):
        hm, half = c // 2, c % 2
        yg[hm * 128:(hm + 1) * 128, half * NQ:(half + 1) * NQ] = r["ygB"]
    return yg


SEGC = 1024
NW = 3
GN_EPS = 64e-5
DEC_C = float(np.exp(-0.5))


def emit_p1c(b, st, ones, ident, xT, wc_d, pc_d, wup_d, gnb_d, yg_d, T=SEQ):
    S = b.S
    b.pfx = "c_"
    nseg = T // SEGC
    NCH = SEGC // 128
    pc = b.sb("pc", [64, 32], st=st); b.load(pc[:], pc_d, "pc")
    wup = b.sb("wup", [64, 128], st=st); b.load(wup[:], wup_d, "wup")
    gnb = b.sb("gnb", [128, 192], st=st); b.load(gnb[:], gnb_d, "gnb")
    wc = b.sb("wc", [128, KCH, 384], BF16, st=st)
    S.dma("pool", wc[:], w_ap(wc_d), writes=["wc"])
    pcx = b.sb("pcx", [64, 4], st=st)
    b.ts(pcx[:, 0:1], pc[:, 13:14], -1.0, 1.0, ALU.mult, ALU.add, ["pc"], ["pcx"])
    mask2 = b.sb("mask2", [128, 2, 128], st=st); maskL = b.sb("maskL", [128, 128], st=st); maskc = b.sb("maskc", [64, SEGC], st=st)
    S.op("pool", lambda e: e.affine_select(out=mask2[:, 0, :], in_=ones[:, 0:128], pattern=[[1, 128]], compare_op=ALU.is_gt,
                                           fill=0.0, base=0, channel_multiplier=-1), ["c_ones"], ["mask2"])
    S.op("pool", lambda e: e.affine_select(out=mask2[:, 1, :], in_=ones[:, 0:128], pattern=[[1, 128]], compare_op=ALU.is_ge,
                                           fill=0.0, base=0, channel_multiplier=-1), ["c_ones"], ["mask2"])
    S.op("pool", lambda e: e.affine_select(out=maskL[:], in_=ones[:, 0:128], pattern=[[-1, 128]], compare_op=ALU.is_gt,
                                           fill=0.0, base=0, channel_multiplier=1), ["c_ones"], ["maskL"])
    b.memset(maskc[:], 1.0, ["maskc"])
    b.memset(maskc[:, 0:SEGC - 127:128], 0.0, ["maskc"])

    xb = [b.sb("xb%d" % i, [128, KCH, 512], BF16, st=st) for i in range(2)]
    raw = [b.sb("raw%d" % i, [64, SEGC + 1], st=st) for i in range(5)]
    sh = [b.sb("sh%d" % i, [64, SEGC], st=st) for i in range(5)]
    names = ["lw", "aicl", "kk", "kc", "bb", "cum", "tmp1", "tmp2", "bt", "kt", "bh", "kh", "rkr"]
    A = {n: b.sb(n, [64, SEGC], st=st) for n in names}
    atrt = b.sb("atrt", [64, 2, SEGC], st=st)
    wl = b.sb("wl", [64, NCH], st=st)
    sgc = b.sb("sgc", [64, SEGC], st=st)
    obuf = [b.sb("obuf%d" % i, [64, SEGC], st=st) for i in range(2)]
    ST = [b.sb("ST%d" % i, [64, 64], st=st) for i in range(2)]
    tok4 = [b.sb("tok4_%d" % i, [128, 5, 64], st=st) for i in range(NW)]
    AB = [b.sb("AB%d" % i, [128, 2, 128], st=st) for i in range(NW)]
    AK = [b.sb("AK%d" % i, [128, 2, 128], st=st) for i in range(NW)]
    Mb = [[b.sb("Mb%d_%d" % (q, i), [128, 128], st=st) for i in range(2)] for q in range(NW)]
    MTb = [[b.sb("MTb%d_%d" % (q, i), [128, 128], st=st) for i in range(2)] for q in range(NW)]
    TTb = [[b.sb("TTb%d_%d" % (q, i), [128, 128], st=st) for i in range(2)] for q in range(NW)]
    Xsb = [b.sb("Xsb%d" % q, [128, 64], st=st) for q in range(NW)]
    PQ = [b.sb("PQ%d" % i, [128, 2, 64], st=st) for i in range(NW)]
    GT = [b.sb("GT%d" % q, [64, 64], st=st) for q in range(NW)]; Hs = [b.sb("Hs%d" % q, [64, 64], st=st) for q in range(NW)]
    R2T = [b.sb("R2T%d" % q, [64, 128], st=st) for q in range(NW)]
    bst = [b.sb("bst%d" % q, [128, 6], st=st) for q in range(NW)]; mv = [b.sb("mv%d" % q, [128, 4], st=st) for q in range(NW)]
    bsc = [b.sb("bsc%d" % q, [128, 1], st=st) for q in range(NW)]
    yn = [b.sb("yn%d" % i, [128, 64], st=st) for i in range(NW)]

    b.memset(ST[0][:], 0.0, [("ST", 0)])
    for g in range(5):
        b.memset(raw[g][:, 0:1], 0.0, [("raw", g)])
    xi = 0
    stc = 0
    outk = []
    for s in range(nseg):
        NTL = SEGC // 512

        def load_x(seg):
            nonlocal xi
            out = []
            for tl in range(NTL):
                xs = xb[xi % 2]; xk = ("xb", xi % 2); xi += 1
                S.dma("pool", xs[:], xT_tile_ap(xT, seg * SEGC + tl * 512, 512), writes=[xk])
                out.append((xs, xk))
            return out
        xss = x_next if s > 0 else load_x(0)
        for g in range(6):
            pss = [b.psum() for _ in range(NTL)]
            for k in range(KCH):
                for tl in range(NTL):
                    b.mm(pss[tl][0][0:64, :], wc[:, k, g * 64:(g + 1) * 64], xss[tl][0][:, k, :], k == 0, k == KCH - 1, ["wc", xss[tl][1]], [pss[tl][1]])
            for tl in range(NTL):
                ps, pk = pss[tl]
                if g < 5:
                    b.act(raw[g][:, 1 + tl * 512:1 + (tl + 1) * 512], ps[0:64, :], AF.Identity, [pk, "pc"], [("raw", g)], bias=pc[:, g:g + 1])
                else:
                    b.act(sgc[:, tl * 512:(tl + 1) * 512], ps[0:64, :], AF.Silu, [pk, "pc"], ["sgc"], bias=pc[:, 16:17])
        x_next = load_x(s + 1) if s + 1 < nseg else None
        for g in range(5):
            b.tt(sh[g][:], raw[g][:, 0:SEGC], raw[g][:, 1:SEGC + 1], ALU.subtract, [("raw", g)], [("sh", g)])
            b.stt(sh[g][:], sh[g][:], pc[:, 5 + g:6 + g], raw[g][:, 1:SEGC + 1], ALU.mult, ALU.add, [("sh", g), "pc", ("raw", g)], [("sh", g)])
            b.cp(raw[g][:, 0:1], raw[g][:, SEGC:SEGC + 1], [("raw", g)], [("raw", g)], eng="pool")
        s_r, s_k, s_v, s_w, s_a = sh
        kr, kk_, kv_, kw_, ka_ = [("sh", g) for g in range(5)]
        b.act(s_w[:], s_w[:], AF.Tanh, [kw_], [kw_])
        for tl in range(2):
            sl = slice(tl * 512, (tl + 1) * 512)
            ps, pk = b.psum()
            b.mm(ps[0:64, :], wup[:, 0:64], s_w[:, sl], True, True, ["wup", kw_], [pk])
            b.act(A["lw"][:, sl], ps[0:64, :], AF.Sigmoid, [pk, "pc"], ["lw"], bias=pc[:, 10:11])
            ps, pk = b.psum()
            b.mm(ps[0:64, :], wup[:, 64:128], s_a[:, sl], True, True, ["wup", ka_], [pk])
            b.act(A["aicl"][:, sl], ps[0:64, :], AF.Sigmoid, [pk, "pc"], ["aicl"], bias=pc[:, 11:12])
        b.ts(A["lw"][:], A["lw"][:], -DEC_C, None, ALU.mult, None, ["lw"], ["lw"])
        b.ts(A["kk"][:], s_k[:], pc[:, 12:13], None, ALU.mult, None, [kk_, "pc"], ["kk"])
        b.tt(A["tmp1"][:], A["kk"][:], A["kk"][:], ALU.mult, ["kk"], ["tmp1"])
        for tl in range(2):
            sl = slice(tl * 512, (tl + 1) * 512)
            ps, pk = b.psum()
            b.mm(ps[0:64, :], ones[0:64, 0:64], A["tmp1"][:, sl], True, True, ["c_ones", "tmp1"], [pk])
            b.act(A["tmp2"][:, sl], ps[0:64, :], AF.Sqrt, [pk], ["tmp2"])
        b.ts(A["tmp2"][:], A["tmp2"][:], 1e-12, None, ALU.max, None, ["tmp2"], ["tmp2"])
        S.op("dve", lambda e: e.reciprocal(out=A["tmp2"][:], in_=A["tmp2"][:]), ["tmp2"], ["tmp2"])
        b.tt(A["kk"][:], A["kk"][:], A["tmp2"][:], ALU.mult, ["kk", "tmp2"], ["kk"])
        b.ts(A["tmp1"][:], A["aicl"][:], pc[:, 13:14], pcx[:, 0:1], ALU.mult, ALU.add, ["aicl", "pc", "pcx"], ["tmp1"])
        b.tt(A["kc"][:], s_k[:], A["tmp1"][:], ALU.mult, [kk_, "tmp1"], ["kc"])
        b.tt(A["bb"][:], A["kk"][:], A["aicl"][:], ALU.mult, ["kk", "aicl"], ["bb"], eng="pool")
        S.op("dve", lambda e: e.tensor_tensor_scan(out=A["cum"][:], data0=maskc[:], data1=A["lw"][:], initial=0.0, op0=ALU.mult, op1=ALU.add),
             ["maskc", "lw"], ["cum"])
        b.tt(A["tmp1"][:], A["cum"][:], A["lw"][:], ALU.subtract, ["cum", "lw"], ["tmp1"])
        b.act(A["tmp1"][:], A["tmp1"][:], AF.Exp, ["tmp1"], ["tmp1"])
        b.stt(atrt[:, 0, :], A["kk"][:], -1.0, A["tmp1"][:], ALU.mult, ALU.mult, ["kk", "tmp1"], ["atrt"])
        b.act(A["tmp2"][:], A["cum"][:], AF.Exp, ["cum"], ["tmp2"])
        b.tt(atrt[:, 1, :], s_r[:], A["tmp2"][:], ALU.mult, [kr, "tmp2"], ["atrt"])
        b.act(A["tmp2"][:], A["cum"][:], AF.Exp, ["cum", "atrt"], ["tmp2"], scale=-1.0)
        b.tt(A["bt"][:], A["bb"][:], A["tmp2"][:], ALU.mult, ["bb", "tmp2"], ["bt"])
        b.tt(A["kt"][:], A["kc"][:], A["tmp2"][:], ALU.mult, ["kc", "tmp2"], ["kt"], eng="pool")
        for c in range(NCH):
            cs = slice(c * 128, (c + 1) * 128)
            b.act(A["tmp1"][:, cs], A["cum"][:, cs], AF.Exp, ["cum", "atrt"], ["tmp1"], scale=-1.0, bias=A["cum"][:, c * 128 + 127:c * 128 + 128])
        b.tt(A["bh"][:], A["bb"][:], A["tmp1"][:], ALU.mult, ["bb", "tmp1"], ["bh"])
        b.tt(A["kh"][:], A["kc"][:], A["tmp1"][:], ALU.mult, ["kc", "tmp1"], ["kh"], eng="pool")
        b.act(wl[:], A["cum"][:, 127:SEGC:128], AF.Exp, ["cum"], ["wl"])
        b.stt(A["rkr"][:], s_r[:], pc[:, 15:16], A["kc"][:], ALU.mult, ALU.mult, [kr, "pc", "kc"], ["rkr"])

        ob = obuf[s % 2]; obk = ("obuf", s % 2)
        def chunk_steps(c, p):
            nonlocal stc
            cs = slice(c * 128, (c + 1) * 128)
            t4 = tok4[p]; t4k = ("tok4", p)
            psT, kT = b.psum()
            b.tr(psT[:, 0:64], atrt[:, 0, cs], ident[0:64, 0:64], ["atrt", "c_ident"], [kT])
            b.tr(psT[:, 64:128], s_v[:, cs], ident[0:64, 0:64], [kv_, "c_ident"], [kT])
            b.tr(psT[:, 128:192], A["bh"][:, cs], ident[0:64, 0:64], ["bh", "c_ident"], [kT])
            b.tr(psT[:, 192:256], A["kh"][:, cs], ident[0:64, 0:64], ["kh", "c_ident"], [kT])
            b.tr(psT[:, 256:320], sgc[:, cs], ident[0:64, 0:64], ["sgc", "c_ident"], [kT])
            b.cp(t4[:].rearrange("p a q -> p (a q)"), psT[:, 0:320], [kT], [t4k], eng="act")
            yield
            ab = AB[p]; abk = ("AB", p); ak = AK[p]; akk = ("AK", p)
            ps1, k1 = b.psum()
            b.mm(ps1[:, 0:256], A["bt"][:, cs], atrt[:, :, cs], True, True, ["bt", "atrt"], [k1])
            b.tt(ab[:].rearrange("p a q -> p (a q)"), ps1[:, 0:256], mask2[:].rearrange("p a q -> p (a q)"), ALU.mult, [k1, "mask2"], [abk])
            yield
            ps2, k2 = b.psum()
            b.mm(ps2[:, 0:256], A["kt"][:, cs], atrt[:, :, cs], True, True, ["kt", "atrt"], [k2])
            b.tt(ak[:].rearrange("p a q -> p (a q)"), ps2[:, 0:256], mask2[:].rearrange("p a q -> p (a q)"), ALU.mult, [k2, "mask2"], [akk])
            yield
            ps3, k3 = b.psum()
            b.mm(ps3[:, 0:128], atrt[:, 0, cs], A["bt"][:, cs], True, True, ["bt", "atrt"], [k3])
            b.tt(Mb[p][0][:], ps3[:, 0:128], maskL[:], ALU.mult, [k3, "maskL"], [("Mb", p, 0)])
            b.tt(TTb[p][0][:], ab[:, 0, :], ident[:], ALU.add, [abk, "c_ident"], [("TTb", p, 0)], eng="pool")
            yield
            Mc, Mck = Mb[p][0], ("Mb", p, 0)
            MTc, MTck = ab[:, 0, :], abk
            for lev in range(1, 7):
                Mn, Mnk = Mb[p][lev % 2], ("Mb", p, lev % 2)
                psm, km = b.psum()
                b.mm(psm[:, 0:128], MTc, Mc[:], True, True, [MTck, Mck], [km])
                if lev < 6:
                    MTn, MTnk = MTb[p][lev % 2], ("MTb", p, lev % 2)
                    psn, kn = b.psum()
                    b.mm(psn[:, 0:128], Mc[:], MTc, True, True, [MTck, Mck], [kn])
                b.cp(Mn[:], psm[:, 0:128], [km], [Mnk], eng="act")
                if lev < 6:
                    b.cp(MTn[:], psn[:, 0:128], [kn], [MTnk], eng="dve")
                yield
                pst, kt_ = b.psum()
                b.mm(pst[:, 0:128], Mn[:], TTb[p][(lev - 1) % 2][:], True, True, [Mnk, ("TTb", p, (lev - 1) % 2)], [kt_])
                b.tt(TTb[p][lev % 2][:], TTb[p][(lev - 1) % 2][:], pst[:, 0:128], ALU.add, [("TTb", p, (lev - 1) % 2), kt_], [("TTb", p, lev % 2)])
                yield
                Mc, Mck = Mn, Mnk
                if lev < 6:
                    MTc, MTck = MTn[:], MTnk
            TT, TTk = TTb[p][0], ("TTb", p, 0)
            pq = PQ[p]; pqk = ("PQ", p)
            psP, kP = b.psum()
            b.mm(psP[:, 0:64], TT[:], t4[:, 0, :], True, True, [TTk, t4k], [kP])
            psX, kX = b.psum()
            b.mm(psX[:, 0:64], ak[:, 0, :], t4[:, 1, :], True, True, [akk, t4k], [kX])
            b.cp(Xsb[p][:], psX[:, 0:64], [kX], [("Xsb", p)], eng="act")
            yield
            b.mm(psP[:, 64:128], TT[:], Xsb[p][:], True, True, [TTk, ("Xsb", p)], [kP])
            b.cp(pq[:].rearrange("p a q -> p (a q)"), psP[:, 0:128], [kP], [pqk], eng="dve")
            yield
            psg, kg = b.psum()
            b.mm(psg[0:64, 0:64], pq[:, 0, :], t4[:, 2, :], True, True, [pqk, t4k], [kg])
            b.stt(GT[p][:], ident[0:64, 0:64], wl[:, c:c + 1], psg[0:64, 0:64], ALU.mult, ALU.add, ["c_ident", "wl", kg], [("GT", p)])
            psh, kh_ = b.psum()
            b.mm(psh[0:64, 0:64], t4[:, 2, :], pq[:, 1, :], True, False, [pqk, t4k], [kh_])
            b.mm(psh[0:64, 0:64], t4[:, 3, :], t4[:, 1, :], False, True, [t4k], [kh_])
            b.cp(Hs[p][:], psh[0:64, 0:64], [kh_], [("Hs", p)], eng="act")
            yield
            psr, kr_ = b.psum()
            b.mm(psr[0:64, 0:128], pq[:, 0, :], ab[:, 1, :], True, True, [pqk, abk], [kr_])
            b.tt(R2T[p][:], psr[0:64, 0:128], atrt[:, 1, cs], ALU.add, [kr_, "atrt"], [("R2T", p)])
            psb, kb = b.psum()
            b.mm(psb[:, 0:1], A["rkr"][:, cs], ones[0:64, 0:1], True, True, ["rkr", "c_ones"], [kb])
            b.cp(bsc[p][:], psb[:, 0:1], [kb], [("bsc", p)], eng="act")
            yield
            stcur, stk = ST[stc % 2], ("ST", stc % 2)
            stnew, stnk = ST[(stc + 1) % 2], ("ST", (stc + 1) % 2)
            stc += 1
            psy, ky = b.psum()
            b.mm(psy[:, 0:64], R2T[p][:], stcur[:], True, False, [("R2T", p), stk], [ky])
            b.mm(psy[:, 0:64], ab[:, 1, :], pq[:, 1, :], False, False, [abk, pqk], [ky])
            b.mm(psy[:, 0:64], ak[:, 1, :], t4[:, 1, :], False, True, [akk, t4k], [ky])
            pss, ks = b.psum()
            b.mm(pss[0:64, 0:64], GT[p][:], stcur[:], True, True, [("GT", p), stk], [ks])
            b.tt(stnew[:], pss[0:64, 0:64], Hs[p][:], ALU.add, [ks, ("Hs", p)], [stnk])
            S.op("dve", lambda e, psy=psy: e.bn_stats(out=bst[p][:], in_=psy[:, 0:64]), [ky], [("bst", p)])
            S.op("dve", lambda e: e.bn_aggr(out=mv[p][:, 0:2], in_=bst[p][:]), [("bst", p)], [("mv", p)])
            yield
            b.act(mv[p][:, 2:3], mv[p][:, 1:2], AF.Sqrt, [("mv", p)], [("mv2", p)], bias=GN_EPS)
            S.op("dve", lambda e: e.reciprocal(out=mv[p][:, 2:3], in_=mv[p][:, 2:3]), [("mv2", p)], [("mv2", p)])
            b.stt(mv[p][:, 3:4], mv[p][:, 0:1], -1.0, mv[p][:, 2:3], ALU.mult, ALU.mult, [("mv", p), ("mv2", p)], [("mv3", p)])
            y_ = yn[p]; ynk = ("yn", p)
            b.act(y_[:], psy[:, 0:64], AF.Identity, [ky, ("mv2", p), ("mv3", p)], [ynk], bias=mv[p][:, 3:4], scale=mv[p][:, 2:3])
            yield
            b.tt(y_[:], y_[:], gnb[:, 0:64], ALU.mult, [ynk, "gnb"], [ynk])
            b.tt(y_[:], y_[:], gnb[:, 64:128], ALU.add, [ynk, "gnb"], [ynk], eng="pool")
            b.stt(y_[:], t4[:, 1, :], bsc[p][:, 0:1], y_[:], ALU.mult, ALU.add, [t4k, ("bsc", p), ynk], [ynk])
            b.tt(y_[:], y_[:], t4[:, 4, :], ALU.mult, [ynk, t4k], [ynk], eng="pool")
            pso, ko = b.psum()
            b.tr(pso[0:64, 0:128], y_[:], ident[:], [ynk, "c_ident"], [ko])
            b.cp(ob[:, cs], pso[0:64, 0:128], [ko], [obk], eng="act")
            yield

        import itertools
        for c0 in range(0, NCH, NW):
            gens = [chunk_steps(c0 + q, q) for q in range(min(NW, NCH - c0))]
            for _ in itertools.zip_longest(*gens):
                pass
        S.dma("sp", yg_d[:, s * SEGC:(s + 1) * SEGC], ob[:], reads=[obk], writes=[("outC", s)])
        outk.append(("outC", s))
    return outk


def p1c_dram(b, T=SEQ):
    return (b.din("wc", [D_MODEL, 384]), b.din("pc", [64, 32]), b.din("wup", [64, 128]), b.din("gnb", [128, 192]), b.dout("ygC", [64, T]))


def build_p1c(T=SEQ):
    b = B()
    ones, ident = b.ident()
    st = contextlib.ExitStack()
    keys = emit_p1c(b, st, ones, ident, b.din("xT", [D_MODEL, T]), *p1c_dram(b, T), T=T)
    return b.finish(keys, extra=[st]), b.outs


def host_p1c(inp, l, xT):
    w_in = inp["w_in"][l]; b_in = inp["b_in"][l]
    C0 = 1024 + 5120
    mu = inp["rwkv_mu"][l]
    maps = []
    for c in range(8):
        hs = slice(64 * c, 64 * c + 64)
        secs = [(C0, hs), (C0 + 512, hs), (C0 + 1024, hs), (C0 + 1536, slice(0, 64)), (C0 + 1600, slice(0, 64)), (C0 + 1664, hs)]
        wc = np.concatenate([w_in[:, o:o + 512][:, sl_] if sl_ is hs else w_in[:, o:o + 64] for o, sl_ in secs], axis=1)
        pc = np.zeros((64, 32), np.float32)
        pc[:, 0] = b_in[C0:C0 + 512][hs]; pc[:, 1] = b_in[C0 + 512:C0 + 1024][hs]; pc[:, 2] = b_in[C0 + 1024:C0 + 1536][hs]
        pc[:, 3] = b_in[C0 + 1536:C0 + 1600]; pc[:, 4] = b_in[C0 + 1600:C0 + 1664]
        pc[:, 5] = mu[0:512][hs]; pc[:, 6] = mu[512:1024][hs]; pc[:, 7] = mu[1024:1536][hs]
        pc[:, 8] = mu[1536:1600]; pc[:, 9] = mu[1600:1664]
        pc[:, 10] = inp["rwkv_w0"][l][hs]; pc[:, 11] = inp["rwkv_a0"][l][hs]
        pc[:, 12] = inp["rwkv_k_k"][l][hs]; pc[:, 13] = inp["rwkv_k_a"][l][hs]
        pc[:, 15] = inp["rwkv_r_k"][l][c]
        pc[:, 16] = b_in[C0 + 1664:C0 + 2176][hs]
        wup = np.concatenate([inp["rwkv_w_up"][l][:, hs], inp["rwkv_a_up"][l][:, hs]], axis=1)
        gn = np.concatenate([inp["rwkv_gn_g"][l][hs], inp["rwkv_gn_b"][l][hs], b_in[C0 + 1664:C0 + 2176][hs]])
        gnb = np.ascontiguousarray(np.broadcast_to(gn[None, :], (128, 192)))
        maps.append({"xT": xT, "wc": np.ascontiguousarray(wc), "pc": pc, "wup": np.ascontiguousarray(wup), "gnb": gnb})
    return maps


def build_p1():
    b = B()
    S = b.S
    ones, ident = b.ident()
    xT = b.din("xT", [D_MODEL, SEQ])
    a_dram = (b.din("wa", [D_MODEL, 128]), b.din("pa", [64, 16]), b.din("gw", [64, 128]), b.dout("ygA", [64, SEQ]))
    b_dram = p1b_dram(b)
    c_dram = p1c_dram(b)
    keys = []
    st = contextlib.ExitStack()
    keys += emit_p1a(b, st, xT, *a_dram)
    S.barrier(); st.close()
    st = contextlib.ExitStack()
    keys += emit_p1b(b, st, ones, ident, *b_dram)
    S.barrier(); st.close()
    st = contextlib.ExitStack()
    keys += emit_p1c(b, st, ones, ident, xT, *c_dram)
    return b.finish(keys, extra=[st]), b.outs


def host_p1(inp, l, xT):
    ma, mb, mc = host_p1a(inp, l, xT), host_p1b(inp, l, xT), host_p1c(inp, l, xT)
    maps = []
    for c in range(8):
        m = dict(ma[c]); m.update(mb[c]); m.update(mc[c])
        maps.append(m)
    return maps


ALPHA = (2.0 * 2) ** 0.25
LN_EPS = 1e-5
NT = 1024
HALO = 32


def build_p2():
    b = B()
    S = b.S
    xTh = b.din("xTh", [D_MODEL, NT + HALO])
    xown = b.din("xown", [NT, D_MODEL])
    ygT = b.din("ygT", [1536, NT])
    wm_d = b.din("wm", [D_MODEL, 8192])
    wd_d = b.din("wd", [D_MODEL, 1536])
    wbr_d = b.din("wbr", [2048, D_MODEL])
    wout_d = b.din("wout", [D_MODEL, D_MODEL])
    pm_d = b.din("pm", [128, 64])
    pd_d = b.din("pd", [128, 64])
    dww_d = b.din("dww", [128, 4 * 31])
    lng_d = b.din("lng", [128, D_MODEL])
    lnb_d = b.din("lnb", [128, D_MODEL])
    xo_d = b.dout("xo", [NT, D_MODEL])

    ones, ident = b.ident()
    pm = b.sb("pm", [128, 64]); pd = b.sb("pd", [128, 64]); dww = b.sb("dww", [128, 4, 31])
    b.load(pm[:], pm_d, "pm"); b.load(pd[:], pd_d, "pd")
    b.load(dww[:], dww_d.rearrange("p (c j) -> p c j", j=31), "dww")
    mixT = b.sb("mixT", [128, KCH, NT], BF16)

    stDM = contextlib.ExitStack()
    xb = b.sb("xb", [128, KCH, NT + HALO], BF16, st=stDM)
    ygd = b.sb("ygd", [128, 4, NT], BF16, st=stDM)
    S.dma("pool", xb[:], xT_tile_ap(xTh, 0, NT + HALO), writes=["xb"])

    stD = contextlib.ExitStack()
    wdb = [b.sb("wdb%d" % i, [128, KCH, 3, 128], BF16, st=stD) for i in range(2)]
    cu = b.sb("cu", [128, NT + HALO], F32, st=stD)
    t1 = b.sb("t1", [128, NT + HALO], F32, st=stD)
    t2 = b.sb("t2", [128, NT + HALO], F32, st=stD)
    cv = b.sb("cv", [128, 4, NT], F32, st=stD)
    dg = b.sb("dg", [128, 4, NT], F32, st=stD)
    sq = b.sb("sq", [128, NT], F32, st=stD)
    mean = b.sb("mean", [128, NT], F32, st=stD)
    rstd = b.sb("rstd", [128, NT], F32, st=stD)
    tiles = [(0, HALO), (HALO, 512), (HALO + 512, 512)]
    for cc in range(4):
        w = wdb[cc % 2]; wk = ("wdb", cc % 2)
        for j in range(3):
            S.dma("pool", w[:, :, j, :], w_ap(wd_d[:, j * 512 + cc * 128: j * 512 + cc * 128 + 128]), writes=[wk])
        groups = [[tiles[0]], [tiles[1], tiles[2]]]
        for grp in groups:
            for sec in range(3):
                if sec == 2 and grp[0][0] < HALO:
                    continue
                pss = [b.psum() for _ in grp]
                for k in range(KCH):
                    for (ps_, pk_), (t0, n) in zip(pss, grp):
                        b.mm(ps_[:, :n], w[:, k, sec, :], xb[:, k, t0:t0 + n], k == 0, k == KCH - 1, [wk, "xb"], [pk_])
                for (ps_, pk_), (t0, n) in zip(pss, grp):
                    if sec == 0:
                        b.act(t1[:, t0:t0 + n], ps_[:, :n], AF.Identity, [pk_, "pd"], [("t1", t0)], bias=pd[:, cc:cc + 1])
                    elif sec == 1:
                        b.act(t2[:, t0:t0 + n], ps_[:, :n], AF.Sigmoid, [pk_, "pd"], [("t2", t0)], bias=pd[:, 4 + cc:5 + cc])
                        b.tt(cu[:, t0:t0 + n], t1[:, t0:t0 + n], t2[:, t0:t0 + n], ALU.mult, [("t1", t0), ("t2", t0)], ["cu"])
                    else:
                        b.act(dg[:, cc, t0 - HALO:t0 - HALO + n], ps_[:, :n], AF.Silu, [pk_, "pd"], [("dg", cc)], bias=pd[:, 8 + cc:9 + cc])
        b.ts(cu[:, 0:HALO], cu[:, 0:HALO], pd[:, 24:25], None, ALU.mult, None, ["cu", "pd"], ["cu"])
        ck = ("cv", cc)
        b.ts(cv[:, cc, :], cu[:, 2:2 + NT], dww[:, cc, 0:1], pd[:, 12 + cc:13 + cc], ALU.mult, ALU.add, ["cu", "dww", "pd"], [ck])
        for j in range(1, 31):
            b.stt(cv[:, cc, :], cu[:, 2 + j:2 + j + NT], dww[:, cc, j:j + 1], cv[:, cc, :], ALU.mult, ALU.add, ["cu", "dww", ck], [ck])
    for tt_ in range(2):
        sl = slice(tt_ * 512, (tt_ + 1) * 512)
        ps1, k1 = b.psum()
        for cc in range(4):
            b.mm(ps1[:, :], ones[:, 0:128], cv[:, cc, sl], cc == 0, cc == 3, ["c_ones", ("cv", cc)], [k1])
        b.act(mean[:, sl], ps1[:, :], AF.Copy, [k1], ["mean"], scale=1.0 / 512)
        ps2, k2 = b.psum()
        for cc in range(4):
            b.act(sq[:, sl], cv[:, cc, sl], AF.Square, [("cv", cc)], ["sq"])
            b.mm(ps2[:, :], ones[:, 0:128], sq[:, sl], cc == 0, cc == 3, ["c_ones", "sq"], [k2])
        b.act(rstd[:, sl], ps2[:, :], AF.Copy, [k2], ["rstd"], scale=1.0 / 512)
    b.tt(sq[:], mean[:], mean[:], ALU.mult, ["mean"], ["sq"])
    b.tt(rstd[:], rstd[:], sq[:], ALU.subtract, ["rstd", "sq"], ["rstd"])
    b.act(rstd[:], rstd[:], AF.Sqrt, ["rstd"], ["rstd"], bias=LN_EPS)
    S.op("dve", lambda e: e.reciprocal(out=rstd[:], in_=rstd[:]), ["rstd"], ["rstd"])
    for cc in range(4):
        b.tt(sq[:], cv[:, cc, :], mean[:], ALU.subtract, [("cv", cc), "mean"], ["sq"])
        b.tt(sq[:], sq[:], rstd[:], ALU.mult, ["sq", "rstd"], ["sq"], eng="pool")
        b.act(sq[:], sq[:], AF.Silu, ["sq", "pd"], ["sq"], bias=pd[:, 20 + cc:21 + cc], scale=pd[:, 16 + cc:17 + cc])
        b.tt(ygd[:, cc, :], sq[:], dg[:, cc, :], ALU.mult, ["sq", ("dg", cc)], ["ygd"])
    S.barrier()
    stD.close()

    stM = contextlib.ExitStack()
    ygb = b.sb("ygb", [128, 12, NT], BF16, st=stM)
    S.dma("pool", ygb[:], ygT.rearrange("(c p) t -> p c t", p=128), writes=["ygb"])
    wmb = [b.sb("wmb%d" % i, [128, KCH, 512], BF16, st=stM) for i in range(3)]
    wbb = [b.sb("wbb%d" % i, [128, 4, 512], BF16, st=stM) for i in range(3)]
    macc = b.sb("macc", [128, 4, NT], F32, st=stM)
    mg = [b.sb("mg%d" % i, [128, 512], F32, st=stM) for i in range(2)]
    tmp = [b.sb("tmp%d" % i, [128, 512], F32, st=stM) for i in range(2)]
    it = 0
    for dcg in range(4):
        for n in range(4):
            wi = (dcg * 4 + n) % 3
            wm_, wmk = wmb[wi], ("wmb", wi)
            wb_, wbk = wbb[wi], ("wbb", wi)
            c0 = n * 2048 + dcg * 512
            S.dma("pool", wm_[:], w_ap(wm_d[:, c0:c0 + 512]), writes=[wmk])
            S.dma("pool", wb_[:], wbr_d[n * 512:(n + 1) * 512, dcg * 512:(dcg + 1) * 512].rearrange("(c p) d -> p c d", p=128), writes=[wbk])
            for j in range(4):
                psm2 = [b.psum() for _ in range(2)]
                for k in range(KCH):
                    for tt_ in range(2):
                        b.mm(psm2[tt_][0][:, :], wm_[:, k, j * 128:(j + 1) * 128], xb[:, k, HALO + tt_ * 512:HALO + (tt_ + 1) * 512],
                             k == 0, k == KCH - 1, [wmk, "xb"], [psm2[tt_][1]])
                psb2 = [b.psum() for _ in range(2)]
                for c4 in range(4):
                    for tt_ in range(2):
                        sl = slice(tt_ * 512, (tt_ + 1) * 512)
                        rhs = ygb[:, n * 4 + c4, sl] if n < 3 else ygd[:, c4, sl]
                        b.mm(psb2[tt_][0][:, :], wb_[:, c4, j * 128:(j + 1) * 128], rhs, c4 == 0, c4 == 3, [wbk, "ygb", "ygd"], [psb2[tt_][1]])
                for tt_ in range(2):
                    sl = slice(tt_ * 512, (tt_ + 1) * 512)
                    psm, km = psm2[tt_]
                    psb, kb = psb2[tt_]
                    m_ = mg[it % 2]; mk = ("mg", it % 2)
                    col = n * 16 + dcg * 4 + j
                    b.act(m_[:], psm[:, :], AF.Sigmoid, [km, "pm"], [mk], bias=pm[:, col:col + 1])
                    ak = ("macc", j, tt_)
                    if n == 0:
                        b.tt(macc[:, j, sl], m_[:], psb[:, :], ALU.mult, [mk, kb], [ak])
                    else:
                        t_ = tmp[it % 2]; tk = ("tmp", it % 2)
                        b.tt(t_[:], m_[:], psb[:, :], ALU.mult, [mk, kb], [tk])
                        b.tt(macc[:, j, sl], macc[:, j, sl], t_[:], ALU.add, [ak, tk], [ak])
                    it += 1
                    if n == 3:
                        b.cp(mixT[:, dcg * 4 + j, sl], macc[:, j, sl], [ak], ["mixT"], eng="act")
    S.barrier()
    stM.close()
    stDM.close()

    stO = contextlib.ExitStack()
    wob = b.sb("wob", [128, KCH, D_MODEL], BF16, st=stO)
    for eg in range(4):
        S.dma("pool", wob[:, :, eg * 512:(eg + 1) * 512], w_ap(wout_d[:, eg * 512:(eg + 1) * 512]), writes=[("wob", eg)])
    lng = b.sb("lng", [128, D_MODEL], F32, st=stO); lnb = b.sb("lnb", [128, D_MODEL], F32, st=stO)
    b.load(lng[:], lng_d, "lng"); b.load(lnb[:], lnb_d, "lnb")
    xt = [b.sb("xt%d" % i, [128, D_MODEL], F32, st=stO) for i in range(2)]
    z = [b.sb("z%d" % i, [128, D_MODEL], F32, st=stO) for i in range(2)]
    bst = b.sb("bst", [128, 4, 6], F32, st=stO)
    mv = b.sb("mv", [128, 4], F32, st=stO)
    outk = []
    for tt_ in range(NT // 128):
        x_ = xt[tt_ % 2]; xk = ("xt", tt_ % 2)
        z_ = z[tt_ % 2]; zk = ("z", tt_ % 2)
        b.load(x_[:], xown[tt_ * 128:(tt_ + 1) * 128, :], xk)
        pso = [b.psum() for _ in range(4)]
        for dc in range(KCH):
            for eg in range(4):
                b.mm(pso[eg][0][:, :], mixT[:, dc, tt_ * 128:(tt_ + 1) * 128], wob[:, dc, eg * 512:(eg + 1) * 512], dc == 0, dc == KCH - 1,
                     ["mixT", ("wob", eg)], [pso[eg][1]])
        for eg in range(4):
            es = slice(eg * 512, (eg + 1) * 512)
            ps, pk = pso[eg]
            b.stt(z_[:, es], x_[:, es], ALPHA, ps[:, :], ALU.mult, ALU.add, [xk, pk], [(zk, eg)])
            S.op("dve", lambda e, eg=eg, z_=z_, es=es: e.bn_stats(out=bst[:, eg, :], in_=z_[:, es]), [(zk, eg)], [("bst", eg)])
        S.op("dve", lambda e: e.bn_aggr(out=mv[:, 0:2], in_=bst[:].rearrange("p a b -> p (a b)")), [("bst", eg) for eg in range(4)], ["mv"])
        b.act(mv[:, 2:3], mv[:, 1:2], AF.Sqrt, ["mv"], ["mv2"], bias=LN_EPS)
        S.op("dve", lambda e: e.reciprocal(out=mv[:, 2:3], in_=mv[:, 2:3]), ["mv2"], ["mv2"])
        b.stt(mv[:, 3:4], mv[:, 0:1], -1.0, mv[:, 2:3], ALU.mult, ALU.mult, ["mv", "mv2"], ["mv3"])
        b.act(z_[:], z_[:], AF.Identity, [(zk, eg) for eg in range(4)] + ["mv2", "mv3"], [zk], bias=mv[:, 3:4], scale=mv[:, 2:3])
        b.tt(z_[:], z_[:], lng[:], ALU.mult, [zk, "lng"], [zk])
        b.tt(z_[:], z_[:], lnb[:], ALU.add, [zk, "lnb"], [zk] + [(zk, eg) for eg in range(4)], eng="pool")
        S.dma("sp", xo_d[tt_ * 128:(tt_ + 1) * 128, :], z_[:], reads=[zk], writes=[zk, ("xo", tt_)] + [(zk, eg) for eg in range(4)])
        outk.append(("xo", tt_))
    nc = b.finish(outk, extra=[stO])
    return nc, b.outs


def host_p2(inp, l, x_tok, ygT_full):
    w_in = inp["w_in"][l]; b_in = inp["b_in"][l]
    D0 = 1024 + 5120 + 2176
    M0 = D0 + 1536
    wm = np.ascontiguousarray(w_in[:, M0:M0 + 8192])
    wd = np.ascontiguousarray(w_in[:, D0:D0 + 1536])
    wbr = np.ascontiguousarray(inp["w_br"][l].reshape(2048, 2048))
    wout = np.ascontiguousarray(inp["w_out"][l])
    pm = np.ascontiguousarray(b_in[M0:M0 + 8192].reshape(64, 128).T)
    lng = np.ascontiguousarray(np.broadcast_to(inp["ln_g"][l][None, :], (128, 2048)))
    lnb = np.ascontiguousarray(np.broadcast_to(inp["ln_b"][l][None, :], (128, 2048)))
    dww = np.ascontiguousarray(inp["conf_dw_w"][l].T.reshape(4, 128, 31).transpose(1, 0, 2).reshape(128, 124))
    xT = x_tok.T
    maps = []
    for c in range(8):
        pd = np.zeros((128, 64), np.float32)
        bd = b_in[D0:D0 + 1536]
        for cc in range(4):
            pd[:, cc] = bd[cc * 128:(cc + 1) * 128]
            pd[:, 4 + cc] = bd[512 + cc * 128:512 + (cc + 1) * 128]
            pd[:, 8 + cc] = bd[1024 + cc * 128:1024 + (cc + 1) * 128]
            pd[:, 12 + cc] = inp["conf_dw_b"][l][cc * 128:(cc + 1) * 128]
            pd[:, 16 + cc] = inp["conf_ln_g"][l][cc * 128:(cc + 1) * 128]
            pd[:, 20 + cc] = inp["conf_ln_b"][l][cc * 128:(cc + 1) * 128]
        pd[:, 24] = 0.0 if c == 0 else 1.0
        t0 = c * NT
        xTh = np.zeros((2048, NT + HALO), np.float32)
        if c == 0:
            xTh[:, HALO:] = xT[:, 0:NT]
        else:
            xTh[:] = xT[:, t0 - HALO:t0 + NT]
        maps.append({"xTh": xTh, "xown": np.ascontiguousarray(x_tok[t0:t0 + NT]), "ygT": np.ascontiguousarray(ygT_full[:, t0:t0 + NT]),
                     "wm": wm, "wd": wd, "wbr": wbr, "wout": wout, "pm": pm, "pd": pd, "dww": dww, "lng": lng, "lnb": lnb})
    return maps


CORES = list(range(8))


def kernel(**inputs):
    inp = {k: np.asarray(v) for k, v in inputs.items()}
    x = np.ascontiguousarray(inp["x"][0], dtype=np.float32)
    for l in range(2):
        xT = np.ascontiguousarray(x.T)
        nc, _ = build_p1()
        r1 = run_bass_kernel_spmd(nc, host_p1(inp, l, xT), core_ids=CORES)
        ygA = np.concatenate([r["ygA"] for r in r1.results], axis=0)
        ygB = gather_p1b(r1.results)
        ygC = np.concatenate([r["ygC"] for r in r1.results], axis=0)
        ygT = np.ascontiguousarray(np.concatenate([ygA, ygB, ygC], axis=0))
        nc, _ = build_p2()
        r2 = run_bass_kernel_spmd(nc, host_p2(inp, l, x, ygT), core_ids=CORES)
        x = np.ascontiguousarray(np.concatenate([r["xo"] for r in r2.results], axis=0))
    return x[None].astype(np.float32)
```

```python
import contextlib
import numpy as np
import concourse.bass as bass
import concourse.mybir as mybir

F32 = mybir.dt.float32
BF16 = mybir.dt.bfloat16
AF = mybir.ActivationFunctionType
ALU = mybir.AluOpType
AX = mybir.AxisListType


class Sched:
    ENGS = ("pe", "dve", "act", "pool", "sp")

    def __init__(self, nc, stack, n_lanes=24, n_sw_lanes=16):
        self.nc = nc
        self.h = {"pe": nc.tensor, "dve": nc.vector, "act": nc.scalar,
                  "pool": nc.gpsimd, "sp": nc.sync}
        self.sem = {e: stack.enter_context(nc.semaphore("s_" + e)) for e in self.ENGS}
        self.cnt = {e: 0 for e in self.ENGS}
        self.n_hw = n_lanes
        self.lanes = [stack.enter_context(nc.semaphore("l%d" % i)) for i in range(n_lanes)]
        self.lanes += [stack.enter_context(nc.semaphore("w%d" % i)) for i in range(n_sw_lanes)]
        self.lane_tot = [0] * (n_lanes + n_sw_lanes)
        self.lane_next = 0
        self.sw_next = 0
        self.prog = {e: [] for e in self.ENGS}
        self.seen = {e: {} for e in self.ENGS}
        self.lastw = {}
        self.readers = {}
        self.semobj = {}
        self.unread = set()

    def _need(self, e, tok, waits):
        if tok is None:
            return
        k, v = tok
        if e == "pe" and k == ("e", "pe"):
            return
        if self.seen[e].get(k, 0) >= v:
            return
        self.seen[e][k] = v
        waits.append((k, v))

    def _semof(self, k):
        return self.sem[k[1]] if k[0] == "e" else self.lanes[k[1]]

    def _deps(self, e, reads, writes):
        waits = []
        for r in reads:
            self._need(e, self.lastw.get(r), waits)
        for w in writes:
            self._need(e, self.lastw.get(w), waits)
            for t in self.readers.get(w, ()):
                self._need(e, t, waits)
        return waits

    def _commit(self, tok, reads, writes):
        for r in reads:
            self.unread.discard(r)
            self.readers.setdefault(r, []).append(tok)
        for w in writes:
            self.lastw[w] = tok
            self.readers[w] = []

    def op(self, e, fn, reads=(), writes=()):
        waits = self._deps(e, reads, writes)
        self.cnt[e] += 1
        tok = (("e", e), self.cnt[e])
        self.prog[e].append((waits, fn, (self.sem[e], 1)))
        self._commit(tok, reads, writes)
        return tok

    def take_lane(self, q):
        if q == "pool":
            lane = self.n_hw + self.sw_next
            self.sw_next = (self.sw_next + 1) % (len(self.lanes) - self.n_hw)
        else:
            lane = self.lane_next
            self.lane_next = (self.lane_next + 1) % self.n_hw
        return lane

    def dma(self, q, out, in_, reads=(), writes=(), **kw):
        lane = self.take_lane(q)
        waits = self._deps(q, reads, writes)
        self._need(q, (("l", lane), self.lane_tot[lane]) if self.lane_tot[lane] else None, waits)
        self.lane_tot[lane] += 16
        tok = (("l", lane), self.lane_tot[lane])

        def fn(eng, out=out, in_=in_, kw=kw):
            return eng.dma_start(out=out, in_=in_, **kw)
        self.prog[q].append((waits, fn, (self.lanes[lane], 16)))
        self._commit(tok, reads, writes)
        return tok

    def wait_all(self, e, toks):
        waits = []
        for t in toks:
            self._need(e, t, waits)
        self.prog[e].append((waits, None, None))

    def barrier(self):
        toks = [(("e", e), self.cnt[e]) for e in self.ENGS if self.cnt[e]]
        toks += [(("l", i), t) for i, t in enumerate(self.lane_tot) if t]
        for e in self.ENGS:
            self.wait_all(e, toks)

    def emit(self, block):
        def mk(e):
            def body(eng):
                for waits, fn, inc in self.prog[e]:
                    for k, v in waits:
                        eng.wait_ge(self._semof(k), v)
                    if fn is not None:
                        ins = fn(eng)
                        ins.then_inc(inc[0], inc[1])
            return body
        block.tensor(mk("pe"))
        block.vector(mk("dve"))
        block.scalar(mk("act"))
        block.gpsimd(mk("pool"))
        block.sync(mk("sp"))


import contextlib
import numpy as np
import concourse.bass as bass
import concourse.mybir as mybir
from concourse.bass_utils import run_bass_kernel_spmd

F32 = mybir.dt.float32
BF16 = mybir.dt.bfloat16
AF = mybir.ActivationFunctionType
ALU = mybir.AluOpType

D_MODEL = 2048
SEQ = 8192
KCH = 16


class B:
    def __init__(self):
        self.nc = bass.Bass("TRN2", target_bir_lowering=False, num_devices=8)
        self.st = contextlib.ExitStack()
        self.S = Sched(self.nc, self.st)
        self.ps = [self.st.enter_context(self.nc.psum_tensor("ps%d" % i, [128, 512], F32)) for i in range(8)]
        self.ps_i = 0
        self.uid = 0
        self.outs = []
        self.pfx = ""

    def din(self, name, shape, dt=F32):
        return self.nc.dram_tensor(name, list(shape), dt, kind="ExternalInput").ap()

    def dout(self, name, shape, dt=F32):
        self.outs.append(name)
        return self.nc.dram_tensor(name, list(shape), dt, kind="ExternalOutput").ap()

    def sb(self, name, shape, dt=F32, st=None):
        return (st or self.st).enter_context(self.nc.sbuf_tensor("sb_" + self.pfx + name, list(shape), dt))

    def psum(self):
        i = self.ps_i
        self.ps_i = (i + 1) % 8
        if ("ps", i) in self.S.unread:
            raise RuntimeError("PSUM bank %d handed out again before its previous contents were read" % i)
        self.S.unread.add(("ps", i))
        return self.ps[i], ("ps", i)

    def key(self, p="t"):
        self.uid += 1
        return (p, self.uid)

    def finish(self, out_keys, extra=()):
        S = self.S
        S.wait_all("sp", [S.lastw[k] for k in out_keys])
        with self.nc.Block() as block:
            S.emit(block)
        for e in extra:
            e.close()
        self.st.close()
        return self.nc

    def mm(self, out, lhsT, rhs, start, stop, reads, writes):
        return self.S.op("pe", lambda e: e.matmul(out, lhsT=lhsT, rhs=rhs, start=start, stop=stop), reads, writes)

    def tr(self, out, in_, ident, reads, writes):
        return self.S.op("pe", lambda e: e.transpose(out=out, in_=in_, identity=ident), reads, writes)

    def act(self, out, in_, func, reads, writes, bias=None, scale=None, eng="act"):
        kw = {}
        if bias is not None:
            kw["bias"] = bias
        if scale is not None:
            kw["scale"] = scale
        return self.S.op("act", lambda e: e.activation(out=out, in_=in_, func=func, **kw), reads, writes)

    def tt(self, out, in0, in1, op, reads, writes, eng="dve"):
        return self.S.op(eng, lambda e: e.tensor_tensor(out=out, in0=in0, in1=in1, op=op), reads, writes)

    def ts(self, out, in0, s1, s2, op0, op1, reads, writes, eng="dve"):
        if op1 is None:
            return self.S.op(eng, lambda e: e.tensor_scalar(out=out, in0=in0, scalar1=s1, scalar2=None, op0=op0), reads, writes)
        return self.S.op(eng, lambda e: e.tensor_scalar(out=out, in0=in0, scalar1=s1, scalar2=s2, op0=op0, op1=op1), reads, writes)

    def stt(self, out, in0, scalar, in1, op0, op1, reads, writes):
        return self.S.op("dve", lambda e: e.scalar_tensor_tensor(out=out, in0=in0, scalar=scalar, in1=in1, op0=op0, op1=op1), reads, writes)

    def cp(self, out, in_, reads, writes, eng="dve"):
        if eng == "act":
            return self.S.op("act", lambda e: e.copy(out=out, in_=in_), reads, writes)
        return self.S.op(eng, lambda e: e.tensor_copy(out=out, in_=in_), reads, writes)

    def memset(self, ap, val, writes, eng="pool"):
        return self.S.op(eng, lambda e: e.memset(ap, val), (), writes)

    def load(self, sb_ap, dram_ap, key, q="sp", **kw):
        return self.S.dma(q, sb_ap, dram_ap, writes=[key], **kw)

    def _lane_op(self, q, fn, reads, writes):
        S = self.S
        lane = S.take_lane(q)
        waits = S._deps(q, reads, writes)
        S._need(q, (("l", lane), S.lane_tot[lane]) if S.lane_tot[lane] else None, waits)
        S.lane_tot[lane] += 16
        tok = (("l", lane), S.lane_tot[lane])
        S.prog[q].append((waits, fn, (S.lanes[lane], 16)))
        S._commit(tok, reads, writes)
        return tok

    def coll(self, kind, src, dst, reads, writes):
        return self._lane_op("pool", lambda e: e.collective_compute(kind, ALU.bypass, replica_groups=[list(range(8))],
                                                                    ins=[src], outs=[dst]), reads, writes)

    def gather(self, out, in_, idx, reads, writes):
        return self._lane_op("pool", lambda e: e.indirect_dma_start(out=out, out_offset=None, in_=in_,
                                                                    in_offset=bass.IndirectOffsetOnAxis(ap=idx, axis=0)), reads, writes)

    def ident(self, n=128):
        ones = self.sb("c_ones", [128, 512], F32)
        ident = self.sb("c_ident", [128, 128], F32)
        self.memset(ones[:], 1.0, ["c_ones"])
        self.S.op("pool", lambda e: e.affine_select(out=ident[:], in_=ones[:, 0:128], pattern=[[1, 128]],
                                                    compare_op=ALU.is_equal, fill=0.0, base=0, channel_multiplier=-1),
                  ["c_ones"], ["c_ident"])
        return ones, ident


def xT_tile_ap(xT, t0, n):
    return xT[:, t0:t0 + n].rearrange("(k p) t -> p k t", p=128)


def x_tile(xt, i):
    return xt[i].rearrange("p (k t) -> p k t", k=KCH)


def tile_xT(xT):
    n = xT.shape[1] // 512
    return np.ascontiguousarray(xT.reshape(KCH, 128, n, 512).transpose(2, 1, 0, 3).reshape(n, 128, KCH * 512))


def w_ap(w):
    return w.rearrange("(k p) c -> p k c", p=128)


def emit_p1a(b, st, xT, wa_d, pa_d, gw_d, yg_d, T=SEQ, SEG=512):
    S = b.S
    b.pfx = "a_"
    nseg = T // SEG

    wa = b.sb("wa", [128, KCH, 128], BF16, st=st)
    pa = b.sb("pa", [64, 16], st=st)
    gw = b.sb("gw", [64, 128], st=st)
    c1 = b.sb("c1", [64, 4], st=st)
    SUP = 4
    xb = [b.sb("xb%d" % i, [128, KCH, SUP * SEG], BF16, st=st) for i in range(2)]
    sgs = b.sb("sgs", [64, SUP * SEG], st=st)
    axb = b.sb("axb", [64, 4 * SEG + 3], st=st)
    u = b.sb("u", [64, SEG], st=st)
    gr = b.sb("gr", [64, SEG], st=st)
    gi = b.sb("gi", [64, SEG], st=st)
    at = b.sb("at", [64, SEG], st=st)
    a2 = b.sb("a2", [64, SEG], st=st)
    bt = b.sb("bt", [64, SEG], st=st)
    hb = [b.sb("hb%d" % i, [64, SEG], st=st) for i in range(2)]
    sg = b.sb("sg", [64, SEG], st=st)
    yo = [b.sb("yo%d" % i, [64, SEG], st=st) for i in range(2)]

    S.dma("pool", wa[:], w_ap(wa_d), writes=["wa"])
    b.load(pa[:], pa_d, "pa")
    b.load(gw[:], gw_d, "gw")
    b.memset(axb[:, 0:3], 0.0, ["axb"])
    b.act(c1[:, 2:3], pa[:, 9:10], AF.Exp, ["pa"], ["c1"], scale=-1.0)
    b.act(c1[:, 3:4], c1[:, 2:3], AF.Ln, ["c1"], ["c1"], bias=1.0)
    b.ts(c1[:, 0:1], c1[:, 3:4], -8.0, None, ALU.mult, None, ["c1"], ["c1"])
    b.ts(c1[:, 1:2], c1[:, 3:4], -16.0, None, ALU.mult, None, ["c1"], ["c1"])

    for s in range(nseg):
        sup, sub = s // SUP, s % SUP
        xs = xb[sup % 2]
        xk = ("xb", sup % 2)
        if sub == 0:
            nsub = min(SUP, nseg - s)
            for q in range(nsub):
                S.dma("pool", xs[:, :, q * SEG:(q + 1) * SEG], x_tile(xT, s + q), writes=[(xk, q)])
            pa_ps = [b.psum() for _ in range(nsub)]
            for k in range(KCH):
                for q in range(nsub):
                    b.mm(pa_ps[q][0][0:64, :SEG], wa[:, k, 0:64], xs[:, k, q * SEG:(q + 1) * SEG], k == 0, k == KCH - 1, ["wa", (xk, q)], [pa_ps[q][1]])
            for q in range(nsub):
                b.act(axb[:, 3 + q * SEG:3 + (q + 1) * SEG], pa_ps[q][0][0:64, :SEG], AF.Identity, [pa_ps[q][1], "pa"], ["axb"], bias=pa[:, 0:1])
            pg_ps = [b.psum() for _ in range(nsub)]
            for k in range(KCH):
                for q in range(nsub):
                    b.mm(pg_ps[q][0][0:64, :SEG], wa[:, k, 64:128], xs[:, k, q * SEG:(q + 1) * SEG], k == 0, k == KCH - 1, ["wa", (xk, q)], [pg_ps[q][1]])
            for q in range(nsub):
                b.act(sgs[:, q * SEG:(q + 1) * SEG], pg_ps[q][0][0:64, :SEG], AF.Silu, [pg_ps[q][1], "pa"], [("sgs", q)], bias=pa[:, 1:2])
        o0 = sub * SEG
        b.ts(u[:], axb[:, o0:o0 + SEG], pa[:, 3:4], pa[:, 2:3], ALU.mult, ALU.add, ["axb", "pa"], ["u"])
        for j in range(1, 4):
            b.stt(u[:], axb[:, o0 + j:o0 + j + SEG], pa[:, 3 + j:4 + j], u[:], ALU.mult, ALU.add, ["axb", "pa", "u"], ["u"])
        if sub == SUP - 1 or s == nseg - 1:
            b.cp(axb[:, 0:3], axb[:, (sub + 1) * SEG:(sub + 1) * SEG + 3], ["axb", "u"], ["axb"], eng="pool")
        ps3, pk3 = b.psum()
        b.mm(ps3[0:64, :SEG], gw[:, 0:64], u[:], True, True, ["gw", "u"], [pk3])
        b.act(gr[:], ps3[0:64, :SEG], AF.Sigmoid, [pk3, "pa"], ["gr"], bias=pa[:, 7:8])
        ps4, pk4 = b.psum()
        b.mm(ps4[0:64, :SEG], gw[:, 64:128], u[:], True, True, ["gw", "u"], [pk4])
        b.act(gi[:], ps4[0:64, :SEG], AF.Sigmoid, [pk4, "pa"], ["gi"], bias=pa[:, 8:9])
        b.act(at[:], gr[:], AF.Exp, ["gr", "c1"], ["at"], scale=c1[:, 0:1])
        b.act(a2[:], gr[:], AF.Exp, ["gr", "c1"], ["a2"], scale=c1[:, 1:2])
        b.ts(a2[:], a2[:], -1.0, 1.0, ALU.mult, ALU.add, ["a2"], ["a2"])
        b.act(a2[:], a2[:], AF.Sqrt, ["a2"], ["a2"])
        b.tt(bt[:], gi[:], u[:], ALU.mult, ["gi", "u"], ["bt"])
        b.tt(bt[:], bt[:], a2[:], ALU.mult, ["bt", "a2"], ["bt"])
        hcur = hb[s % 2]
        hprev = hb[(s + 1) % 2]
        init = 0.0 if s == 0 else hprev[:, SEG - 1:SEG]
        S.op("dve", lambda e, hcur=hcur, init=init: e.tensor_tensor_scan(out=hcur[:], data0=at[:], data1=bt[:], initial=init,
                                                                        op0=ALU.mult, op1=ALU.add),
             ["at", "bt", ("hb", (s + 1) % 2)], [("hb", s % 2)])
        yt = yo[s % 2]
        b.tt(yt[:], hcur[:], sgs[:, sub * SEG:(sub + 1) * SEG], ALU.mult, [("hb", s % 2), ("sgs", sub)], [("yo", s % 2)], eng="pool")
        S.dma("sp", yg_d[:, s * SEG:(s + 1) * SEG], yt[:], reads=[("yo", s % 2)], writes=[("ygd", s)])
    return [("ygd", s) for s in range(nseg)]


def build_p1a(T=SEQ, SEG=512):
    b = B()
    st = contextlib.ExitStack()
    keys = emit_p1a(b, st, b.din("xTt", [T // 512, 128, KCH * 512]), b.din("wa", [D_MODEL, 128]), b.din("pa", [64, 16]),
                    b.din("gw", [64, 128]), b.dout("ygA", [64, T]), T, SEG)
    return b.finish(keys, extra=[st]), b.outs


def host_p1a(inp, l, xT):
    sp = np.cumsum([0, 512, 512])
    w_in = inp["w_in"][l]
    b_in = inp["b_in"][l]
    maps = []
    for c in range(8):
        cs = slice(64 * c, 64 * c + 64)
        wa = np.concatenate([w_in[:, 0:512][:, cs], w_in[:, 512:1024][:, cs]], axis=1)
        pa = np.zeros((64, 16), np.float32)
        pa[:, 0] = b_in[0:512][cs]
        pa[:, 1] = b_in[512:1024][cs]
        pa[:, 2] = inp["lru_conv_b"][l][cs]
        pa[:, 3:7] = inp["lru_conv_w"][l][:, cs].T
        pa[:, 7] = inp["lru_gate_a_b"][l][cs]
        pa[:, 8] = inp["lru_gate_x_b"][l][cs]
        pa[:, 9] = inp["lru_lambda"][l][cs]
        gw = np.concatenate([inp["lru_gate_a_w"][l][c], inp["lru_gate_x_w"][l][c]], axis=1)
        maps.append({"xTt": xT, "wa": np.ascontiguousarray(wa), "pa": pa, "gw": np.ascontiguousarray(gw)})
    return maps


NQ = 4096
NKV = 6144
HAL = 2048
DILS = (1, 4, 16)
N_BUCKETS = 32
MAX_DISTANCE = 2048


def emit_p1b(b, st, ones, ident, xTh, wq_d, wk_d, wv_d, wg_d, pb_d, bias_d, yg_d):
    S = b.S
    b.pfx = "b_"
    onesb = b.sb("onesb", [128, 128], BF16, st=st)
    b.cp(onesb[:], ones[:, 0:128], ["c_ones"], ["onesb"])
    pb = b.sb("pb", [128, 16], st=st); b.load(pb[:], pb_d, "pb")
    eb = b.sb("eb", [128, 3, 2, 128], st=st); ebf = b.sb("ebf", [128, 3, 2, 128], st=st)
    b.load(eb[:], bias_d.rearrange("p (g k q) -> p g k q", g=3, k=2), "eb")
    b.act(eb[:], eb[:], AF.Exp, ["eb"], ["eb"])
    b.cp(ebf[:], eb[:], ["eb"], ["ebf"])
    for g in range(3):
        b.ts(ebf[:, g, 0, :], ebf[:, g, 0, :], pb[:, 10:11], None, ALU.mult, None, ["ebf", "pb"], ["ebf"])
    wq = b.sb("wq", [128, KCH, 384], BF16, st=st); wk = b.sb("wk", [128, KCH, 384], BF16, st=st); wv = b.sb("wv", [128, KCH, 384], BF16, st=st)
    wg = b.sb("wg", [128, KCH, 128], BF16, st=st)
    S.dma("pool", wk[:], w_ap(wk_d), writes=["wk"]); S.dma("pool", wv[:], w_ap(wv_d), writes=["wv"])
    S.dma("pool", wq[:], w_ap(wq_d), writes=["wq"]); S.dma("pool", wg[:], w_ap(wg_d), writes=["wg"])
    xb = [b.sb("xb%d" % i, [128, KCH, 512], BF16, st=st) for i in range(2)]
    KT = b.sb("KT", [128, NKV], BF16, st=st)
    VT = b.sb("VT", [128, NKV], F32, st=st)
    QT = b.sb("QT", [128, NQ], BF16, st=st)
    bg = b.sb("bg", [128, NQ], F32, st=st)
    ND = b.sb("ND", [128, 2, NQ], F32, st=st)
    esb = [b.sb("esb%d" % i, [128, 256], F32, st=st) for i in range(2)]
    PT = [b.sb("PT%d" % i, [128, 2, 128], BF16, st=st) for i in range(2)]
    Vt = [b.sb("Vt%d" % i, [128, 2, 128], BF16, st=st) for i in range(2)]
    scale = 128.0 ** -0.5
    xi = 0
    bi = 0
    for g, dil in enumerate(DILS):
        start = HAL - max(512, 128 * dil)
        tl_all = list(range(start, NKV, 512))
        for pi in range(0, len(tl_all), 1):
            pair = tl_all[pi:pi + 1]
            xss = []
            for t0 in pair:
                xs = xb[xi % 2]; xk = ("xb", xi % 2); xi += 1
                S.dma("pool", xs[:], x_tile(xTh, t0 // 512), writes=[xk])
                xss.append((xs, xk, t0))
            secs = [("k", wk, "wk"), ("v", wv, "wv"), ("q", wq, "wq")] + ([("g", wg, "wg")] if g == 0 else [])
            for nm, wt, wkey in secs:
                xsel = xss if nm in ("k", "v") else [t for t in xss if t[2] >= HAL]
                if not xsel:
                    continue
                pss = [b.psum() for _ in xsel]
                for k in range(KCH):
                    for (ps_, pk_), (xs, xk, t0) in zip(pss, xsel):
                        lhs = wt[:, k, :] if nm == "g" else wt[:, k, g * 128:(g + 1) * 128]
                        b.mm(ps_[:, :], lhs, xs[:, k, :], k == 0, k == KCH - 1, [wkey, xk], [pk_])
                for (ps_, pk_), (xs, xk, t0) in zip(pss, xsel):
                    if nm == "k":
                        b.act(KT[:, t0:t0 + 512], ps_[:, :], AF.Identity, [pk_, "pb"], [("KT", t0)], bias=pb[:, 3 + g:4 + g])
                    elif nm == "v":
                        b.ts(VT[:, t0:t0 + 512], ps_[:, :], pb[:, 6 + g:7 + g], None, ALU.add, None, [pk_, "pb"], [("VT", t0)])
                    elif nm == "q":
                        b.act(QT[:, t0 - HAL:t0 - HAL + 512], ps_[:, :], AF.Identity, [pk_, "pb"], [("QT", t0 - HAL)], bias=pb[:, g:g + 1])
                    else:
                        b.act(bg[:, t0 - HAL:t0 - HAL + 512], ps_[:, :], AF.Silu, [pk_, "pb"], ["bg"], bias=pb[:, 9:10])
        span = 128 * dil
        allK = [("KT", t) for t in range(start, NKV, 512)]
        allV = [("VT", t) for t in range(start, NKV, 512)]
        allQ = [("QT", t) for t in range(0, NQ, 512)]

        def tiles_of(keys, base, lo, hi):
            return [(keys, t) for t in range(base, 99999, 512) if t < hi and t + 512 > lo]
        for r in range(dil):
            for nl in range(32 // dil):
                qs = r + span * nl
                ssl = lambda a: slice(a, a + 127 * dil + 1, dil)
                qsl = ssl(qs)
                kc_ = ssl(HAL + qs)
                kp_ = ssl(HAL + qs - span)
                qk = [("QT", t) for t in range(0, NQ, 512) if t < qs + span and t + 512 > qs]
                kk_ = [("KT", t) for t in range(start, NKV, 512) if t < HAL + qs + span and t + 512 > HAL + qs - span]
                vk_ = [("VT", t) for t in range(start, NKV, 512) if t < HAL + qs + span and t + 512 > HAL + qs - span]
                psl, kl = b.psum()
                b.mm(psl[:, 0:128], KT[:, kp_], QT[:, qsl], True, True, kk_ + qk, [kl])
                b.mm(psl[:, 128:256], KT[:, kc_], QT[:, qsl], True, True, kk_ + qk, [kl])
                e_ = esb[bi % 2]; ek = ("esb", bi % 2)
                b.act(e_[:], psl[:, 0:256], AF.Exp, [kl], [ek], scale=scale)
                p_ = PT[bi % 2]; pk_ = ("PT", bi % 2)
                ebsel = ebf if nl == 0 else eb
                b.tt(p_[:].rearrange("p a q -> p (a q)"), e_[:], ebsel[:, g].rearrange("p a q -> p (a q)"), ALU.mult, [ek, "eb", "ebf"], [pk_])
                psv, kv = b.psum()
                b.tr(psv[:, 0:128], VT[:, kp_], ident[:], vk_ + ["c_ident"], [kv])
                b.tr(psv[:, 128:256], VT[:, kc_], ident[:], vk_ + ["c_ident"], [kv])
                v_ = Vt[bi % 2]; vk2 = ("Vt", bi % 2)
                b.cp(v_[:].rearrange("p a q -> p (a q)"), psv[:, 0:256], [kv], [vk2], eng="act")
                pso, ko = b.psum()
                b.mm(pso[:, 0:128], v_[:, 0, :], p_[:, 0, :], True, False, [vk2, pk_], [ko])
                b.mm(pso[:, 0:128], v_[:, 1, :], p_[:, 1, :], False, True, [vk2, pk_], [ko])
                b.mm(pso[:, 128:256], onesb[:], p_[:, 0, :], True, False, ["onesb", pk_], [ko])
                b.mm(pso[:, 128:256], onesb[:], p_[:, 1, :], False, True, ["onesb", pk_], [ko])
                src = pso[:, 0:256].rearrange("p (a q) -> p a q", a=2)
                if g == 0:
                    b.cp(ND[:, :, qsl], src, [ko], ["ND"], eng="act")
                else:
                    b.tt(ND[:, :, qsl], ND[:, :, qsl], src, ALU.add, ["ND", ko], ["ND"])
                bi += 1
    S.op("dve", lambda e: e.reciprocal(out=ND[:, 1, :], in_=ND[:, 1, :]), ["ND"], ["ND"])
    b.tt(ND[:, 0, :], ND[:, 0, :], ND[:, 1, :], ALU.mult, ["ND"], ["ND"])
    b.tt(ND[:, 0, :], ND[:, 0, :], bg[:], ALU.mult, ["ND", "bg"], ["ND"], eng="pool")
    S.dma("sp", yg_d, ND[:, 0, :], reads=["ND"], writes=["outB"])
    return ["outB"]


def p1b_dram(b):
    return (b.din("xTht", [NKV // 512, 128, KCH * 512]), b.din("wq", [D_MODEL, 384]), b.din("wk", [D_MODEL, 384]), b.din("wv", [D_MODEL, 384]),
            b.din("wg", [D_MODEL, 128]), b.din("pb", [128, 16]), b.din("biasT", [128, 3 * 2 * 128]), b.dout("ygB", [128, NQ]))


def build_p1b():
    b = B()
    ones, ident = b.ident()
    st = contextlib.ExitStack()
    keys = emit_p1b(b, st, ones, ident, *p1b_dram(b))
    return b.finish(keys, extra=[st]), b.outs


def t5_bucket(dist):
    import math
    max_exact = N_BUCKETS // 2
    large = max_exact + (np.log(np.maximum(dist, 1) / max_exact) / math.log(MAX_DISTANCE / max_exact)
                         * (N_BUCKETS - max_exact)).astype(np.int32)
    large = np.minimum(large, N_BUCKETS - 1)
    return np.where(dist < max_exact, dist, large).astype(np.int32)


def host_p1b(inp, l, xT):
    w_in = inp["w_in"][l]; b_in = inp["b_in"][l]
    Q0 = 1024; K0 = Q0 + 1536; V0 = K0 + 1536; G0 = V0 + 1536
    table = inp["att_rel_bias"]
    ki = np.arange(128)[:, None, None]; kb = np.arange(2)[None, :, None]; qi = np.arange(128)[None, None, :]
    dist = qi + 128 - (kb * 128 + ki)
    valid = (dist >= 0) & (dist <= 128)
    maps = []
    for c in range(8):
        hm, half = c // 2, c % 2
        cols = lambda base: np.concatenate([w_in[:, base + (g * 4 + hm) * 128: base + (g * 4 + hm + 1) * 128] for g in range(3)], axis=1)
        pb = np.zeros((128, 16), np.float32)
        for g in range(3):
            hh = (g * 4 + hm) * 128
            pb[:, g] = b_in[Q0 + hh:Q0 + hh + 128]
            pb[:, 3 + g] = b_in[K0 + hh:K0 + hh + 128]
            pb[:, 6 + g] = b_in[V0 + hh:V0 + hh + 128]
        pb[:, 9] = b_in[G0 + hm * 128:G0 + hm * 128 + 128]
        pb[:, 10] = float(half)
        bT = np.zeros((128, 3, 2, 128), np.float32)
        for g, dil in enumerate(DILS):
            bucket = t5_bucket(np.clip(dist, 0, 128) * dil)
            bT[:, g] = np.where(valid, table[bucket, g * 4 + hm], np.float32(-100.0))
        xh = np.zeros((2048, NKV), np.float32)
        base = half * NQ
        if half == 0:
            xh[:, HAL:] = xT[:, 0:NQ]
        else:
            xh[:] = xT[:, base - HAL:base + NQ]
        maps.append({"xTht": tile_xT(xh), "wq": np.ascontiguousarray(cols(Q0)), "wk": np.ascontiguousarray(cols(K0)),
                     "wv": np.ascontiguousarray(cols(V0)), "wg": np.ascontiguousarray(w_in[:, G0 + hm * 128:G0 + (hm + 1) * 128]),
                     "pb": pb, "biasT": bT.reshape(128, 768)})
    return maps


def gather_p1b(results):
    yg = np.zeros((512, 8192), np.float32)
    for c, r in enumerate(results):
        hm, half = c // 2, c % 2
        yg[hm * 128:(hm + 1) * 128, half * NQ:(half + 1) * NQ] = r["ygB"]
    return yg


SEGC = 1024
NW = 3
GN_EPS = 64e-5
DEC_C = float(np.exp(-0.5))


def emit_p1c(b, st, ones, ident, xT, wc_d, pc_d, wup_d, gnb_d, yg_d, T=SEQ):
    S = b.S
    b.pfx = "c_"
    nseg = T // SEGC
    NCH = SEGC // 128
    pc = b.sb("pc", [64, 32], st=st); b.load(pc[:], pc_d, "pc")
    wup = b.sb("wup", [64, 128], st=st); b.load(wup[:], wup_d, "wup")
    gnb = b.sb("gnb", [128, 192], st=st); b.load(gnb[:], gnb_d, "gnb")
    wc = b.sb("wc", [128, KCH, 384], BF16, st=st)
    S.dma("pool", wc[:], w_ap(wc_d), writes=["wc"])
    pcx = b.sb("pcx", [64, 4], st=st)
    b.ts(pcx[:, 0:1], pc[:, 13:14], -1.0, 1.0, ALU.mult, ALU.add, ["pc"], ["pcx"])
    mask2 = b.sb("mask2", [128, 2, 128], st=st); maskL = b.sb("maskL", [128, 128], st=st); maskc = b.sb("maskc", [64, SEGC], st=st)
    S.op("pool", lambda e: e.affine_select(out=mask2[:, 0, :], in_=ones[:, 0:128], pattern=[[1, 128]], compare_op=ALU.is_gt,
                                           fill=0.0, base=0, channel_multiplier=-1), ["c_ones"], ["mask2"])
    S.op("pool", lambda e: e.affine_select(out=mask2[:, 1, :], in_=ones[:, 0:128], pattern=[[1, 128]], compare_op=ALU.is_ge,
                                           fill=0.0, base=0, channel_multiplier=-1), ["c_ones"], ["mask2"])
    S.op("pool", lambda e: e.affine_select(out=maskL[:], in_=ones[:, 0:128], pattern=[[-1, 128]], compare_op=ALU.is_gt,
                                           fill=0.0, base=0, channel_multiplier=1), ["c_ones"], ["maskL"])
    b.memset(maskc[:], 1.0, ["maskc"])
    b.memset(maskc[:, 0:SEGC - 127:128], 0.0, ["maskc"])

    xb = [b.sb("xb%d" % i, [128, KCH, 512], BF16, st=st) for i in range(2)]
    raw = [b.sb("raw%d" % i, [64, SEGC + 1], st=st) for i in range(5)]
    sh = [b.sb("sh%d" % i, [64, SEGC], st=st) for i in range(5)]
    names = ["lw", "aicl", "kk", "kc", "bb", "cum", "tmp1", "tmp2", "bt", "kt", "bh", "kh", "rkr"]
    A = {n: b.sb(n, [64, SEGC], st=st) for n in names}
    atrt = b.sb("atrt", [64, 2, SEGC], st=st)
    wl = b.sb("wl", [64, NCH], st=st)
    sgc = b.sb("sgc", [64, SEGC], st=st)
    obuf = [b.sb("obuf%d" % i, [64, SEGC], st=st) for i in range(2)]
    ST = [b.sb("ST%d" % i, [64, 64], st=st) for i in range(2)]
    tok4 = [b.sb("tok4_%d" % i, [128, 5, 64], st=st) for i in range(NW)]
    AB = [b.sb("AB%d" % i, [128, 2, 128], st=st) for i in range(NW)]
    AK = [b.sb("AK%d" % i, [128, 2, 128], st=st) for i in range(NW)]
    Mb = [[b.sb("Mb%d_%d" % (q, i), [128, 128], st=st) for i in range(2)] for q in range(NW)]
    MTb = [[b.sb("MTb%d_%d" % (q, i), [128, 128], st=st) for i in range(2)] for q in range(NW)]
    TTb = [[b.sb("TTb%d_%d" % (q, i), [128, 128], st=st) for i in range(2)] for q in range(NW)]
    Xsb = [b.sb("Xsb%d" % q, [128, 64], st=st) for q in range(NW)]
    PQ = [b.sb("PQ%d" % i, [128, 2, 64], st=st) for i in range(NW)]
    GT = [b.sb("GT%d" % q, [64, 64], st=st) for q in range(NW)]; Hs = [b.sb("Hs%d" % q, [64, 64], st=st) for q in range(NW)]
    R2T = [b.sb("R2T%d" % q, [64, 128], st=st) for q in range(NW)]
    bst = [b.sb("bst%d" % q, [128, 6], st=st) for q in range(NW)]; mv = [b.sb("mv%d" % q, [128, 4], st=st) for q in range(NW)]
    bsc = [b.sb("bsc%d" % q, [128, 1], st=st) for q in range(NW)]
    yn = [b.sb("yn%d" % i, [128, 64], st=st) for i in range(NW)]

    b.memset(ST[0][:], 0.0, [("ST", 0)])
    for g in range(5):
        b.memset(raw[g][:, 0:1], 0.0, [("raw", g)])
    xi = 0
    stc = 0
    outk = []
    for s in range(nseg):
        NTL = SEGC // 512

        def load_x(seg):
            nonlocal xi
            out = []
            for tl in range(NTL):
                xs = xb[xi % 2]; xk = ("xb", xi % 2); xi += 1
                S.dma("pool", xs[:], x_tile(xT, seg * NTL + tl), writes=[xk])
                out.append((xs, xk))
            return out
        xss = x_next if s > 0 else load_x(0)
        for g in range(6):
            pss = [b.psum() for _ in range(NTL)]
            for k in range(KCH):
                for tl in range(NTL):
                    b.mm(pss[tl][0][0:64, :], wc[:, k, g * 64:(g + 1) * 64], xss[tl][0][:, k, :], k == 0, k == KCH - 1, ["wc", xss[tl][1]], [pss[tl][1]])
            for tl in range(NTL):
                ps, pk = pss[tl]
                if g < 5:
                    b.act(raw[g][:, 1 + tl * 512:1 + (tl + 1) * 512], ps[0:64, :], AF.Identity, [pk, "pc"], [("raw", g)], bias=pc[:, g:g + 1])
                else:
                    b.act(sgc[:, tl * 512:(tl + 1) * 512], ps[0:64, :], AF.Silu, [pk, "pc"], ["sgc"], bias=pc[:, 16:17])
        x_next = load_x(s + 1) if s + 1 < nseg else None
        for g in range(5):
            b.tt(sh[g][:], raw[g][:, 0:SEGC], raw[g][:, 1:SEGC + 1], ALU.subtract, [("raw", g)], [("sh", g)])
            b.stt(sh[g][:], sh[g][:], pc[:, 5 + g:6 + g], raw[g][:, 1:SEGC + 1], ALU.mult, ALU.add, [("sh", g), "pc", ("raw", g)], [("sh", g)])
            b.cp(raw[g][:, 0:1], raw[g][:, SEGC:SEGC + 1], [("raw", g)], [("raw", g)], eng="pool")
        s_r, s_k, s_v, s_w, s_a = sh
        kr, kk_, kv_, kw_, ka_ = [("sh", g) for g in range(5)]
        b.act(s_w[:], s_w[:], AF.Tanh, [kw_], [kw_])
        for tl in range(2):
            sl = slice(tl * 512, (tl + 1) * 512)
            ps, pk = b.psum()
            b.mm(ps[0:64, :], wup[:, 0:64], s_w[:, sl], True, True, ["wup", kw_], [pk])
            b.act(A["lw"][:, sl], ps[0:64, :], AF.Sigmoid, [pk, "pc"], ["lw"], bias=pc[:, 10:11])
            ps, pk = b.psum()
            b.mm(ps[0:64, :], wup[:, 64:128], s_a[:, sl], True, True, ["wup", ka_], [pk])
            b.act(A["aicl"][:, sl], ps[0:64, :], AF.Sigmoid, [pk, "pc"], ["aicl"], bias=pc[:, 11:12])
        b.ts(A["lw"][:], A["lw"][:], -DEC_C, None, ALU.mult, None, ["lw"], ["lw"])
        b.ts(A["kk"][:], s_k[:], pc[:, 12:13], None, ALU.mult, None, [kk_, "pc"], ["kk"])
        b.tt(A["tmp1"][:], A["kk"][:], A["kk"][:], ALU.mult, ["kk"], ["tmp1"])
        for tl in range(2):
            sl = slice(tl * 512, (tl + 1) * 512)
            ps, pk = b.psum()
            b.mm(ps[0:64, :], ones[0:64, 0:64], A["tmp1"][:, sl], True, True, ["c_ones", "tmp1"], [pk])
            b.act(A["tmp2"][:, sl], ps[0:64, :], AF.Sqrt, [pk], ["tmp2"])
        b.ts(A["tmp2"][:], A["tmp2"][:], 1e-12, None, ALU.max, None, ["tmp2"], ["tmp2"])
        S.op("dve", lambda e: e.reciprocal(out=A["tmp2"][:], in_=A["tmp2"][:]), ["tmp2"], ["tmp2"])
        b.tt(A["kk"][:], A["kk"][:], A["tmp2"][:], ALU.mult, ["kk", "tmp2"], ["kk"])
        b.ts(A["tmp1"][:], A["aicl"][:], pc[:, 13:14], pcx[:, 0:1], ALU.mult, ALU.add, ["aicl", "pc", "pcx"], ["tmp1"])
        b.tt(A["kc"][:], s_k[:], A["tmp1"][:], ALU.mult, [kk_, "tmp1"], ["kc"])
        b.tt(A["bb"][:], A["kk"][:], A["aicl"][:], ALU.mult, ["kk", "aicl"], ["bb"], eng="pool")
        S.op("dve", lambda e: e.tensor_tensor_scan(out=A["cum"][:], data0=maskc[:], data1=A["lw"][:], initial=0.0, op0=ALU.mult, op1=ALU.add),
             ["maskc", "lw"], ["cum"])
        b.tt(A["tmp1"][:], A["cum"][:], A["lw"][:], ALU.subtract, ["cum", "lw"], ["tmp1"])
        b.act(A["tmp1"][:], A["tmp1"][:], AF.Exp, ["tmp1"], ["tmp1"])
        b.stt(atrt[:, 0, :], A["kk"][:], -1.0, A["tmp1"][:], ALU.mult, ALU.mult, ["kk", "tmp1"], ["atrt"])
        b.act(A["tmp2"][:], A["cum"][:], AF.Exp, ["cum"], ["tmp2"])
        b.tt(atrt[:, 1, :], s_r[:], A["tmp2"][:], ALU.mult, [kr, "tmp2"], ["atrt"])
        b.act(A["tmp2"][:], A["cum"][:], AF.Exp, ["cum", "atrt"], ["tmp2"], scale=-1.0)
        b.tt(A["bt"][:], A["bb"][:], A["tmp2"][:], ALU.mult, ["bb", "tmp2"], ["bt"])
        b.tt(A["kt"][:], A["kc"][:], A["tmp2"][:], ALU.mult, ["kc", "tmp2"], ["kt"], eng="pool")
        for c in range(NCH):
            cs = slice(c * 128, (c + 1) * 128)
            b.act(A["tmp1"][:, cs], A["cum"][:, cs], AF.Exp, ["cum", "atrt"], ["tmp1"], scale=-1.0, bias=A["cum"][:, c * 128 + 127:c * 128 + 128])
        b.tt(A["bh"][:], A["bb"][:], A["tmp1"][:], ALU.mult, ["bb", "tmp1"], ["bh"])
        b.tt(A["kh"][:], A["kc"][:], A["tmp1"][:], ALU.mult, ["kc", "tmp1"], ["kh"], eng="pool")
        b.act(wl[:], A["cum"][:, 127:SEGC:128], AF.Exp, ["cum"], ["wl"])
        b.stt(A["rkr"][:], s_r[:], pc[:, 15:16], A["kc"][:], ALU.mult, ALU.mult, [kr, "pc", "kc"], ["rkr"])

        ob = obuf[s % 2]; obk = ("obuf", s % 2)
        def chunk_steps(c, p):
            nonlocal stc
            cs = slice(c * 128, (c + 1) * 128)
            t4 = tok4[p]; t4k = ("tok4", p)
            psT, kT = b.psum()
            b.tr(psT[:, 0:64], atrt[:, 0, cs], ident[0:64, 0:64], ["atrt", "c_ident"], [kT])
            b.tr(psT[:, 64:128], s_v[:, cs], ident[0:64, 0:64], [kv_, "c_ident"], [kT])
            b.tr(psT[:, 128:192], A["bh"][:, cs], ident[0:64, 0:64], ["bh", "c_ident"], [kT])
            b.tr(psT[:, 192:256], A["kh"][:, cs], ident[0:64, 0:64], ["kh", "c_ident"], [kT])
            b.tr(psT[:, 256:320], sgc[:, cs], ident[0:64, 0:64], ["sgc", "c_ident"], [kT])
            b.cp(t4[:].rearrange("p a q -> p (a q)"), psT[:, 0:320], [kT], [t4k], eng="act")
            yield
            ab = AB[p]; abk = ("AB", p); ak = AK[p]; akk = ("AK", p)
            ps1, k1 = b.psum()
            b.mm(ps1[:, 0:256], A["bt"][:, cs], atrt[:, :, cs], True, True, ["bt", "atrt"], [k1])
            b.tt(ab[:].rearrange("p a q -> p (a q)"), ps1[:, 0:256], mask2[:].rearrange("p a q -> p (a q)"), ALU.mult, [k1, "mask2"], [abk])
            yield
            ps2, k2 = b.psum()
            b.mm(ps2[:, 0:256], A["kt"][:, cs], atrt[:, :, cs], True, True, ["kt", "atrt"], [k2])
            b.tt(ak[:].rearrange("p a q -> p (a q)"), ps2[:, 0:256], mask2[:].rearrange("p a q -> p (a q)"), ALU.mult, [k2, "mask2"], [akk])
            yield
            ps3, k3 = b.psum()
            b.mm(ps3[:, 0:128], atrt[:, 0, cs], A["bt"][:, cs], True, True, ["bt", "atrt"], [k3])
            b.tt(Mb[p][0][:], ps3[:, 0:128], maskL[:], ALU.mult, [k3, "maskL"], [("Mb", p, 0)])
            b.tt(TTb[p][0][:], ab[:, 0, :], ident[:], ALU.add, [abk, "c_ident"], [("TTb", p, 0)], eng="pool")
            yield
            Mc, Mck = Mb[p][0], ("Mb", p, 0)
            MTc, MTck = ab[:, 0, :], abk
            for lev in range(1, 7):
                Mn, Mnk = Mb[p][lev % 2], ("Mb", p, lev % 2)
                psm, km = b.psum()
                b.mm(psm[:, 0:128], MTc, Mc[:], True, True, [MTck, Mck], [km])
                if lev < 6:
                    MTn, MTnk = MTb[p][lev % 2], ("MTb", p, lev % 2)
                    psn, kn = b.psum()
                    b.mm(psn[:, 0:128], Mc[:], MTc, True, True, [MTck, Mck], [kn])
                b.cp(Mn[:], psm[:, 0:128], [km], [Mnk], eng="act")
                if lev < 6:
                    b.cp(MTn[:], psn[:, 0:128], [kn], [MTnk], eng="dve")
                yield
                pst, kt_ = b.psum()
                b.mm(pst[:, 0:128], Mn[:], TTb[p][(lev - 1) % 2][:], True, True, [Mnk, ("TTb", p, (lev - 1) % 2)], [kt_])
                b.tt(TTb[p][lev % 2][:], TTb[p][(lev - 1) % 2][:], pst[:, 0:128], ALU.add, [("TTb", p, (lev - 1) % 2), kt_], [("TTb", p, lev % 2)])
                yield
                Mc, Mck = Mn, Mnk
                if lev < 6:
                    MTc, MTck = MTn[:], MTnk
            TT, TTk = TTb[p][0], ("TTb", p, 0)
            pq = PQ[p]; pqk = ("PQ", p)
            psP, kP = b.psum()
            b.mm(psP[:, 0:64], TT[:], t4[:, 0, :], True, True, [TTk, t4k], [kP])
            psX, kX = b.psum()
            b.mm(psX[:, 0:64], ak[:, 0, :], t4[:, 1, :], True, True, [akk, t4k], [kX])
            b.cp(Xsb[p][:], psX[:, 0:64], [kX], [("Xsb", p)], eng="act")
            yield
            b.mm(psP[:, 64:128], TT[:], Xsb[p][:], True, True, [TTk, ("Xsb", p)], [kP])
            b.cp(pq[:].rearrange("p a q -> p (a q)"), psP[:, 0:128], [kP], [pqk], eng="dve")
            yield
            psg, kg = b.psum()
            b.mm(psg[0:64, 0:64], pq[:, 0, :], t4[:, 2, :], True, True, [pqk, t4k], [kg])
            b.stt(GT[p][:], ident[0:64, 0:64], wl[:, c:c + 1], psg[0:64, 0:64], ALU.mult, ALU.add, ["c_ident", "wl", kg], [("GT", p)])
            psh, kh_ = b.psum()
            b.mm(psh[0:64, 0:64], t4[:, 2, :], pq[:, 1, :], True, False, [pqk, t4k], [kh_])
            b.mm(psh[0:64, 0:64], t4[:, 3, :], t4[:, 1, :], False, True, [t4k], [kh_])
            b.cp(Hs[p][:], psh[0:64, 0:64], [kh_], [("Hs", p)], eng="act")
            yield
            psr, kr_ = b.psum()
            b.mm(psr[0:64, 0:128], pq[:, 0, :], ab[:, 1, :], True, True, [pqk, abk], [kr_])
            b.tt(R2T[p][:], psr[0:64, 0:128], atrt[:, 1, cs], ALU.add, [kr_, "atrt"], [("R2T", p)])
            psb, kb = b.psum()
            b.mm(psb[:, 0:1], A["rkr"][:, cs], ones[0:64, 0:1], True, True, ["rkr", "c_ones"], [kb])
            b.cp(bsc[p][:], psb[:, 0:1], [kb], [("bsc", p)], eng="act")
            yield
            stcur, stk = ST[stc % 2], ("ST", stc % 2)
            stnew, stnk = ST[(stc + 1) % 2], ("ST", (stc + 1) % 2)
            stc += 1
            psy, ky = b.psum()
            b.mm(psy[:, 0:64], R2T[p][:], stcur[:], True, False, [("R2T", p), stk], [ky])
            b.mm(psy[:, 0:64], ab[:, 1, :], pq[:, 1, :], False, False, [abk, pqk], [ky])
            b.mm(psy[:, 0:64], ak[:, 1, :], t4[:, 1, :], False, True, [akk, t4k], [ky])
            pss, ks = b.psum()
            b.mm(pss[0:64, 0:64], GT[p][:], stcur[:], True, True, [("GT", p), stk], [ks])
            b.tt(stnew[:], pss[0:64, 0:64], Hs[p][:], ALU.add, [ks, ("Hs", p)], [stnk])
            S.op("dve", lambda e, psy=psy: e.bn_stats(out=bst[p][:], in_=psy[:, 0:64]), [ky], [("bst", p)])
            S.op("dve", lambda e: e.bn_aggr(out=mv[p][:, 0:2], in_=bst[p][:]), [("bst", p)], [("mv", p)])
            yield
            b.act(mv[p][:, 2:3], mv[p][:, 1:2], AF.Sqrt, [("mv", p)], [("mv2", p)], bias=GN_EPS)
            S.op("dve", lambda e: e.reciprocal(out=mv[p][:, 2:3], in_=mv[p][:, 2:3]), [("mv2", p)], [("mv2", p)])
            b.stt(mv[p][:, 3:4], mv[p][:, 0:1], -1.0, mv[p][:, 2:3], ALU.mult, ALU.mult, [("mv", p), ("mv2", p)], [("mv3", p)])
            y_ = yn[p]; ynk = ("yn", p)
            b.act(y_[:], psy[:, 0:64], AF.Identity, [ky, ("mv2", p), ("mv3", p)], [ynk], bias=mv[p][:, 3:4], scale=mv[p][:, 2:3])
            yield
            b.tt(y_[:], y_[:], gnb[:, 0:64], ALU.mult, [ynk, "gnb"], [ynk])
            b.tt(y_[:], y_[:], gnb[:, 64:128], ALU.add, [ynk, "gnb"], [ynk], eng="pool")
            b.stt(y_[:], t4[:, 1, :], bsc[p][:, 0:1], y_[:], ALU.mult, ALU.add, [t4k, ("bsc", p), ynk], [ynk])
            b.tt(y_[:], y_[:], t4[:, 4, :], ALU.mult, [ynk, t4k], [ynk], eng="pool")
            pso, ko = b.psum()
            b.tr(pso[0:64, 0:128], y_[:], ident[:], [ynk, "c_ident"], [ko])
            b.cp(ob[:, cs], pso[0:64, 0:128], [ko], [obk], eng="act")
            yield

        import itertools
        for c0 in range(0, NCH, NW):
            gens = [chunk_steps(c0 + q, q) for q in range(min(NW, NCH - c0))]
            for _ in itertools.zip_longest(*gens):
                pass
        S.dma("sp", yg_d[:, s * SEGC:(s + 1) * SEGC], ob[:], reads=[obk], writes=[("outC", s)])
        outk.append(("outC", s))
    return outk


def p1c_dram(b, T=SEQ):
    return (b.din("wc", [D_MODEL, 384]), b.din("pc", [64, 32]), b.din("wup", [64, 128]), b.din("gnb", [128, 192]), b.dout("ygC", [64, T]))


def build_p1c(T=SEQ):
    b = B()
    ones, ident = b.ident()
    st = contextlib.ExitStack()
    keys = emit_p1c(b, st, ones, ident, b.din("xTt", [T // 512, 128, KCH * 512]), *p1c_dram(b, T), T=T)
    return b.finish(keys, extra=[st]), b.outs


def host_p1c(inp, l, xT):
    w_in = inp["w_in"][l]; b_in = inp["b_in"][l]
    C0 = 1024 + 5120
    mu = inp["rwkv_mu"][l]
    maps = []
    for c in range(8):
        hs = slice(64 * c, 64 * c + 64)
        secs = [(C0, hs), (C0 + 512, hs), (C0 + 1024, hs), (C0 + 1536, slice(0, 64)), (C0 + 1600, slice(0, 64)), (C0 + 1664, hs)]
        wc = np.concatenate([w_in[:, o:o + 512][:, sl_] if sl_ is hs else w_in[:, o:o + 64] for o, sl_ in secs], axis=1)
        pc = np.zeros((64, 32), np.float32)
        pc[:, 0] = b_in[C0:C0 + 512][hs]; pc[:, 1] = b_in[C0 + 512:C0 + 1024][hs]; pc[:, 2] = b_in[C0 + 1024:C0 + 1536][hs]
        pc[:, 3] = b_in[C0 + 1536:C0 + 1600]; pc[:, 4] = b_in[C0 + 1600:C0 + 1664]
        pc[:, 5] = mu[0:512][hs]; pc[:, 6] = mu[512:1024][hs]; pc[:, 7] = mu[1024:1536][hs]
        pc[:, 8] = mu[1536:1600]; pc[:, 9] = mu[1600:1664]
        pc[:, 10] = inp["rwkv_w0"][l][hs]; pc[:, 11] = inp["rwkv_a0"][l][hs]
        pc[:, 12] = inp["rwkv_k_k"][l][hs]; pc[:, 13] = inp["rwkv_k_a"][l][hs]
        pc[:, 15] = inp["rwkv_r_k"][l][c]
        pc[:, 16] = b_in[C0 + 1664:C0 + 2176][hs]
        wup = np.concatenate([inp["rwkv_w_up"][l][:, hs], inp["rwkv_a_up"][l][:, hs]], axis=1)
        gn = np.concatenate([inp["rwkv_gn_g"][l][hs], inp["rwkv_gn_b"][l][hs], b_in[C0 + 1664:C0 + 2176][hs]])
        gnb = np.ascontiguousarray(np.broadcast_to(gn[None, :], (128, 192)))
        maps.append({"xTt": xT, "wc": np.ascontiguousarray(wc), "pc": pc, "wup": np.ascontiguousarray(wup), "gnb": gnb})
    return maps


def build_p1():
    b = B()
    S = b.S
    ones, ident = b.ident()
    xT = b.din("xTt", [SEQ // 512, 128, KCH * 512])
    a_dram = (b.din("wa", [D_MODEL, 128]), b.din("pa", [64, 16]), b.din("gw", [64, 128]), b.dout("ygA", [64, SEQ]))
    b_dram = p1b_dram(b)
    c_dram = p1c_dram(b)
    keys = []
    st = contextlib.ExitStack()
    keys += emit_p1a(b, st, xT, *a_dram)
    S.barrier(); st.close()
    st = contextlib.ExitStack()
    keys += emit_p1b(b, st, ones, ident, *b_dram)
    S.barrier(); st.close()
    st = contextlib.ExitStack()
    keys += emit_p1c(b, st, ones, ident, xT, *c_dram)
    return b.finish(keys, extra=[st]), b.outs


def host_p1(inp, l, xT):
    xTt = tile_xT(xT)
    ma, mb, mc = host_p1a(inp, l, xTt), host_p1b(inp, l, xT), host_p1c(inp, l, xTt)
    maps = []
    for c in range(8):
        m = dict(ma[c]); m.update(mb[c]); m.update(mc[c])
        maps.append(m)
    return maps


ALPHA = (2.0 * 2) ** 0.25
LN_EPS = 1e-5
NT = 1024
HALO = 32


def build_p2():
    b = B()
    S = b.S
    xTh = b.din("xTh", [D_MODEL, NT + HALO])
    xown = b.din("xown", [NT, D_MODEL])
    ygT = b.din("ygT", [1536, NT])
    wm_d = b.din("wm", [D_MODEL, 8192])
    wd_d = b.din("wd", [D_MODEL, 1536])
    wbr_d = b.din("wbr", [2048, D_MODEL])
    wout_d = b.din("wout", [D_MODEL, D_MODEL])
    pm_d = b.din("pm", [128, 64])
    pd_d = b.din("pd", [128, 64])
    dww_d = b.din("dww", [128, 4 * 31])
    lng_d = b.din("lng", [128, D_MODEL])
    lnb_d = b.din("lnb", [128, D_MODEL])
    xo_d = b.dout("xo", [NT, D_MODEL])

    ones, ident = b.ident()
    pm = b.sb("pm", [128, 64]); pd = b.sb("pd", [128, 64]); dww = b.sb("dww", [128, 4, 31])
    b.load(pm[:], pm_d, "pm"); b.load(pd[:], pd_d, "pd")
    b.load(dww[:], dww_d.rearrange("p (c j) -> p c j", j=31), "dww")
    mixT = b.sb("mixT", [128, KCH, NT], BF16)

    stDM = contextlib.ExitStack()
    xb = b.sb("xb", [128, KCH, NT + HALO], BF16, st=stDM)
    ygd = b.sb("ygd", [128, 4, NT], BF16, st=stDM)
    S.dma("pool", xb[:], xT_tile_ap(xTh, 0, NT + HALO), writes=["xb"])

    stD = contextlib.ExitStack()
    wdb = [b.sb("wdb%d" % i, [128, KCH, 3, 128], BF16, st=stD) for i in range(2)]
    cu = b.sb("cu", [128, NT + HALO], F32, st=stD)
    t1 = b.sb("t1", [128, NT + HALO], F32, st=stD)
    t2 = b.sb("t2", [128, NT + HALO], F32, st=stD)
    cv = b.sb("cv", [128, 4, NT], F32, st=stD)
    dg = b.sb("dg", [128, 4, NT], F32, st=stD)
    sq = b.sb("sq", [128, NT], F32, st=stD)
    mean = b.sb("mean", [128, NT], F32, st=stD)
    rstd = b.sb("rstd", [128, NT], F32, st=stD)
    tiles = [(0, HALO), (HALO, 512), (HALO + 512, 512)]
    for cc in range(4):
        w = wdb[cc % 2]; wk = ("wdb", cc % 2)
        for j in range(3):
            S.dma("pool", w[:, :, j, :], w_ap(wd_d[:, j * 512 + cc * 128: j * 512 + cc * 128 + 128]), writes=[wk])
        groups = [[tiles[0]], [tiles[1], tiles[2]]]
        for grp in groups:
            for sec in range(3):
                if sec == 2 and grp[0][0] < HALO:
                    continue
                pss = [b.psum() for _ in grp]
                for k in range(KCH):
                    for (ps_, pk_), (t0, n) in zip(pss, grp):
                        b.mm(ps_[:, :n], w[:, k, sec, :], xb[:, k, t0:t0 + n], k == 0, k == KCH - 1, [wk, "xb"], [pk_])
                for (ps_, pk_), (t0, n) in zip(pss, grp):
                    if sec == 0:
                        b.act(t1[:, t0:t0 + n], ps_[:, :n], AF.Identity, [pk_, "pd"], [("t1", t0)], bias=pd[:, cc:cc + 1])
                    elif sec == 1:
                        b.act(t2[:, t0:t0 + n], ps_[:, :n], AF.Sigmoid, [pk_, "pd"], [("t2", t0)], bias=pd[:, 4 + cc:5 + cc])
                        b.tt(cu[:, t0:t0 + n], t1[:, t0:t0 + n], t2[:, t0:t0 + n], ALU.mult, [("t1", t0), ("t2", t0)], ["cu"])
                    else:
                        b.act(dg[:, cc, t0 - HALO:t0 - HALO + n], ps_[:, :n], AF.Silu, [pk_, "pd"], [("dg", cc)], bias=pd[:, 8 + cc:9 + cc])
        b.ts(cu[:, 0:HALO], cu[:, 0:HALO], pd[:, 24:25], None, ALU.mult, None, ["cu", "pd"], ["cu"])
        ck = ("cv", cc)
        b.ts(cv[:, cc, :], cu[:, 2:2 + NT], dww[:, cc, 0:1], pd[:, 12 + cc:13 + cc], ALU.mult, ALU.add, ["cu", "dww", "pd"], [ck])
        for j in range(1, 31):
            b.stt(cv[:, cc, :], cu[:, 2 + j:2 + j + NT], dww[:, cc, j:j + 1], cv[:, cc, :], ALU.mult, ALU.add, ["cu", "dww", ck], [ck])
    for tt_ in range(2):
        sl = slice(tt_ * 512, (tt_ + 1) * 512)
        ps1, k1 = b.psum()
        for cc in range(4):
            b.mm(ps1[:, :], ones[:, 0:128], cv[:, cc, sl], cc == 0, cc == 3, ["c_ones", ("cv", cc)], [k1])
        b.act(mean[:, sl], ps1[:, :], AF.Copy, [k1], ["mean"], scale=1.0 / 512)
        ps2, k2 = b.psum()
        for cc in range(4):
            b.act(sq[:, sl], cv[:, cc, sl], AF.Square, [("cv", cc)], ["sq"])
            b.mm(ps2[:, :], ones[:, 0:128], sq[:, sl], cc == 0, cc == 3, ["c_ones", "sq"], [k2])
        b.act(rstd[:, sl], ps2[:, :], AF.Copy, [k2], ["rstd"], scale=1.0 / 512)
    b.tt(sq[:], mean[:], mean[:], ALU.mult, ["mean"], ["sq"])
    b.tt(rstd[:], rstd[:], sq[:], ALU.subtract, ["rstd", "sq"], ["rstd"])
    b.act(rstd[:], rstd[:], AF.Sqrt, ["rstd"], ["rstd"], bias=LN_EPS)
    S.op("dve", lambda e: e.reciprocal(out=rstd[:], in_=rstd[:]), ["rstd"], ["rstd"])
    for cc in range(4):
        b.tt(sq[:], cv[:, cc, :], mean[:], ALU.subtract, [("cv", cc), "mean"], ["sq"])
        b.tt(sq[:], sq[:], rstd[:], ALU.mult, ["sq", "rstd"], ["sq"], eng="pool")
        b.act(sq[:], sq[:], AF.Silu, ["sq", "pd"], ["sq"], bias=pd[:, 20 + cc:21 + cc], scale=pd[:, 16 + cc:17 + cc])
        b.tt(ygd[:, cc, :], sq[:], dg[:, cc, :], ALU.mult, ["sq", ("dg", cc)], ["ygd"])
    S.barrier()
    stD.close()

    stM = contextlib.ExitStack()
    ygb = b.sb("ygb", [128, 12, NT], BF16, st=stM)
    S.dma("pool", ygb[:], ygT.rearrange("(c p) t -> p c t", p=128), writes=["ygb"])
    wmb = [b.sb("wmb%d" % i, [128, KCH, 512], BF16, st=stM) for i in range(3)]
    wbb = [b.sb("wbb%d" % i, [128, 4, 512], BF16, st=stM) for i in range(3)]
    macc = b.sb("macc", [128, 4, NT], F32, st=stM)
    mg = [b.sb("mg%d" % i, [128, 512], F32, st=stM) for i in range(2)]
    tmp = [b.sb("tmp%d" % i, [128, 512], F32, st=stM) for i in range(2)]
    it = 0
    for dcg in range(4):
        for n in range(4):
            wi = (dcg * 4 + n) % 3
            wm_, wmk = wmb[wi], ("wmb", wi)
            wb_, wbk = wbb[wi], ("wbb", wi)
            c0 = n * 2048 + dcg * 512
            S.dma("pool", wm_[:], w_ap(wm_d[:, c0:c0 + 512]), writes=[wmk])
            S.dma("pool", wb_[:], wbr_d[n * 512:(n + 1) * 512, dcg * 512:(dcg + 1) * 512].rearrange("(c p) d -> p c d", p=128), writes=[wbk])
            for j in range(4):
                psm2 = [b.psum() for _ in range(2)]
                for k in range(KCH):
                    for tt_ in range(2):
                        b.mm(psm2[tt_][0][:, :], wm_[:, k, j * 128:(j + 1) * 128], xb[:, k, HALO + tt_ * 512:HALO + (tt_ + 1) * 512],
                             k == 0, k == KCH - 1, [wmk, "xb"], [psm2[tt_][1]])
                psb2 = [b.psum() for _ in range(2)]
                for c4 in range(4):
                    for tt_ in range(2):
                        sl = slice(tt_ * 512, (tt_ + 1) * 512)
                        rhs = ygb[:, n * 4 + c4, sl] if n < 3 else ygd[:, c4, sl]
                        b.mm(psb2[tt_][0][:, :], wb_[:, c4, j * 128:(j + 1) * 128], rhs, c4 == 0, c4 == 3, [wbk, "ygb", "ygd"], [psb2[tt_][1]])
                for tt_ in range(2):
                    sl = slice(tt_ * 512, (tt_ + 1) * 512)
                    psm, km = psm2[tt_]
                    psb, kb = psb2[tt_]
                    m_ = mg[it % 2]; mk = ("mg", it % 2)
                    col = n * 16 + dcg * 4 + j
                    b.act(m_[:], psm[:, :], AF.Sigmoid, [km, "pm"], [mk], bias=pm[:, col:col + 1])
                    ak = ("macc", j, tt_)
                    if n == 0:
                        b.tt(macc[:, j, sl], m_[:], psb[:, :], ALU.mult, [mk, kb], [ak])
                    else:
                        t_ = tmp[it % 2]; tk = ("tmp", it % 2)
                        b.tt(t_[:], m_[:], psb[:, :], ALU.mult, [mk, kb], [tk])
                        b.tt(macc[:, j, sl], macc[:, j, sl], t_[:], ALU.add, [ak, tk], [ak])
                    it += 1
                    if n == 3:
                        b.cp(mixT[:, dcg * 4 + j, sl], macc[:, j, sl], [ak], ["mixT"], eng="act")
    S.barrier()
    stM.close()
    stDM.close()

    stO = contextlib.ExitStack()
    wob = b.sb("wob", [128, KCH, D_MODEL], BF16, st=stO)
    for eg in range(4):
        S.dma("pool", wob[:, :, eg * 512:(eg + 1) * 512], w_ap(wout_d[:, eg * 512:(eg + 1) * 512]), writes=[("wob", eg)])
    lng = b.sb("lng", [128, D_MODEL], F32, st=stO); lnb = b.sb("lnb", [128, D_MODEL], F32, st=stO)
    b.load(lng[:], lng_d, "lng"); b.load(lnb[:], lnb_d, "lnb")
    xt = [b.sb("xt%d" % i, [128, D_MODEL], F32, st=stO) for i in range(2)]
    z = [b.sb("z%d" % i, [128, D_MODEL], F32, st=stO) for i in range(2)]
    bst = b.sb("bst", [128, 4, 6], F32, st=stO)
    mv = b.sb("mv", [128, 4], F32, st=stO)
    outk = []
    for tt_ in range(NT // 128):
        x_ = xt[tt_ % 2]; xk = ("xt", tt_ % 2)
        z_ = z[tt_ % 2]; zk = ("z", tt_ % 2)
        b.load(x_[:], xown[tt_ * 128:(tt_ + 1) * 128, :], xk)
        pso = [b.psum() for _ in range(4)]
        for dc in range(KCH):
            for eg in range(4):
                b.mm(pso[eg][0][:, :], mixT[:, dc, tt_ * 128:(tt_ + 1) * 128], wob[:, dc, eg * 512:(eg + 1) * 512], dc == 0, dc == KCH - 1,
                     ["mixT", ("wob", eg)], [pso[eg][1]])
        for eg in range(4):
            es = slice(eg * 512, (eg + 1) * 512)
            ps, pk = pso[eg]
            b.stt(z_[:, es], x_[:, es], ALPHA, ps[:, :], ALU.mult, ALU.add, [xk, pk], [(zk, eg)])
            S.op("dve", lambda e, eg=eg, z_=z_, es=es: e.bn_stats(out=bst[:, eg, :], in_=z_[:, es]), [(zk, eg)], [("bst", eg)])
        S.op("dve", lambda e: e.bn_aggr(out=mv[:, 0:2], in_=bst[:].rearrange("p a b -> p (a b)")), [("bst", eg) for eg in range(4)], ["mv"])
        b.act(mv[:, 2:3], mv[:, 1:2], AF.Sqrt, ["mv"], ["mv2"], bias=LN_EPS)
        S.op("dve", lambda e: e.reciprocal(out=mv[:, 2:3], in_=mv[:, 2:3]), ["mv2"], ["mv2"])
        b.stt(mv[:, 3:4], mv[:, 0:1], -1.0, mv[:, 2:3], ALU.mult, ALU.mult, ["mv", "mv2"], ["mv3"])
        b.act(z_[:], z_[:], AF.Identity, [(zk, eg) for eg in range(4)] + ["mv2", "mv3"], [zk], bias=mv[:, 3:4], scale=mv[:, 2:3])
        b.tt(z_[:], z_[:], lng[:], ALU.mult, [zk, "lng"], [zk])
        b.tt(z_[:], z_[:], lnb[:], ALU.add, [zk, "lnb"], [zk] + [(zk, eg) for eg in range(4)], eng="pool")
        S.dma("sp", xo_d[tt_ * 128:(tt_ + 1) * 128, :], z_[:], reads=[zk], writes=[zk, ("xo", tt_)] + [(zk, eg) for eg in range(4)])
        outk.append(("xo", tt_))
    nc = b.finish(outk, extra=[stO])
    return nc, b.outs


def host_p2(inp, l, x_tok, ygT_full):
    w_in = inp["w_in"][l]; b_in = inp["b_in"][l]
    D0 = 1024 + 5120 + 2176
    M0 = D0 + 1536
    wm = np.ascontiguousarray(w_in[:, M0:M0 + 8192])
    wd = np.ascontiguousarray(w_in[:, D0:D0 + 1536])
    wbr = np.ascontiguousarray(inp["w_br"][l].reshape(2048, 2048))
    wout = np.ascontiguousarray(inp["w_out"][l])
    pm = np.ascontiguousarray(b_in[M0:M0 + 8192].reshape(64, 128).T)
    lng = np.ascontiguousarray(np.broadcast_to(inp["ln_g"][l][None, :], (128, 2048)))
    lnb = np.ascontiguousarray(np.broadcast_to(inp["ln_b"][l][None, :], (128, 2048)))
    dww = np.ascontiguousarray(inp["conf_dw_w"][l].T.reshape(4, 128, 31).transpose(1, 0, 2).reshape(128, 124))
    xT = x_tok.T
    maps = []
    for c in range(8):
        pd = np.zeros((128, 64), np.float32)
        bd = b_in[D0:D0 + 1536]
        for cc in range(4):
            pd[:, cc] = bd[cc * 128:(cc + 1) * 128]
            pd[:, 4 + cc] = bd[512 + cc * 128:512 + (cc + 1) * 128]
            pd[:, 8 + cc] = bd[1024 + cc * 128:1024 + (cc + 1) * 128]
            pd[:, 12 + cc] = inp["conf_dw_b"][l][cc * 128:(cc + 1) * 128]
            pd[:, 16 + cc] = inp["conf_ln_g"][l][cc * 128:(cc + 1) * 128]
            pd[:, 20 + cc] = inp["conf_ln_b"][l][cc * 128:(cc + 1) * 128]
        pd[:, 24] = 0.0 if c == 0 else 1.0
        t0 = c * NT
        xTh = np.zeros((2048, NT + HALO), np.float32)
        if c == 0:
            xTh[:, HALO:] = xT[:, 0:NT]
        else:
            xTh[:] = xT[:, t0 - HALO:t0 + NT]
        maps.append({"xTh": xTh, "xown": np.ascontiguousarray(x_tok[t0:t0 + NT]), "ygT": np.ascontiguousarray(ygT_full[:, t0:t0 + NT]),
                     "wm": wm, "wd": wd, "wbr": wbr, "wout": wout, "pm": pm, "pd": pd, "dww": dww, "lng": lng, "lnb": lnb})
    return maps


CORES = list(range(8))


def kernel(**inputs):
    inp = {k: np.asarray(v) for k, v in inputs.items()}
    x = np.ascontiguousarray(inp["x"][0], dtype=np.float32)
    for l in range(2):
        xT = np.ascontiguousarray(x.T)
        nc, _ = build_p1()
        r1 = run_bass_kernel_spmd(nc, host_p1(inp, l, xT), core_ids=CORES)
        ygA = np.concatenate([r["ygA"] for r in r1.results], axis=0)
        ygB = gather_p1b(r1.results)
        ygC = np.concatenate([r["ygC"] for r in r1.results], axis=0)
        ygT = np.ascontiguousarray(np.concatenate([ygA, ygB, ygC], axis=0))
        nc, _ = build_p2()
        r2 = run_bass_kernel_spmd(nc, host_p2(inp, l, x, ygT), core_ids=CORES)
        x = np.ascontiguousarray(np.concatenate([r["xo"] for r in r2.results], axis=0))
    return x[None].astype(np.float32)
```

```python
import contextlib
import numpy as np
import concourse.bass as bass
import concourse.mybir as mybir

F32 = mybir.dt.float32
BF16 = mybir.dt.bfloat16
AF = mybir.ActivationFunctionType
ALU = mybir.AluOpType
AX = mybir.AxisListType


class Sched:
    ENGS = ("pe", "dve", "act", "pool", "sp")

    def __init__(self, nc, stack, n_lanes=24, n_sw_lanes=16):
        self.nc = nc
        self.h = {"pe": nc.tensor, "dve": nc.vector, "act": nc.scalar,
                  "pool": nc.gpsimd, "sp": nc.sync}
        self.sem = {e: stack.enter_context(nc.semaphore("s_" + e)) for e in self.ENGS}
        self.cnt = {e: 0 for e in self.ENGS}
        self.n_hw = n_lanes
        self.lanes = [stack.enter_context(nc.semaphore("l%d" % i)) for i in range(n_lanes)]
        self.lanes += [stack.enter_context(nc.semaphore("w%d" % i)) for i in range(n_sw_lanes)]
        self.lane_tot = [0] * (n_lanes + n_sw_lanes)
        self.lane_next = 0
        self.sw_next = 0
        self.prog = {e: [] for e in self.ENGS}
        self.seen = {e: {} for e in self.ENGS}
        self.lastw = {}
        self.readers = {}
        self.semobj = {}
        self.unread = set()

    def _need(self, e, tok, waits):
        if tok is None:
            return
        k, v = tok
        if e == "pe" and k == ("e", "pe"):
            return
        if self.seen[e].get(k, 0) >= v:
            return
        self.seen[e][k] = v
        waits.append((k, v))

    def _semof(self, k):
        return self.sem[k[1]] if k[0] == "e" else self.lanes[k[1]]

    def _deps(self, e, reads, writes):
        waits = []
        for r in reads:
            self._need(e, self.lastw.get(r), waits)
        for w in writes:
            self._need(e, self.lastw.get(w), waits)
            for t in self.readers.get(w, ()):
                self._need(e, t, waits)
        return waits

    def _commit(self, tok, reads, writes):
        for r in reads:
            self.unread.discard(r)
            self.readers.setdefault(r, []).append(tok)
        for w in writes:
            self.lastw[w] = tok
            self.readers[w] = []

    def op(self, e, fn, reads=(), writes=()):
        waits = self._deps(e, reads, writes)
        self.cnt[e] += 1
        tok = (("e", e), self.cnt[e])
        self.prog[e].append((waits, fn, (self.sem[e], 1)))
        self._commit(tok, reads, writes)
        return tok

    def take_lane(self, q):
        if q == "pool":
            lane = self.n_hw + self.sw_next
            self.sw_next = (self.sw_next + 1) % (len(self.lanes) - self.n_hw)
        else:
            lane = self.lane_next
            self.lane_next = (self.lane_next + 1) % self.n_hw
        return lane

    def dma(self, q, out, in_, reads=(), writes=(), **kw):
        lane = self.take_lane(q)
        waits = self._deps(q, reads, writes)
        self._need(q, (("l", lane), self.lane_tot[lane]) if self.lane_tot[lane] else None, waits)
        self.lane_tot[lane] += 16
        tok = (("l", lane), self.lane_tot[lane])

        def fn(eng, out=out, in_=in_, kw=kw):
            return eng.dma_start(out=out, in_=in_, **kw)
        self.prog[q].append((waits, fn, (self.lanes[lane], 16)))
        self._commit(tok, reads, writes)
        return tok

    def wait_all(self, e, toks):
        waits = []
        for t in toks:
            self._need(e, t, waits)
        self.prog[e].append((waits, None, None))

    def barrier(self):
        toks = [(("e", e), self.cnt[e]) for e in self.ENGS if self.cnt[e]]
        toks += [(("l", i), t) for i, t in enumerate(self.lane_tot) if t]
        for e in self.ENGS:
            self.wait_all(e, toks)

    def emit(self, block):
        def mk(e):
            def body(eng):
                for waits, fn, inc in self.prog[e]:
                    for k, v in waits:
                        eng.wait_ge(self._semof(k), v)
                    if fn is not None:
                        ins = fn(eng)
                        ins.then_inc(inc[0], inc[1])
            return body
        block.tensor(mk("pe"))
        block.vector(mk("dve"))
        block.scalar(mk("act"))
        block.gpsimd(mk("pool"))
        block.sync(mk("sp"))


import contextlib
import numpy as np
import concourse.bass as bass
import concourse.mybir as mybir
from concourse.bass_utils import run_bass_kernel_spmd

F32 = mybir.dt.float32
BF16 = mybir.dt.bfloat16
AF = mybir.ActivationFunctionType
ALU = mybir.AluOpType

D_MODEL = 2048
SEQ = 8192
KCH = 16


class B:
    def __init__(self):
        self.nc = bass.Bass("TRN2", target_bir_lowering=False, num_devices=8)
        self.st = contextlib.ExitStack()
        self.S = Sched(self.nc, self.st)
        self.ps = [self.st.enter_context(self.nc.psum_tensor("ps%d" % i, [128, 512], F32)) for i in range(8)]
        self.ps_i = 0
        self.uid = 0
        self.outs = []
        self.pfx = ""

    def din(self, name, shape, dt=F32):
        return self.nc.dram_tensor(name, list(shape), dt, kind="ExternalInput").ap()

    def dout(self, name, shape, dt=F32):
        self.outs.append(name)
        return self.nc.dram_tensor(name, list(shape), dt, kind="ExternalOutput").ap()

    def sb(self, name, shape, dt=F32, st=None):
        return (st or self.st).enter_context(self.nc.sbuf_tensor("sb_" + self.pfx + name, list(shape), dt))

    def psum(self):
        i = self.ps_i
        self.ps_i = (i + 1) % 8
        if ("ps", i) in self.S.unread:
            raise RuntimeError("PSUM bank %d handed out again before its previous contents were read" % i)
        self.S.unread.add(("ps", i))
        return self.ps[i], ("ps", i)

    def key(self, p="t"):
        self.uid += 1
        return (p, self.uid)

    def finish(self, out_keys, extra=()):
        S = self.S
        S.wait_all("sp", [S.lastw[k] for k in out_keys])
        with self.nc.Block() as block:
            S.emit(block)
        for e in extra:
            e.close()
        self.st.close()
        return self.nc

    def mm(self, out, lhsT, rhs, start, stop, reads, writes):
        return self.S.op("pe", lambda e: e.matmul(out, lhsT=lhsT, rhs=rhs, start=start, stop=stop), reads, writes)

    def tr(self, out, in_, ident, reads, writes):
        return self.S.op("pe", lambda e: e.transpose(out=out, in_=in_, identity=ident), reads, writes)

    def act(self, out, in_, func, reads, writes, bias=None, scale=None, eng="act"):
        kw = {}
        if bias is not None:
            kw["bias"] = bias
        if scale is not None:
            kw["scale"] = scale
        return self.S.op("act", lambda e: e.activation(out=out, in_=in_, func=func, **kw), reads, writes)

    def tt(self, out, in0, in1, op, reads, writes, eng="dve"):
        return self.S.op(eng, lambda e: e.tensor_tensor(out=out, in0=in0, in1=in1, op=op), reads, writes)

    def ts(self, out, in0, s1, s2, op0, op1, reads, writes, eng="dve"):
        if op1 is None:
            return self.S.op(eng, lambda e: e.tensor_scalar(out=out, in0=in0, scalar1=s1, scalar2=None, op0=op0), reads, writes)
        return self.S.op(eng, lambda e: e.tensor_scalar(out=out, in0=in0, scalar1=s1, scalar2=s2, op0=op0, op1=op1), reads, writes)

    def stt(self, out, in0, scalar, in1, op0, op1, reads, writes):
        return self.S.op("dve", lambda e: e.scalar_tensor_tensor(out=out, in0=in0, scalar=scalar, in1=in1, op0=op0, op1=op1), reads, writes)

    def cp(self, out, in_, reads, writes, eng="dve"):
        if eng == "act":
            return self.S.op("act", lambda e: e.copy(out=out, in_=in_), reads, writes)
        return self.S.op(eng, lambda e: e.tensor_copy(out=out, in_=in_), reads, writes)

    def memset(self, ap, val, writes, eng="pool"):
        return self.S.op(eng, lambda e: e.memset(ap, val), (), writes)

    def load(self, sb_ap, dram_ap, key, q="sp", **kw):
        return self.S.dma(q, sb_ap, dram_ap, writes=[key], **kw)

    def _lane_op(self, q, fn, reads, writes):
        S = self.S
        lane = S.take_lane(q)
        waits = S._deps(q, reads, writes)
        S._need(q, (("l", lane), S.lane_tot[lane]) if S.lane_tot[lane] else None, waits)
        S.lane_tot[lane] += 16
        tok = (("l", lane), S.lane_tot[lane])
        S.prog[q].append((waits, fn, (S.lanes[lane], 16)))
        S._commit(tok, reads, writes)
        return tok

    def coll(self, kind, src, dst, reads, writes):
        return self._lane_op("pool", lambda e: e.collective_compute(kind, ALU.bypass, replica_groups=[list(range(8))],
                                                                    ins=[src], outs=[dst]), reads, writes)

    def gather(self, out, in_, idx, reads, writes):
        return self._lane_op("pool", lambda e: e.indirect_dma_start(out=out, out_offset=None, in_=in_,
                                                                    in_offset=bass.IndirectOffsetOnAxis(ap=idx, axis=0)), reads, writes)

    def ident(self, n=128):
        ones = self.sb("c_ones", [128, 512], F32)
        ident = self.sb("c_ident", [128, 128], F32)
        self.memset(ones[:], 1.0, ["c_ones"])
        self.S.op("pool", lambda e: e.affine_select(out=ident[:], in_=ones[:, 0:128], pattern=[[1, 128]],
                                                    compare_op=ALU.is_equal, fill=0.0, base=0, channel_multiplier=-1),
                  ["c_ones"], ["c_ident"])
        return ones, ident


def xT_tile_ap(xT, t0, n):
    return xT[:, t0:t0 + n].rearrange("(k p) t -> p k t", p=128)


def x_tile(xt, i):
    return xt[i].rearrange("p (k t) -> p k t", k=KCH)


def tile_xT(xT):
    n = xT.shape[1] // 512
    return np.ascontiguousarray(xT.reshape(KCH, 128, n, 512).transpose(2, 1, 0, 3).reshape(n, 128, KCH * 512))


def w_ap(w):
    return w.rearrange("(k p) c -> p k c", p=128)


def emit_p1a(b, st, xT, wa_d, pa_d, gw_d, yg_d, T=SEQ, SEG=512):
    S = b.S
    b.pfx = "a_"
    nseg = T // SEG

    wa = b.sb("wa", [128, KCH, 128], BF16, st=st)
    pa = b.sb("pa", [64, 16], st=st)
    gw = b.sb("gw", [64, 128], st=st)
    c1 = b.sb("c1", [64, 4], st=st)
    SUP = 4
    xb = [b.sb("xb%d" % i, [128, KCH, SUP * SEG], BF16, st=st) for i in range(2)]
    sgs = b.sb("sgs", [64, SUP * SEG], st=st)
    axb = b.sb("axb", [64, 4 * SEG + 3], st=st)
    u = b.sb("u", [64, SEG], st=st)
    gr = b.sb("gr", [64, SEG], st=st)
    gi = b.sb("gi", [64, SEG], st=st)
    at = b.sb("at", [64, SEG], st=st)
    a2 = b.sb("a2", [64, SEG], st=st)
    bt = b.sb("bt", [64, SEG], st=st)
    hb = [b.sb("hb%d" % i, [64, SEG], st=st) for i in range(2)]
    sg = b.sb("sg", [64, SEG], st=st)
    yo = [b.sb("yo%d" % i, [64, SEG], st=st) for i in range(2)]

    S.dma("pool", wa[:], w_ap(wa_d), writes=["wa"])
    b.load(pa[:], pa_d, "pa")
    b.load(gw[:], gw_d, "gw")
    b.memset(axb[:, 0:3], 0.0, ["axb"])
    b.act(c1[:, 2:3], pa[:, 9:10], AF.Exp, ["pa"], ["c1"], scale=-1.0)
    b.act(c1[:, 3:4], c1[:, 2:3], AF.Ln, ["c1"], ["c1"], bias=1.0)
    b.ts(c1[:, 0:1], c1[:, 3:4], -8.0, None, ALU.mult, None, ["c1"], ["c1"])
    b.ts(c1[:, 1:2], c1[:, 3:4], -16.0, None, ALU.mult, None, ["c1"], ["c1"])

    for s in range(nseg):
        sup, sub = s // SUP, s % SUP
        xs = xb[sup % 2]
        xk = ("xb", sup % 2)
        if sub == 0:
            nsub = min(SUP, nseg - s)
            for q in range(nsub):
                S.dma("pool", xs[:, :, q * SEG:(q + 1) * SEG], x_tile(xT, s + q), writes=[(xk, q)])
            pa_ps = [b.psum() for _ in range(nsub)]
            for k in range(KCH):
                for q in range(nsub):
                    b.mm(pa_ps[q][0][0:64, :SEG], wa[:, k, 0:64], xs[:, k, q * SEG:(q + 1) * SEG], k == 0, k == KCH - 1, ["wa", (xk, q)], [pa_ps[q][1]])
            for q in range(nsub):
                b.act(axb[:, 3 + q * SEG:3 + (q + 1) * SEG], pa_ps[q][0][0:64, :SEG], AF.Identity, [pa_ps[q][1], "pa"], ["axb"], bias=pa[:, 0:1])
            pg_ps = [b.psum() for _ in range(nsub)]
            for k in range(KCH):
                for q in range(nsub):
                    b.mm(pg_ps[q][0][0:64, :SEG], wa[:, k, 64:128], xs[:, k, q * SEG:(q + 1) * SEG], k == 0, k == KCH - 1, ["wa", (xk, q)], [pg_ps[q][1]])
            for q in range(nsub):
                b.act(sgs[:, q * SEG:(q + 1) * SEG], pg_ps[q][0][0:64, :SEG], AF.Silu, [pg_ps[q][1], "pa"], [("sgs", q)], bias=pa[:, 1:2])
        o0 = sub * SEG
        b.ts(u[:], axb[:, o0:o0 + SEG], pa[:, 3:4], pa[:, 2:3], ALU.mult, ALU.add, ["axb", "pa"], ["u"])
        for j in range(1, 4):
            b.stt(u[:], axb[:, o0 + j:o0 + j + SEG], pa[:, 3 + j:4 + j], u[:], ALU.mult, ALU.add, ["axb", "pa", "u"], ["u"])
        if sub == SUP - 1 or s == nseg - 1:
            b.cp(axb[:, 0:3], axb[:, (sub + 1) * SEG:(sub + 1) * SEG + 3], ["axb", "u"], ["axb"], eng="pool")
        ps3, pk3 = b.psum()
        b.mm(ps3[0:64, :SEG], gw[:, 0:64], u[:], True, True, ["gw", "u"], [pk3])
        b.act(gr[:], ps3[0:64, :SEG], AF.Sigmoid, [pk3, "pa"], ["gr"], bias=pa[:, 7:8])
        ps4, pk4 = b.psum()
        b.mm(ps4[0:64, :SEG], gw[:, 64:128], u[:], True, True, ["gw", "u"], [pk4])
        b.act(gi[:], ps4[0:64, :SEG], AF.Sigmoid, [pk4, "pa"], ["gi"], bias=pa[:, 8:9])
        b.act(at[:], gr[:], AF.Exp, ["gr", "c1"], ["at"], scale=c1[:, 0:1])
        b.act(a2[:], gr[:], AF.Exp, ["gr", "c1"], ["a2"], scale=c1[:, 1:2])
        b.ts(a2[:], a2[:], -1.0, 1.0, ALU.mult, ALU.add, ["a2"], ["a2"])
        b.act(a2[:], a2[:], AF.Sqrt, ["a2"], ["a2"])
        b.tt(bt[:], gi[:], u[:], ALU.mult, ["gi", "u"], ["bt"])
        b.tt(bt[:], bt[:], a2[:], ALU.mult, ["bt", "a2"], ["bt"])
        hcur = hb[s % 2]
        hprev = hb[(s + 1) % 2]
        init = 0.0 if s == 0 else hprev[:, SEG - 1:SEG]
        S.op("dve", lambda e, hcur=hcur, init=init: e.tensor_tensor_scan(out=hcur[:], data0=at[:], data1=bt[:], initial=init,
                                                                        op0=ALU.mult, op1=ALU.add),
             ["at", "bt", ("hb", (s + 1) % 2)], [("hb", s % 2)])
        yt = yo[s % 2]
        b.tt(yt[:], hcur[:], sgs[:, sub * SEG:(sub + 1) * SEG], ALU.mult, [("hb", s % 2), ("sgs", sub)], [("yo", s % 2)], eng="pool")
        S.dma("sp", yg_d[:, s * SEG:(s + 1) * SEG], yt[:], reads=[("yo", s % 2)], writes=[("ygd", s)])
    return [("ygd", s) for s in range(nseg)]


def build_p1a(T=SEQ, SEG=512):
    b = B()
    st = contextlib.ExitStack()
    keys = emit_p1a(b, st, b.din("xTt", [T // 512, 128, KCH * 512]), b.din("wa", [D_MODEL, 128]), b.din("pa", [64, 16]),
                    b.din("gw", [64, 128]), b.dout("ygA", [64, T]), T, SEG)
    return b.finish(keys, extra=[st]), b.outs


def host_p1a(inp, l, xT):
    sp = np.cumsum([0, 512, 512])
    w_in = inp["w_in"][l]
    b_in = inp["b_in"][l]
    maps = []
    for c in range(8):
        cs = slice(64 * c, 64 * c + 64)
        wa = np.concatenate([w_in[:, 0:512][:, cs], w_in[:, 512:1024][:, cs]], axis=1)
        pa = np.zeros((64, 16), np.float32)
        pa[:, 0] = b_in[0:512][cs]
        pa[:, 1] = b_in[512:1024][cs]
        pa[:, 2] = inp["lru_conv_b"][l][cs]
        pa[:, 3:7] = inp["lru_conv_w"][l][:, cs].T
        pa[:, 7] = inp["lru_gate_a_b"][l][cs]
        pa[:, 8] = inp["lru_gate_x_b"][l][cs]
        pa[:, 9] = inp["lru_lambda"][l][cs]
        gw = np.concatenate([inp["lru_gate_a_w"][l][c], inp["lru_gate_x_w"][l][c]], axis=1)
        maps.append({"xTt": xT, "wa": np.ascontiguousarray(wa), "pa": pa, "gw": np.ascontiguousarray(gw)})
    return maps


NQ = 4096
NKV = 6144
HAL = 2048
DILS = (1, 4, 16)
N_BUCKETS = 32
MAX_DISTANCE = 2048


def emit_p1b(b, st, ones, ident, xTh, wq_d, wk_d, wv_d, wg_d, pb_d, bias_d, yg_d):
    S = b.S
    b.pfx = "b_"
    onesb = b.sb("onesb", [128, 128], BF16, st=st)
    b.cp(onesb[:], ones[:, 0:128], ["c_ones"], ["onesb"])
    pb = b.sb("pb", [128, 16], st=st); b.load(pb[:], pb_d, "pb")
    eb = b.sb("eb", [128, 3, 2, 128], st=st); ebf = b.sb("ebf", [128, 3, 2, 128], st=st)
    b.load(eb[:], bias_d.rearrange("p (g k q) -> p g k q", g=3, k=2), "eb")
    b.act(eb[:], eb[:], AF.Exp, ["eb"], ["eb"])
    b.cp(ebf[:], eb[:], ["eb"], ["ebf"])
    for g in range(3):
        b.ts(ebf[:, g, 0, :], ebf[:, g, 0, :], pb[:, 10:11], None, ALU.mult, None, ["ebf", "pb"], ["ebf"])
    wq = b.sb("wq", [128, KCH, 384], BF16, st=st); wk = b.sb("wk", [128, KCH, 384], BF16, st=st); wv = b.sb("wv", [128, KCH, 384], BF16, st=st)
    wg = b.sb("wg", [128, KCH, 128], BF16, st=st)
    S.dma("pool", wk[:], w_ap(wk_d), writes=["wk"]); S.dma("pool", wv[:], w_ap(wv_d), writes=["wv"])
    S.dma("pool", wq[:], w_ap(wq_d), writes=["wq"]); S.dma("pool", wg[:], w_ap(wg_d), writes=["wg"])
    xb = [b.sb("xb%d" % i, [128, KCH, 512], BF16, st=st) for i in range(2)]
    KT = b.sb("KT", [128, NKV], BF16, st=st)
    VT = b.sb("VT", [128, NKV], F32, st=st)
    QT = b.sb("QT", [128, NQ], BF16, st=st)
    bg = b.sb("bg", [128, NQ], F32, st=st)
    ND = b.sb("ND", [128, 2, NQ], F32, st=st)
    esb = [b.sb("esb%d" % i, [128, 256], F32, st=st) for i in range(2)]
    PT = [b.sb("PT%d" % i, [128, 2, 128], BF16, st=st) for i in range(2)]
    Vt = [b.sb("Vt%d" % i, [128, 2, 128], BF16, st=st) for i in range(2)]
    scale = 128.0 ** -0.5
    xi = 0
    bi = 0
    for g, dil in enumerate(DILS):
        start = HAL - max(512, 128 * dil)
        tl_all = list(range(start, NKV, 512))
        for pi in range(0, len(tl_all), 1):
            pair = tl_all[pi:pi + 1]
            xss = []
            for t0 in pair:
                xs = xb[xi % 2]; xk = ("xb", xi % 2); xi += 1
                S.dma("pool", xs[:], x_tile(xTh, t0 // 512), writes=[xk])
                xss.append((xs, xk, t0))
            secs = [("k", wk, "wk"), ("v", wv, "wv"), ("q", wq, "wq")] + ([("g", wg, "wg")] if g == 0 else [])
            for nm, wt, wkey in secs:
                xsel = xss if nm in ("k", "v") else [t for t in xss if t[2] >= HAL]
                if not xsel:
                    continue
                pss = [b.psum() for _ in xsel]
                for k in range(KCH):
                    for (ps_, pk_), (xs, xk, t0) in zip(pss, xsel):
                        lhs = wt[:, k, :] if nm == "g" else wt[:, k, g * 128:(g + 1) * 128]
                        b.mm(ps_[:, :], lhs, xs[:, k, :], k == 0, k == KCH - 1, [wkey, xk], [pk_])
                for (ps_, pk_), (xs, xk, t0) in zip(pss, xsel):
                    if nm == "k":
                        b.act(KT[:, t0:t0 + 512], ps_[:, :], AF.Identity, [pk_, "pb"], [("KT", t0)], bias=pb[:, 3 + g:4 + g])
                    elif nm == "v":
                        b.ts(VT[:, t0:t0 + 512], ps_[:, :], pb[:, 6 + g:7 + g], None, ALU.add, None, [pk_, "pb"], [("VT", t0)])
                    elif nm == "q":
                        b.act(QT[:, t0 - HAL:t0 - HAL + 512], ps_[:, :], AF.Identity, [pk_, "pb"], [("QT", t0 - HAL)], bias=pb[:, g:g + 1])
                    else:
                        b.act(bg[:, t0 - HAL:t0 - HAL + 512], ps_[:, :], AF.Silu, [pk_, "pb"], ["bg"], bias=pb[:, 9:10])
        span = 128 * dil
        allK = [("KT", t) for t in range(start, NKV, 512)]
        allV = [("VT", t) for t in range(start, NKV, 512)]
        allQ = [("QT", t) for t in range(0, NQ, 512)]

        def tiles_of(keys, base, lo, hi):
            return [(keys, t) for t in range(base, 99999, 512) if t < hi and t + 512 > lo]
        for r in range(dil):
            for nl in range(32 // dil):
                qs = r + span * nl
                ssl = lambda a: slice(a, a + 127 * dil + 1, dil)
                qsl = ssl(qs)
                kc_ = ssl(HAL + qs)
                kp_ = ssl(HAL + qs - span)
                qk = [("QT", t) for t in range(0, NQ, 512) if t < qs + span and t + 512 > qs]
                kk_ = [("KT", t) for t in range(start, NKV, 512) if t < HAL + qs + span and t + 512 > HAL + qs - span]
                vk_ = [("VT", t) for t in range(start, NKV, 512) if t < HAL + qs + span and t + 512 > HAL + qs - span]
                psl, kl = b.psum()
                b.mm(psl[:, 0:128], KT[:, kp_], QT[:, qsl], True, True, kk_ + qk, [kl])
                b.mm(psl[:, 128:256], KT[:, kc_], QT[:, qsl], True, True, kk_ + qk, [kl])
                e_ = esb[bi % 2]; ek = ("esb", bi % 2)
                b.act(e_[:], psl[:, 0:256], AF.Exp, [kl], [ek], scale=scale)
                p_ = PT[bi % 2]; pk_ = ("PT", bi % 2)
                ebsel = ebf if nl == 0 else eb
                b.tt(p_[:].rearrange("p a q -> p (a q)"), e_[:], ebsel[:, g].rearrange("p a q -> p (a q)"), ALU.mult, [ek, "eb", "ebf"], [pk_])
                psv, kv = b.psum()
                b.tr(psv[:, 0:128], VT[:, kp_], ident[:], vk_ + ["c_ident"], [kv])
                b.tr(psv[:, 128:256], VT[:, kc_], ident[:], vk_ + ["c_ident"], [kv])
                v_ = Vt[bi % 2]; vk2 = ("Vt", bi % 2)
                b.cp(v_[:].rearrange("p a q -> p (a q)"), psv[:, 0:256], [kv], [vk2], eng="act")
                pso, ko = b.psum()
                b.mm(pso[:, 0:128], v_[:, 0, :], p_[:, 0, :], True, False, [vk2, pk_], [ko])
                b.mm(pso[:, 0:128], v_[:, 1, :], p_[:, 1, :], False, True, [vk2, pk_], [ko])
                b.mm(pso[:, 128:256], onesb[:], p_[:, 0, :], True, False, ["onesb", pk_], [ko])
                b.mm(pso[:, 128:256], onesb[:], p_[:, 1, :], False, True, ["onesb", pk_], [ko])
                src = pso[:, 0:256].rearrange("p (a q) -> p a q", a=2)
                if g == 0:
                    b.cp(ND[:, :, qsl], src, [ko], ["ND"], eng="act")
                else:
                    b.tt(ND[:, :, qsl], ND[:, :, qsl], src, ALU.add, ["ND", ko], ["ND"])
                bi += 1
    S.op("dve", lambda e: e.reciprocal(out=ND[:, 1, :], in_=ND[:, 1, :]), ["ND"], ["ND"])
    b.tt(ND[:, 0, :], ND[:, 0, :], ND[:, 1, :], ALU.mult, ["ND"], ["ND"])
    b.tt(ND[:, 0, :], ND[:, 0, :], bg[:], ALU.mult, ["ND", "bg"], ["ND"], eng="pool")
    S.dma("sp", yg_d, ND[:, 0, :], reads=["ND"], writes=["outB"])
    return ["outB"]


def p1b_dram(b):
    return (b.din("xTht", [NKV // 512, 128, KCH * 512]), b.din("wq", [D_MODEL, 384]), b.din("wk", [D_MODEL, 384]), b.din("wv", [D_MODEL, 384]),
            b.din("wg", [D_MODEL, 128]), b.din("pb", [128, 16]), b.din("biasT", [128, 3 * 2 * 128]), b.dout("ygB", [128, NQ]))


def build_p1b():
    b = B()
    ones, ident = b.ident()
    st = contextlib.ExitStack()
    keys = emit_p1b(b, st, ones, ident, *p1b_dram(b))
    return b.finish(keys, extra=[st]), b.outs


def t5_bucket(dist):
    import math
    max_exact = N_BUCKETS // 2
    large = max_exact + (np.log(np.maximum(dist, 1) / max_exact) / math.log(MAX_DISTANCE / max_exact)
                         * (N_BUCKETS - max_exact)).astype(np.int32)
    large = np.minimum(large, N_BUCKETS - 1)
    return np.where(dist < max_exact, dist, large).astype(np.int32)


def host_p1b(inp, l, xT):
    w_in = inp["w_in"][l]; b_in = inp["b_in"][l]
    Q0 = 1024; K0 = Q0 + 1536; V0 = K0 + 1536; G0 = V0 + 1536
    table = inp["att_rel_bias"]
    ki = np.arange(128)[:, None, None]; kb = np.arange(2)[None, :, None]; qi = np.arange(128)[None, None, :]
    dist = qi + 128 - (kb * 128 + ki)
    valid = (dist >= 0) & (dist <= 128)
    maps = []
    for c in range(8):
        hm, half = c // 2, c % 2
        cols = lambda base: np.concatenate([w_in[:, base + (g * 4 + hm) * 128: base + (g * 4 + hm + 1) * 128] for g in range(3)], axis=1)
        pb = np.zeros((128, 16), np.float32)
        for g in range(3):
            hh = (g * 4 + hm) * 128
            pb[:, g] = b_in[Q0 + hh:Q0 + hh + 128]
            pb[:, 3 + g] = b_in[K0 + hh:K0 + hh + 128]
            pb[:, 6 + g] = b_in[V0 + hh:V0 + hh + 128]
        pb[:, 9] = b_in[G0 + hm * 128:G0 + hm * 128 + 128]
        pb[:, 10] = float(half)
        bT = np.zeros((128, 3, 2, 128), np.float32)
        for g, dil in enumerate(DILS):
            bucket = t5_bucket(np.clip(dist, 0, 128) * dil)
            bT[:, g] = np.where(valid, table[bucket, g * 4 + hm], np.float32(-100.0))
        xh = np.zeros((2048, NKV), np.float32)
        base = half * NQ
        if half == 0:
            xh[:, HAL:] = xT[:, 0:NQ]
        else:
            xh[:] = xT[:, base - HAL:base + NQ]
        maps.append({"xTht": tile_xT(xh), "wq": np.ascontiguousarray(cols(Q0)), "wk": np.ascontiguousarray(cols(K0)),
                     "wv": np.ascontiguousarray(cols(V0)), "wg": np.ascontiguousarray(w_in[:, G0 + hm * 128:G0 + (hm + 1) * 128]),
                     "pb": pb, "biasT": bT.reshape(128, 768)})
    return maps


def gather_p1b(results):
    yg = np.zeros((512, 8192), np.float32)
    for c, r in enumerate(results):
        hm, half = c // 2, c % 2
        yg[hm * 128:(hm + 1) * 128, half * NQ:(half + 1) * NQ] = r["ygB"]
    return yg


SEGC = 1024
NW = 4
GN_EPS = 64e-5
DEC_C = float(np.exp(-0.5))


def emit_p1c(b, st, ones, ident, xT, wc_d, pc_d, wup_d, gnb_d, yg_d, T=SEQ):
    S = b.S
    b.pfx = "c_"
    nseg = T // SEGC
    NCH = SEGC // 128
    pc = b.sb("pc", [64, 32], st=st); b.load(pc[:], pc_d, "pc")
    wup = b.sb("wup", [64, 128], st=st); b.load(wup[:], wup_d, "wup")
    gnb = b.sb("gnb", [128, 192], st=st); b.load(gnb[:], gnb_d, "gnb")
    wc = b.sb("wc", [128, KCH, 384], BF16, st=st)
    S.dma("pool", wc[:], w_ap(wc_d), writes=["wc"])
    pcx = b.sb("pcx", [64, 4], st=st)
    b.ts(pcx[:, 0:1], pc[:, 13:14], -1.0, 1.0, ALU.mult, ALU.add, ["pc"], ["pcx"])
    mask2 = b.sb("mask2", [128, 2, 128], st=st); maskL = b.sb("maskL", [128, 128], st=st); maskc = b.sb("maskc", [64, SEGC], st=st)
    S.op("pool", lambda e: e.affine_select(out=mask2[:, 0, :], in_=ones[:, 0:128], pattern=[[1, 128]], compare_op=ALU.is_gt,
                                           fill=0.0, base=0, channel_multiplier=-1), ["c_ones"], ["mask2"])
    S.op("pool", lambda e: e.affine_select(out=mask2[:, 1, :], in_=ones[:, 0:128], pattern=[[1, 128]], compare_op=ALU.is_ge,
                                           fill=0.0, base=0, channel_multiplier=-1), ["c_ones"], ["mask2"])
    S.op("pool", lambda e: e.affine_select(out=maskL[:], in_=ones[:, 0:128], pattern=[[-1, 128]], compare_op=ALU.is_gt,
                                           fill=0.0, base=0, channel_multiplier=1), ["c_ones"], ["maskL"])
    b.memset(maskc[:], 1.0, ["maskc"])
    b.memset(maskc[:, 0:SEGC - 127:128], 0.0, ["maskc"])

    xb = [b.sb("xb%d" % i, [128, KCH, 512], BF16, st=st) for i in range(2)]
    raw = [b.sb("raw%d" % i, [64, SEGC + 1], st=st) for i in range(5)]
    sh = [b.sb("sh%d" % i, [64, SEGC], st=st) for i in range(5)]
    names = ["lw", "aicl", "kk", "kc", "bb", "cum", "tmp1", "tmp2", "bt", "kt", "bh", "kh", "rkr"]
    A = {n: b.sb(n, [64, SEGC], st=st) for n in names}
    atrt = b.sb("atrt", [64, 2, SEGC], st=st)
    wl = b.sb("wl", [64, NCH], st=st)
    sgc = b.sb("sgc", [64, SEGC], st=st)
    obuf = [b.sb("obuf%d" % i, [64, SEGC], st=st) for i in range(2)]
    ST = [b.sb("ST%d" % i, [64, 64], st=st) for i in range(2)]
    tok4 = [b.sb("tok4_%d" % i, [128, 5, 64], st=st) for i in range(NW)]
    AB = [b.sb("AB%d" % i, [128, 2, 128], st=st) for i in range(NW)]
    AK = [b.sb("AK%d" % i, [128, 2, 128], st=st) for i in range(NW)]
    Mb = [[b.sb("Mb%d_%d" % (q, i), [128, 128], st=st) for i in range(2)] for q in range(NW)]
    MTb = [[b.sb("MTb%d_%d" % (q, i), [128, 128], st=st) for i in range(2)] for q in range(NW)]
    TTb = [[b.sb("TTb%d_%d" % (q, i), [128, 128], st=st) for i in range(2)] for q in range(NW)]
    Xsb = [b.sb("Xsb%d" % q, [128, 64], st=st) for q in range(NW)]
    PQ = [b.sb("PQ%d" % i, [128, 2, 64], st=st) for i in range(NW)]
    GT = [b.sb("GT%d" % q, [64, 64], st=st) for q in range(NW)]; Hs = [b.sb("Hs%d" % q, [64, 64], st=st) for q in range(NW)]
    R2T = [b.sb("R2T%d" % q, [64, 128], st=st) for q in range(NW)]
    bst = [b.sb("bst%d" % q, [128, 6], st=st) for q in range(NW)]; mv = [b.sb("mv%d" % q, [128, 4], st=st) for q in range(NW)]
    bsc = [b.sb("bsc%d" % q, [128, 1], st=st) for q in range(NW)]
    yn = [b.sb("yn%d" % i, [128, 64], st=st) for i in range(NW)]

    b.memset(ST[0][:], 0.0, [("ST", 0)])
    for g in range(5):
        b.memset(raw[g][:, 0:1], 0.0, [("raw", g)])
    xi = 0
    stc = 0
    outk = []
    for s in range(nseg):
        NTL = SEGC // 512

        def load_x(seg):
            nonlocal xi
            out = []
            for tl in range(NTL):
                xs = xb[xi % 2]; xk = ("xb", xi % 2); xi += 1
                S.dma("pool", xs[:], x_tile(xT, seg * NTL + tl), writes=[xk])
                out.append((xs, xk))
            return out
        xss = x_next if s > 0 else load_x(0)
        for g in range(6):
            pss = [b.psum() for _ in range(NTL)]
            for k in range(KCH):
                for tl in range(NTL):
                    b.mm(pss[tl][0][0:64, :], wc[:, k, g * 64:(g + 1) * 64], xss[tl][0][:, k, :], k == 0, k == KCH - 1, ["wc", xss[tl][1]], [pss[tl][1]])
            for tl in range(NTL):
                ps, pk = pss[tl]
                if g < 5:
                    b.act(raw[g][:, 1 + tl * 512:1 + (tl + 1) * 512], ps[0:64, :], AF.Identity, [pk, "pc"], [("raw", g)], bias=pc[:, g:g + 1])
                else:
                    b.act(sgc[:, tl * 512:(tl + 1) * 512], ps[0:64, :], AF.Silu, [pk, "pc"], ["sgc"], bias=pc[:, 16:17])
        x_next = load_x(s + 1) if s + 1 < nseg else None
        for g in range(5):
            b.tt(sh[g][:], raw[g][:, 0:SEGC], raw[g][:, 1:SEGC + 1], ALU.subtract, [("raw", g)], [("sh", g)])
            b.stt(sh[g][:], sh[g][:], pc[:, 5 + g:6 + g], raw[g][:, 1:SEGC + 1], ALU.mult, ALU.add, [("sh", g), "pc", ("raw", g)], [("sh", g)])
            b.cp(raw[g][:, 0:1], raw[g][:, SEGC:SEGC + 1], [("raw", g)], [("raw", g)], eng="pool")
        s_r, s_k, s_v, s_w, s_a = sh
        kr, kk_, kv_, kw_, ka_ = [("sh", g) for g in range(5)]
        b.act(s_w[:], s_w[:], AF.Tanh, [kw_], [kw_])
        for tl in range(2):
            sl = slice(tl * 512, (tl + 1) * 512)
            ps, pk = b.psum()
            b.mm(ps[0:64, :], wup[:, 0:64], s_w[:, sl], True, True, ["wup", kw_], [pk])
            b.act(A["lw"][:, sl], ps[0:64, :], AF.Sigmoid, [pk, "pc"], ["lw"], bias=pc[:, 10:11])
            ps, pk = b.psum()
            b.mm(ps[0:64, :], wup[:, 64:128], s_a[:, sl], True, True, ["wup", ka_], [pk])
            b.act(A["aicl"][:, sl], ps[0:64, :], AF.Sigmoid, [pk, "pc"], ["aicl"], bias=pc[:, 11:12])
        b.ts(A["lw"][:], A["lw"][:], -DEC_C, None, ALU.mult, None, ["lw"], ["lw"])
        b.ts(A["kk"][:], s_k[:], pc[:, 12:13], None, ALU.mult, None, [kk_, "pc"], ["kk"])
        b.tt(A["tmp1"][:], A["kk"][:], A["kk"][:], ALU.mult, ["kk"], ["tmp1"])
        for tl in range(2):
            sl = slice(tl * 512, (tl + 1) * 512)
            ps, pk = b.psum()
            b.mm(ps[0:64, :], ones[0:64, 0:64], A["tmp1"][:, sl], True, True, ["c_ones", "tmp1"], [pk])
            b.act(A["tmp2"][:, sl], ps[0:64, :], AF.Sqrt, [pk], ["tmp2"])
        b.ts(A["tmp2"][:], A["tmp2"][:], 1e-12, None, ALU.max, None, ["tmp2"], ["tmp2"])
        S.op("dve", lambda e: e.reciprocal(out=A["tmp2"][:], in_=A["tmp2"][:]), ["tmp2"], ["tmp2"])
        b.tt(A["kk"][:], A["kk"][:], A["tmp2"][:], ALU.mult, ["kk", "tmp2"], ["kk"])
        b.ts(A["tmp1"][:], A["aicl"][:], pc[:, 13:14], pcx[:, 0:1], ALU.mult, ALU.add, ["aicl", "pc", "pcx"], ["tmp1"])
        b.tt(A["kc"][:], s_k[:], A["tmp1"][:], ALU.mult, [kk_, "tmp1"], ["kc"])
        b.tt(A["bb"][:], A["kk"][:], A["aicl"][:], ALU.mult, ["kk", "aicl"], ["bb"], eng="pool")
        S.op("dve", lambda e: e.tensor_tensor_scan(out=A["cum"][:], data0=maskc[:], data1=A["lw"][:], initial=0.0, op0=ALU.mult, op1=ALU.add),
             ["maskc", "lw"], ["cum"])
        b.tt(A["tmp1"][:], A["cum"][:], A["lw"][:], ALU.subtract, ["cum", "lw"], ["tmp1"])
        b.act(A["tmp1"][:], A["tmp1"][:], AF.Exp, ["tmp1"], ["tmp1"])
        b.stt(atrt[:, 0, :], A["kk"][:], -1.0, A["tmp1"][:], ALU.mult, ALU.mult, ["kk", "tmp1"], ["atrt"])
        b.act(A["tmp2"][:], A["cum"][:], AF.Exp, ["cum"], ["tmp2"])
        b.tt(atrt[:, 1, :], s_r[:], A["tmp2"][:], ALU.mult, [kr, "tmp2"], ["atrt"])
        b.act(A["tmp2"][:], A["cum"][:], AF.Exp, ["cum", "atrt"], ["tmp2"], scale=-1.0)
        b.tt(A["bt"][:], A["bb"][:], A["tmp2"][:], ALU.mult, ["bb", "tmp2"], ["bt"])
        b.tt(A["kt"][:], A["kc"][:], A["tmp2"][:], ALU.mult, ["kc", "tmp2"], ["kt"], eng="pool")
        for c in range(NCH):
            cs = slice(c * 128, (c + 1) * 128)
            b.act(A["tmp1"][:, cs], A["cum"][:, cs], AF.Exp, ["cum", "atrt"], ["tmp1"], scale=-1.0, bias=A["cum"][:, c * 128 + 127:c * 128 + 128])
        b.tt(A["bh"][:], A["bb"][:], A["tmp1"][:], ALU.mult, ["bb", "tmp1"], ["bh"])
        b.tt(A["kh"][:], A["kc"][:], A["tmp1"][:], ALU.mult, ["kc", "tmp1"], ["kh"], eng="pool")
        b.act(wl[:], A["cum"][:, 127:SEGC:128], AF.Exp, ["cum"], ["wl"])
        b.stt(A["rkr"][:], s_r[:], pc[:, 15:16], A["kc"][:], ALU.mult, ALU.mult, [kr, "pc", "kc"], ["rkr"])

        ob = obuf[s % 2]; obk = ("obuf", s % 2)
        def chunk_steps(c, p):
            nonlocal stc
            cs = slice(c * 128, (c + 1) * 128)
            t4 = tok4[p]; t4k = ("tok4", p)
            psT, kT = b.psum()
            b.tr(psT[:, 0:64], atrt[:, 0, cs], ident[0:64, 0:64], ["atrt", "c_ident"], [kT])
            b.tr(psT[:, 64:128], s_v[:, cs], ident[0:64, 0:64], [kv_, "c_ident"], [kT])
            b.tr(psT[:, 128:192], A["bh"][:, cs], ident[0:64, 0:64], ["bh", "c_ident"], [kT])
            b.tr(psT[:, 192:256], A["kh"][:, cs], ident[0:64, 0:64], ["kh", "c_ident"], [kT])
            b.tr(psT[:, 256:320], sgc[:, cs], ident[0:64, 0:64], ["sgc", "c_ident"], [kT])
            b.cp(t4[:].rearrange("p a q -> p (a q)"), psT[:, 0:320], [kT], [t4k], eng="act")
            yield
            ab = AB[p]; abk = ("AB", p); ak = AK[p]; akk = ("AK", p)
            ps1, k1 = b.psum()
            b.mm(ps1[:, 0:256], A["bt"][:, cs], atrt[:, :, cs], True, True, ["bt", "atrt"], [k1])
            b.tt(ab[:].rearrange("p a q -> p (a q)"), ps1[:, 0:256], mask2[:].rearrange("p a q -> p (a q)"), ALU.mult, [k1, "mask2"], [abk])
            yield
            ps2, k2 = b.psum()
            b.mm(ps2[:, 0:256], A["kt"][:, cs], atrt[:, :, cs], True, True, ["kt", "atrt"], [k2])
            b.tt(ak[:].rearrange("p a q -> p (a q)"), ps2[:, 0:256], mask2[:].rearrange("p a q -> p (a q)"), ALU.mult, [k2, "mask2"], [akk])
            yield
            ps3, k3 = b.psum()
            b.mm(ps3[:, 0:128], atrt[:, 0, cs], A["bt"][:, cs], True, True, ["bt", "atrt"], [k3])
            b.tt(Mb[p][0][:], ps3[:, 0:128], maskL[:], ALU.mult, [k3, "maskL"], [("Mb", p, 0)])
            b.tt(TTb[p][0][:], ab[:, 0, :], ident[:], ALU.add, [abk, "c_ident"], [("TTb", p, 0)], eng="pool")
            yield
            Mc, Mck = Mb[p][0], ("Mb", p, 0)
            MTc, MTck = ab[:, 0, :], abk
            for lev in range(1, 7):
                Mn, Mnk = Mb[p][lev % 2], ("Mb", p, lev % 2)
                psm, km = b.psum()
                b.mm(psm[:, 0:128], MTc, Mc[:], True, True, [MTck, Mck], [km])
                if lev < 6:
                    MTn, MTnk = MTb[p][lev % 2], ("MTb", p, lev % 2)
                    psn, kn = b.psum()
                    b.mm(psn[:, 0:128], Mc[:], MTc, True, True, [MTck, Mck], [kn])
                b.cp(Mn[:], psm[:, 0:128], [km], [Mnk], eng="act")
                if lev < 6:
                    b.cp(MTn[:], psn[:, 0:128], [kn], [MTnk], eng="dve")
                yield
                pst, kt_ = b.psum()
                b.mm(pst[:, 0:128], Mn[:], TTb[p][(lev - 1) % 2][:], True, True, [Mnk, ("TTb", p, (lev - 1) % 2)], [kt_])
                b.tt(TTb[p][lev % 2][:], TTb[p][(lev - 1) % 2][:], pst[:, 0:128], ALU.add, [("TTb", p, (lev - 1) % 2), kt_], [("TTb", p, lev % 2)])
                yield
                Mc, Mck = Mn, Mnk
                if lev < 6:
                    MTc, MTck = MTn[:], MTnk
            TT, TTk = TTb[p][0], ("TTb", p, 0)
            pq = PQ[p]; pqk = ("PQ", p)
            psP, kP = b.psum()
            b.mm(psP[:, 0:64], TT[:], t4[:, 0, :], True, True, [TTk, t4k], [kP])
            psX, kX = b.psum()
            b.mm(psX[:, 0:64], ak[:, 0, :], t4[:, 1, :], True, True, [akk, t4k], [kX])
            b.cp(Xsb[p][:], psX[:, 0:64], [kX], [("Xsb", p)], eng="act")
            yield
            b.mm(psP[:, 64:128], TT[:], Xsb[p][:], True, True, [TTk, ("Xsb", p)], [kP])
            b.cp(pq[:].rearrange("p a q -> p (a q)"), psP[:, 0:128], [kP], [pqk], eng="dve")
            yield
            psg, kg = b.psum()
            b.mm(psg[0:64, 0:64], pq[:, 0, :], t4[:, 2, :], True, True, [pqk, t4k], [kg])
            b.stt(GT[p][:], ident[0:64, 0:64], wl[:, c:c + 1], psg[0:64, 0:64], ALU.mult, ALU.add, ["c_ident", "wl", kg], [("GT", p)])
            psh, kh_ = b.psum()
            b.mm(psh[0:64, 0:64], t4[:, 2, :], pq[:, 1, :], True, False, [pqk, t4k], [kh_])
            b.mm(psh[0:64, 0:64], t4[:, 3, :], t4[:, 1, :], False, True, [t4k], [kh_])
            b.cp(Hs[p][:], psh[0:64, 0:64], [kh_], [("Hs", p)], eng="act")
            yield
            psr, kr_ = b.psum()
            b.mm(psr[0:64, 0:128], pq[:, 0, :], ab[:, 1, :], True, True, [pqk, abk], [kr_])
            b.tt(R2T[p][:], psr[0:64, 0:128], atrt[:, 1, cs], ALU.add, [kr_, "atrt"], [("R2T", p)])
            psb, kb = b.psum()
            b.mm(psb[:, 0:1], A["rkr"][:, cs], ones[0:64, 0:1], True, True, ["rkr", "c_ones"], [kb])
            b.cp(bsc[p][:], psb[:, 0:1], [kb], [("bsc", p)], eng="act")
            yield
            stcur, stk = ST[stc % 2], ("ST", stc % 2)
            stnew, stnk = ST[(stc + 1) % 2], ("ST", (stc + 1) % 2)
            stc += 1
            psy, ky = b.psum()
            b.mm(psy[:, 0:64], R2T[p][:], stcur[:], True, False, [("R2T", p), stk], [ky])
            b.mm(psy[:, 0:64], ab[:, 1, :], pq[:, 1, :], False, False, [abk, pqk], [ky])
            b.mm(psy[:, 0:64], ak[:, 1, :], t4[:, 1, :], False, True, [akk, t4k], [ky])
            pss, ks = b.psum()
            b.mm(pss[0:64, 0:64], GT[p][:], stcur[:], True, True, [("GT", p), stk], [ks])
            b.tt(stnew[:], pss[0:64, 0:64], Hs[p][:], ALU.add, [ks, ("Hs", p)], [stnk])
            S.op("dve", lambda e, psy=psy: e.bn_stats(out=bst[p][:], in_=psy[:, 0:64]), [ky], [("bst", p)])
            S.op("dve", lambda e: e.bn_aggr(out=mv[p][:, 0:2], in_=bst[p][:]), [("bst", p)], [("mv", p)])
            yield
            b.act(mv[p][:, 2:3], mv[p][:, 1:2], AF.Sqrt, [("mv", p)], [("mv2", p)], bias=GN_EPS)
            S.op("dve", lambda e: e.reciprocal(out=mv[p][:, 2:3], in_=mv[p][:, 2:3]), [("mv2", p)], [("mv2", p)])
            b.stt(mv[p][:, 3:4], mv[p][:, 0:1], -1.0, mv[p][:, 2:3], ALU.mult, ALU.mult, [("mv", p), ("mv2", p)], [("mv3", p)])
            y_ = yn[p]; ynk = ("yn", p)
            b.act(y_[:], psy[:, 0:64], AF.Identity, [ky, ("mv2", p), ("mv3", p)], [ynk], bias=mv[p][:, 3:4], scale=mv[p][:, 2:3])
            yield
            b.tt(y_[:], y_[:], gnb[:, 0:64], ALU.mult, [ynk, "gnb"], [ynk])
            b.tt(y_[:], y_[:], gnb[:, 64:128], ALU.add, [ynk, "gnb"], [ynk], eng="pool")
            b.stt(y_[:], t4[:, 1, :], bsc[p][:, 0:1], y_[:], ALU.mult, ALU.add, [t4k, ("bsc", p), ynk], [ynk])
            b.tt(y_[:], y_[:], t4[:, 4, :], ALU.mult, [ynk, t4k], [ynk], eng="pool")
            pso, ko = b.psum()
            b.tr(pso[0:64, 0:128], y_[:], ident[:], [ynk, "c_ident"], [ko])
            b.cp(ob[:, cs], pso[0:64, 0:128], [ko], [obk], eng="act")
            yield

        import itertools
        for c0 in range(0, NCH, NW):
            gens = [chunk_steps(c0 + q, q) for q in range(min(NW, NCH - c0))]
            for _ in itertools.zip_longest(*gens):
                pass
        S.dma("sp", yg_d[:, s * SEGC:(s + 1) * SEGC], ob[:], reads=[obk], writes=[("outC", s)])
        outk.append(("outC", s))
    return outk


def p1c_dram(b, T=SEQ):
    return (b.din("wc", [D_MODEL, 384]), b.din("pc", [64, 32]), b.din("wup", [64, 128]), b.din("gnb", [128, 192]), b.dout("ygC", [64, T]))


def build_p1c(T=SEQ):
    b = B()
    ones, ident = b.ident()
    st = contextlib.ExitStack()
    keys = emit_p1c(b, st, ones, ident, b.din("xTt", [T // 512, 128, KCH * 512]), *p1c_dram(b, T), T=T)
    return b.finish(keys, extra=[st]), b.outs


def host_p1c(inp, l, xT):
    w_in = inp["w_in"][l]; b_in = inp["b_in"][l]
    C0 = 1024 + 5120
    mu = inp["rwkv_mu"][l]
    maps = []
    for c in range(8):
        hs = slice(64 * c, 64 * c + 64)
        secs = [(C0, hs), (C0 + 512, hs), (C0 + 1024, hs), (C0 + 1536, slice(0, 64)), (C0 + 1600, slice(0, 64)), (C0 + 1664, hs)]
        wc = np.concatenate([w_in[:, o:o + 512][:, sl_] if sl_ is hs else w_in[:, o:o + 64] for o, sl_ in secs], axis=1)
        pc = np.zeros((64, 32), np.float32)
        pc[:, 0] = b_in[C0:C0 + 512][hs]; pc[:, 1] = b_in[C0 + 512:C0 + 1024][hs]; pc[:, 2] = b_in[C0 + 1024:C0 + 1536][hs]
        pc[:, 3] = b_in[C0 + 1536:C0 + 1600]; pc[:, 4] = b_in[C0 + 1600:C0 + 1664]
        pc[:, 5] = mu[0:512][hs]; pc[:, 6] = mu[512:1024][hs]; pc[:, 7] = mu[1024:1536][hs]
        pc[:, 8] = mu[1536:1600]; pc[:, 9] = mu[1600:1664]
        pc[:, 10] = inp["rwkv_w0"][l][hs]; pc[:, 11] = inp["rwkv_a0"][l][hs]
        pc[:, 12] = inp["rwkv_k_k"][l][hs]; pc[:, 13] = inp["rwkv_k_a"][l][hs]
        pc[:, 15] = inp["rwkv_r_k"][l][c]
        pc[:, 16] = b_in[C0 + 1664:C0 + 2176][hs]
        wup = np.concatenate([inp["rwkv_w_up"][l][:, hs], inp["rwkv_a_up"][l][:, hs]], axis=1)
        gn = np.concatenate([inp["rwkv_gn_g"][l][hs], inp["rwkv_gn_b"][l][hs], b_in[C0 + 1664:C0 + 2176][hs]])
        gnb = np.ascontiguousarray(np.broadcast_to(gn[None, :], (128, 192)))
        maps.append({"xTt": xT, "wc": np.ascontiguousarray(wc), "pc": pc, "wup": np.ascontiguousarray(wup), "gnb": gnb})
    return maps


def build_p1():
    b = B()
    S = b.S
    ones, ident = b.ident()
    xT = b.din("xTt", [SEQ // 512, 128, KCH * 512])
    a_dram = (b.din("wa", [D_MODEL, 128]), b.din("pa", [64, 16]), b.din("gw", [64, 128]), b.dout("ygA", [64, SEQ]))
    b_dram = p1b_dram(b)
    c_dram = p1c_dram(b)
    keys = []
    st = contextlib.ExitStack()
    keys += emit_p1a(b, st, xT, *a_dram)
    S.barrier(); st.close()
    st = contextlib.ExitStack()
    keys += emit_p1b(b, st, ones, ident, *b_dram)
    S.barrier(); st.close()
    st = contextlib.ExitStack()
    keys += emit_p1c(b, st, ones, ident, xT, *c_dram)
    return b.finish(keys, extra=[st]), b.outs


def host_p1(inp, l, xT):
    xTt = tile_xT(xT)
    ma, mb, mc = host_p1a(inp, l, xTt), host_p1b(inp, l, xT), host_p1c(inp, l, xTt)
    maps = []
    for c in range(8):
        m = dict(ma[c]); m.update(mb[c]); m.update(mc[c])
        maps.append(m)
    return maps


ALPHA = (2.0 * 2) ** 0.25
LN_EPS = 1e-5
NT = 1024
HALO = 32


def build_p2():
    b = B()
    S = b.S
    xTh = b.din("xTh", [D_MODEL, NT + HALO])
    xown = b.din("xown", [NT, D_MODEL])
    ygT = b.din("ygT", [1536, NT])
    wm_d = b.din("wm", [D_MODEL, 8192])
    wd_d = b.din("wd", [D_MODEL, 1536])
    wbr_d = b.din("wbr", [2048, D_MODEL])
    wout_d = b.din("wout", [D_MODEL, D_MODEL])
    pm_d = b.din("pm", [128, 64])
    pd_d = b.din("pd", [128, 64])
    dww_d = b.din("dww", [128, 4 * 31])
    lng_d = b.din("lng", [128, D_MODEL])
    lnb_d = b.din("lnb", [128, D_MODEL])
    xo_d = b.dout("xo", [NT, D_MODEL])

    ones, ident = b.ident()
    pm = b.sb("pm", [128, 64]); pd = b.sb("pd", [128, 64]); dww = b.sb("dww", [128, 4, 31])
    b.load(pm[:], pm_d, "pm"); b.load(pd[:], pd_d, "pd")
    b.load(dww[:], dww_d.rearrange("p (c j) -> p c j", j=31), "dww")
    mixT = b.sb("mixT", [128, KCH, NT], BF16)

    stDM = contextlib.ExitStack()
    xb = b.sb("xb", [128, KCH, NT + HALO], BF16, st=stDM)
    ygd = b.sb("ygd", [128, 4, NT], BF16, st=stDM)
    S.dma("pool", xb[:], xT_tile_ap(xTh, 0, NT + HALO), writes=["xb"])

    stD = contextlib.ExitStack()
    wdb = [b.sb("wdb%d" % i, [128, KCH, 3, 128], BF16, st=stD) for i in range(2)]
    cu = b.sb("cu", [128, NT + HALO], F32, st=stD)
    t1 = b.sb("t1", [128, NT + HALO], F32, st=stD)
    t2 = b.sb("t2", [128, NT + HALO], F32, st=stD)
    cv = b.sb("cv", [128, 4, NT], F32, st=stD)
    dg = b.sb("dg", [128, 4, NT], F32, st=stD)
    sq = b.sb("sq", [128, NT], F32, st=stD)
    mean = b.sb("mean", [128, NT], F32, st=stD)
    rstd = b.sb("rstd", [128, NT], F32, st=stD)
    tiles = [(0, HALO), (HALO, 512), (HALO + 512, 512)]
    for cc in range(4):
        w = wdb[cc % 2]; wk = ("wdb", cc % 2)
        for j in range(3):
            S.dma("pool", w[:, :, j, :], w_ap(wd_d[:, j * 512 + cc * 128: j * 512 + cc * 128 + 128]), writes=[wk])
        groups = [[tiles[0]], [tiles[1], tiles[2]]]
        for grp in groups:
            for sec in range(3):
                if sec == 2 and grp[0][0] < HALO:
                    continue
                pss = [b.psum() for _ in grp]
                for k in range(KCH):
                    for (ps_, pk_), (t0, n) in zip(pss, grp):
                        b.mm(ps_[:, :n], w[:, k, sec, :], xb[:, k, t0:t0 + n], k == 0, k == KCH - 1, [wk, "xb"], [pk_])
                for (ps_, pk_), (t0, n) in zip(pss, grp):
                    if sec == 0:
                        b.act(t1[:, t0:t0 + n], ps_[:, :n], AF.Identity, [pk_, "pd"], [("t1", t0)], bias=pd[:, cc:cc + 1])
                    elif sec == 1:
                        b.act(t2[:, t0:t0 + n], ps_[:, :n], AF.Sigmoid, [pk_, "pd"], [("t2", t0)], bias=pd[:, 4 + cc:5 + cc])
                        b.tt(cu[:, t0:t0 + n], t1[:, t0:t0 + n], t2[:, t0:t0 + n], ALU.mult, [("t1", t0), ("t2", t0)], ["cu"])
                    else:
                        b.act(dg[:, cc, t0 - HALO:t0 - HALO + n], ps_[:, :n], AF.Silu, [pk_, "pd"], [("dg", cc)], bias=pd[:, 8 + cc:9 + cc])
        b.ts(cu[:, 0:HALO], cu[:, 0:HALO], pd[:, 24:25], None, ALU.mult, None, ["cu", "pd"], ["cu"])
        ck = ("cv", cc)
        b.ts(cv[:, cc, :], cu[:, 2:2 + NT], dww[:, cc, 0:1], pd[:, 12 + cc:13 + cc], ALU.mult, ALU.add, ["cu", "dww", "pd"], [ck])
        for j in range(1, 31):
            b.stt(cv[:, cc, :], cu[:, 2 + j:2 + j + NT], dww[:, cc, j:j + 1], cv[:, cc, :], ALU.mult, ALU.add, ["cu", "dww", ck], [ck])
    for tt_ in range(2):
        sl = slice(tt_ * 512, (tt_ + 1) * 512)
        ps1, k1 = b.psum()
        for cc in range(4):
            b.mm(ps1[:, :], ones[:, 0:128], cv[:, cc, sl], cc == 0, cc == 3, ["c_ones", ("cv", cc)], [k1])
        b.act(mean[:, sl], ps1[:, :], AF.Copy, [k1], ["mean"], scale=1.0 / 512)
        ps2, k2 = b.psum()
        for cc in range(4):
            b.act(sq[:, sl], cv[:, cc, sl], AF.Square, [("cv", cc)], ["sq"])
            b.mm(ps2[:, :], ones[:, 0:128], sq[:, sl], cc == 0, cc == 3, ["c_ones", "sq"], [k2])
        b.act(rstd[:, sl], ps2[:, :], AF.Copy, [k2], ["rstd"], scale=1.0 / 512)
    b.tt(sq[:], mean[:], mean[:], ALU.mult, ["mean"], ["sq"])
    b.tt(rstd[:], rstd[:], sq[:], ALU.subtract, ["rstd", "sq"], ["rstd"])
    b.act(rstd[:], rstd[:], AF.Sqrt, ["rstd"], ["rstd"], bias=LN_EPS)
    S.op("dve", lambda e: e.reciprocal(out=rstd[:], in_=rstd[:]), ["rstd"], ["rstd"])
    for cc in range(4):
        b.tt(sq[:], cv[:, cc, :], mean[:], ALU.subtract, [("cv", cc), "mean"], ["sq"])
        b.tt(sq[:], sq[:], rstd[:], ALU.mult, ["sq", "rstd"], ["sq"], eng="pool")
        b.act(sq[:], sq[:], AF.Silu, ["sq", "pd"], ["sq"], bias=pd[:, 20 + cc:21 + cc], scale=pd[:, 16 + cc:17 + cc])
        b.tt(ygd[:, cc, :], sq[:], dg[:, cc, :], ALU.mult, ["sq", ("dg", cc)], ["ygd"])
    S.barrier()
    stD.close()

    stM = contextlib.ExitStack()
    ygb = b.sb("ygb", [128, 12, NT], BF16, st=stM)
    S.dma("pool", ygb[:], ygT.rearrange("(c p) t -> p c t", p=128), writes=["ygb"])
    wmb = [b.sb("wmb%d" % i, [128, KCH, 512], BF16, st=stM) for i in range(3)]
    wbb = [b.sb("wbb%d" % i, [128, 4, 512], BF16, st=stM) for i in range(3)]
    macc = b.sb("macc", [128, 4, NT], F32, st=stM)
    mg = [b.sb("mg%d" % i, [128, 512], F32, st=stM) for i in range(2)]
    tmp = [b.sb("tmp%d" % i, [128, 512], F32, st=stM) for i in range(2)]
    it = 0
    for dcg in range(4):
        for n in range(4):
            wi = (dcg * 4 + n) % 3
            wm_, wmk = wmb[wi], ("wmb", wi)
            wb_, wbk = wbb[wi], ("wbb", wi)
            c0 = n * 2048 + dcg * 512
            S.dma("pool", wm_[:], w_ap(wm_d[:, c0:c0 + 512]), writes=[wmk])
            S.dma("pool", wb_[:], wbr_d[n * 512:(n + 1) * 512, dcg * 512:(dcg + 1) * 512].rearrange("(c p) d -> p c d", p=128), writes=[wbk])
            for j in range(4):
                psm2 = [b.psum() for _ in range(2)]
                for k in range(KCH):
                    for tt_ in range(2):
                        b.mm(psm2[tt_][0][:, :], wm_[:, k, j * 128:(j + 1) * 128], xb[:, k, HALO + tt_ * 512:HALO + (tt_ + 1) * 512],
                             k == 0, k == KCH - 1, [wmk, "xb"], [psm2[tt_][1]])
                psb2 = [b.psum() for _ in range(2)]
                for c4 in range(4):
                    for tt_ in range(2):
                        sl = slice(tt_ * 512, (tt_ + 1) * 512)
                        rhs = ygb[:, n * 4 + c4, sl] if n < 3 else ygd[:, c4, sl]
                        b.mm(psb2[tt_][0][:, :], wb_[:, c4, j * 128:(j + 1) * 128], rhs, c4 == 0, c4 == 3, [wbk, "ygb", "ygd"], [psb2[tt_][1]])
                for tt_ in range(2):
                    sl = slice(tt_ * 512, (tt_ + 1) * 512)
                    psm, km = psm2[tt_]
                    psb, kb = psb2[tt_]
                    m_ = mg[it % 2]; mk = ("mg", it % 2)
                    col = n * 16 + dcg * 4 + j
                    b.act(m_[:], psm[:, :], AF.Sigmoid, [km, "pm"], [mk], bias=pm[:, col:col + 1])
                    ak = ("macc", j, tt_)
                    if n == 0:
                        b.tt(macc[:, j, sl], m_[:], psb[:, :], ALU.mult, [mk, kb], [ak])
                    else:
                        t_ = tmp[it % 2]; tk = ("tmp", it % 2)
                        b.tt(t_[:], m_[:], psb[:, :], ALU.mult, [mk, kb], [tk])
                        b.tt(macc[:, j, sl], macc[:, j, sl], t_[:], ALU.add, [ak, tk], [ak])
                    it += 1
                    if n == 3:
                        b.cp(mixT[:, dcg * 4 + j, sl], macc[:, j, sl], [ak], ["mixT"], eng="act")
    S.barrier()
    stM.close()
    stDM.close()

    stO = contextlib.ExitStack()
    wob = b.sb("wob", [128, KCH, D_MODEL], BF16, st=stO)
    for eg in range(4):
        S.dma("pool", wob[:, :, eg * 512:(eg + 1) * 512], w_ap(wout_d[:, eg * 512:(eg + 1) * 512]), writes=[("wob", eg)])
    lng = b.sb("lng", [128, D_MODEL], F32, st=stO); lnb = b.sb("lnb", [128, D_MODEL], F32, st=stO)
    b.load(lng[:], lng_d, "lng"); b.load(lnb[:], lnb_d, "lnb")
    xt = [b.sb("xt%d" % i, [128, D_MODEL], F32, st=stO) for i in range(2)]
    z = [b.sb("z%d" % i, [128, D_MODEL], F32, st=stO) for i in range(2)]
    bst = b.sb("bst", [128, 4, 6], F32, st=stO)
    mv = b.sb("mv", [128, 4], F32, st=stO)
    outk = []
    for tt_ in range(NT // 128):
        x_ = xt[tt_ % 2]; xk = ("xt", tt_ % 2)
        z_ = z[tt_ % 2]; zk = ("z", tt_ % 2)
        b.load(x_[:], xown[tt_ * 128:(tt_ + 1) * 128, :], xk)
        pso = [b.psum() for _ in range(4)]
        for dc in range(KCH):
            for eg in range(4):
                b.mm(pso[eg][0][:, :], mixT[:, dc, tt_ * 128:(tt_ + 1) * 128], wob[:, dc, eg * 512:(eg + 1) * 512], dc == 0, dc == KCH - 1,
                     ["mixT", ("wob", eg)], [pso[eg][1]])
        for eg in range(4):
            es = slice(eg * 512, (eg + 1) * 512)
            ps, pk = pso[eg]
            b.stt(z_[:, es], x_[:, es], ALPHA, ps[:, :], ALU.mult, ALU.add, [xk, pk], [(zk, eg)])
            S.op("dve", lambda e, eg=eg, z_=z_, es=es: e.bn_stats(out=bst[:, eg, :], in_=z_[:, es]), [(zk, eg)], [("bst", eg)])
        S.op("dve", lambda e: e.bn_aggr(out=mv[:, 0:2], in_=bst[:].rearrange("p a b -> p (a b)")), [("bst", eg) for eg in range(4)], ["mv"])
        b.act(mv[:, 2:3], mv[:, 1:2], AF.Sqrt, ["mv"], ["mv2"], bias=LN_EPS)
        S.op("dve", lambda e: e.reciprocal(out=mv[:, 2:3], in_=mv[:, 2:3]), ["mv2"], ["mv2"])
        b.stt(mv[:, 3:4], mv[:, 0:1], -1.0, mv[:, 2:3], ALU.mult, ALU.mult, ["mv", "mv2"], ["mv3"])
        b.act(z_[:], z_[:], AF.Identity, [(zk, eg) for eg in range(4)] + ["mv2", "mv3"], [zk], bias=mv[:, 3:4], scale=mv[:, 2:3])
        b.tt(z_[:], z_[:], lng[:], ALU.mult, [zk, "lng"], [zk])
        b.tt(z_[:], z_[:], lnb[:], ALU.add, [zk, "lnb"], [zk] + [(zk, eg) for eg in range(4)], eng="pool")
        S.dma("sp", xo_d[tt_ * 128:(tt_ + 1) * 128, :], z_[:], reads=[zk], writes=[zk, ("xo", tt_)] + [(zk, eg) for eg in range(4)])
        outk.append(("xo", tt_))
    nc = b.finish(outk, extra=[stO])
    return nc, b.outs


def host_p2(inp, l, x_tok, ygT_full):
    w_in = inp["w_in"][l]; b_in = inp["b_in"][l]
    D0 = 1024 + 5120 + 2176
    M0 = D0 + 1536
    wm = np.ascontiguousarray(w_in[:, M0:M0 + 8192])
    wd = np.ascontiguousarray(w_in[:, D0:D0 + 1536])
    wbr = np.ascontiguousarray(inp["w_br"][l].reshape(2048, 2048))
    wout = np.ascontiguousarray(inp["w_out"][l])
    pm = np.ascontiguousarray(b_in[M0:M0 + 8192].reshape(64, 128).T)
    lng = np.ascontiguousarray(np.broadcast_to(inp["ln_g"][l][None, :], (128, 2048)))
    lnb = np.ascontiguousarray(np.broadcast_to(inp["ln_b"][l][None, :], (128, 2048)))
    dww = np.ascontiguousarray(inp["conf_dw_w"][l].T.reshape(4, 128, 31).transpose(1, 0, 2).reshape(128, 124))
    xT = x_tok.T
    maps = []
    for c in range(8):
        pd = np.zeros((128, 64), np.float32)
        bd = b_in[D0:D0 + 1536]
        for cc in range(4):
            pd[:, cc] = bd[cc * 128:(cc + 1) * 128]
            pd[:, 4 + cc] = bd[512 + cc * 128:512 + (cc + 1) * 128]
            pd[:, 8 + cc] = bd[1024 + cc * 128:1024 + (cc + 1) * 128]
            pd[:, 12 + cc] = inp["conf_dw_b"][l][cc * 128:(cc + 1) * 128]
            pd[:, 16 + cc] = inp["conf_ln_g"][l][cc * 128:(cc + 1) * 128]
            pd[:, 20 + cc] = inp["conf_ln_b"][l][cc * 128:(cc + 1) * 128]
        pd[:, 24] = 0.0 if c == 0 else 1.0
        t0 = c * NT
        xTh = np.zeros((2048, NT + HALO), np.float32)
        if c == 0:
            xTh[:, HALO:] = xT[:, 0:NT]
        else:
            xTh[:] = xT[:, t0 - HALO:t0 + NT]
        maps.append({"xTh": xTh, "xown": np.ascontiguousarray(x_tok[t0:t0 + NT]), "ygT": np.ascontiguousarray(ygT_full[:, t0:t0 + NT]),
                     "wm": wm, "wd": wd, "wbr": wbr, "wout": wout, "pm": pm, "pd": pd, "dww": dww, "lng": lng, "lnb": lnb})
    return maps


CORES = list(range(8))


def kernel(**inputs):
    inp = {k: np.asarray(v) for k, v in inputs.items()}
    x = np.ascontiguousarray(inp["x"][0], dtype=np.float32)
    for l in range(2):
        xT = np.ascontiguousarray(x.T)
        nc, _ = build_p1()
        r1 = run_bass_kernel_spmd(nc, host_p1(inp, l, xT), core_ids=CORES)
        ygA = np.concatenate([r["ygA"] for r in r1.results], axis=0)
        ygB = gather_p1b(r1.results)
        ygC = np.concatenate([r["ygC"] for r in r1.results], axis=0)
        ygT = np.ascontiguousarray(np.concatenate([ygA, ygB, ygC], axis=0))
        nc, _ = build_p2()
        r2 = run_bass_kernel_spmd(nc, host_p2(inp, l, x, ygT), core_ids=CORES)
        x = np.ascontiguousarray(np.concatenate([r["xo"] for r in r2.results], axis=0))
    return x[None].astype(np.float32)
```

```python
import contextlib
import numpy as np
import concourse.bass as bass
import concourse.mybir as mybir

F32 = mybir.dt.float32
BF16 = mybir.dt.bfloat16
AF = mybir.ActivationFunctionType
ALU = mybir.AluOpType
AX = mybir.AxisListType


class Sched:
    ENGS = ("pe", "dve", "act", "pool", "sp")

    def __init__(self, nc, stack, n_lanes=24, n_sw_lanes=16):
        self.nc = nc
        self.h = {"pe": nc.tensor, "dve": nc.vector, "act": nc.scalar,
                  "pool": nc.gpsimd, "sp": nc.sync}
        self.sem = {e: stack.enter_context(nc.semaphore("s_" + e)) for e in self.ENGS}
        self.cnt = {e: 0 for e in self.ENGS}
        self.n_hw = n_lanes
        self.lanes = [stack.enter_context(nc.semaphore("l%d" % i)) for i in range(n_lanes)]
        self.lanes += [stack.enter_context(nc.semaphore("w%d" % i)) for i in range(n_sw_lanes)]
        self.lane_tot = [0] * (n_lanes + n_sw_lanes)
        self.lane_next = 0
        self.sw_next = 0
        self.prog = {e: [] for e in self.ENGS}
        self.seen = {e: {} for e in self.ENGS}
        self.lastw = {}
        self.readers = {}
        self.semobj = {}
        self.unread = set()

    def _need(self, e, tok, waits):
        if tok is None:
            return
        k, v = tok
        if e == "pe" and k == ("e", "pe"):
            return
        if self.seen[e].get(k, 0) >= v:
            return
        self.seen[e][k] = v
        waits.append((k, v))

    def _semof(self, k):
        return self.sem[k[1]] if k[0] == "e" else self.lanes[k[1]]

    def _deps(self, e, reads, writes):
        waits = []
        for r in reads:
            self._need(e, self.lastw.get(r), waits)
        for w in writes:
            self._need(e, self.lastw.get(w), waits)
            for t in self.readers.get(w, ()):
                self._need(e, t, waits)
        return waits

    def _commit(self, tok, reads, writes):
        for r in reads:
            self.unread.discard(r)
            self.readers.setdefault(r, []).append(tok)
        for w in writes:
            self.lastw[w] = tok
            self.readers[w] = []

    def op(self, e, fn, reads=(), writes=()):
        waits = self._deps(e, reads, writes)
        self.cnt[e] += 1
        tok = (("e", e), self.cnt[e])
        self.prog[e].append((waits, fn, (self.sem[e], 1)))
        self._commit(tok, reads, writes)
        return tok

    def take_lane(self, q):
        if q == "pool":
            lane = self.n_hw + self.sw_next
            self.sw_next = (self.sw_next + 1) % (len(self.lanes) - self.n_hw)
        else:
            lane = self.lane_next
            self.lane_next = (self.lane_next + 1) % self.n_hw
        return lane

    def dma(self, q, out, in_, reads=(), writes=(), **kw):
        lane = self.take_lane(q)
        waits = self._deps(q, reads, writes)
        self._need(q, (("l", lane), self.lane_tot[lane]) if self.lane_tot[lane] else None, waits)
        self.lane_tot[lane] += 16
        tok = (("l", lane), self.lane_tot[lane])

        def fn(eng, out=out, in_=in_, kw=kw):
            return eng.dma_start(out=out, in_=in_, **kw)
        self.prog[q].append((waits, fn, (self.lanes[lane], 16)))
        self._commit(tok, reads, writes)
        return tok

    def wait_all(self, e, toks):
        waits = []
        for t in toks:
            self._need(e, t, waits)
        self.prog[e].append((waits, None, None))

    def barrier(self):
        toks = [(("e", e), self.cnt[e]) for e in self.ENGS if self.cnt[e]]
        toks += [(("l", i), t) for i, t in enumerate(self.lane_tot) if t]
        for e in self.ENGS:
            self.wait_all(e, toks)

    def emit(self, block):
        def mk(e):
            def body(eng):
                for waits, fn, inc in self.prog[e]:
                    for k, v in waits:
                        eng.wait_ge(self._semof(k), v)
                    if fn is not None:
                        ins = fn(eng)
                        ins.then_inc(inc[0], inc[1])
            return body
        block.tensor(mk("pe"))
        block.vector(mk("dve"))
        block.scalar(mk("act"))
        block.gpsimd(mk("pool"))
        block.sync(mk("sp"))


import contextlib
import numpy as np
import concourse.bass as bass
import concourse.mybir as mybir
from concourse.bass_utils import run_bass_kernel_spmd

F32 = mybir.dt.float32
BF16 = mybir.dt.bfloat16
AF = mybir.ActivationFunctionType
ALU = mybir.AluOpType

D_MODEL = 2048
SEQ = 8192
KCH = 16


class B:
    def __init__(self):
        self.nc = bass.Bass("TRN2", target_bir_lowering=False, num_devices=8)
        self.st = contextlib.ExitStack()
        self.S = Sched(self.nc, self.st)
        self.ps = [self.st.enter_context(self.nc.psum_tensor("ps%d" % i, [128, 512], F32)) for i in range(8)]
        self.ps_i = 0
        self.uid = 0
        self.outs = []
        self.pfx = ""

    def din(self, name, shape, dt=F32):
        return self.nc.dram_tensor(name, list(shape), dt, kind="ExternalInput").ap()

    def dout(self, name, shape, dt=F32):
        self.outs.append(name)
        return self.nc.dram_tensor(name, list(shape), dt, kind="ExternalOutput").ap()

    def sb(self, name, shape, dt=F32, st=None):
        return (st or self.st).enter_context(self.nc.sbuf_tensor("sb_" + self.pfx + name, list(shape), dt))

    def psum(self):
        i = self.ps_i
        self.ps_i = (i + 1) % 8
        if ("ps", i) in self.S.unread:
            raise RuntimeError("PSUM bank %d handed out again before its previous contents were read" % i)
        self.S.unread.add(("ps", i))
        return self.ps[i], ("ps", i)

    def key(self, p="t"):
        self.uid += 1
        return (p, self.uid)

    def finish(self, out_keys, extra=()):
        S = self.S
        S.wait_all("sp", [S.lastw[k] for k in out_keys])
        with self.nc.Block() as block:
            S.emit(block)
        for e in extra:
            e.close()
        self.st.close()
        return self.nc

    def mm(self, out, lhsT, rhs, start, stop, reads, writes):
        return self.S.op("pe", lambda e: e.matmul(out, lhsT=lhsT, rhs=rhs, start=start, stop=stop), reads, writes)

    def tr(self, out, in_, ident, reads, writes):
        return self.S.op("pe", lambda e: e.transpose(out=out, in_=in_, identity=ident), reads, writes)

    def act(self, out, in_, func, reads, writes, bias=None, scale=None, eng="act"):
        kw = {}
        if bias is not None:
            kw["bias"] = bias
        if scale is not None:
            kw["scale"] = scale
        return self.S.op("act", lambda e: e.activation(out=out, in_=in_, func=func, **kw), reads, writes)

    def tt(self, out, in0, in1, op, reads, writes, eng="dve"):
        return self.S.op(eng, lambda e: e.tensor_tensor(out=out, in0=in0, in1=in1, op=op), reads, writes)

    def ts(self, out, in0, s1, s2, op0, op1, reads, writes, eng="dve"):
        if op1 is None:
            return self.S.op(eng, lambda e: e.tensor_scalar(out=out, in0=in0, scalar1=s1, scalar2=None, op0=op0), reads, writes)
        return self.S.op(eng, lambda e: e.tensor_scalar(out=out, in0=in0, scalar1=s1, scalar2=s2, op0=op0, op1=op1), reads, writes)

    def stt(self, out, in0, scalar, in1, op0, op1, reads, writes):
        return self.S.op("dve", lambda e: e.scalar_tensor_tensor(out=out, in0=in0, scalar=scalar, in1=in1, op0=op0, op1=op1), reads, writes)

    def cp(self, out, in_, reads, writes, eng="dve"):
        if eng == "act":
            return self.S.op("act", lambda e: e.copy(out=out, in_=in_), reads, writes)
        return self.S.op(eng, lambda e: e.tensor_copy(out=out, in_=in_), reads, writes)

    def memset(self, ap, val, writes, eng="pool"):
        return self.S.op(eng, lambda e: e.memset(ap, val), (), writes)

    def load(self, sb_ap, dram_ap, key, q="sp", **kw):
        return self.S.dma(q, sb_ap, dram_ap, writes=[key], **kw)

    def _lane_op(self, q, fn, reads, writes):
        S = self.S
        lane = S.take_lane(q)
        waits = S._deps(q, reads, writes)
        S._need(q, (("l", lane), S.lane_tot[lane]) if S.lane_tot[lane] else None, waits)
        S.lane_tot[lane] += 16
        tok = (("l", lane), S.lane_tot[lane])
        S.prog[q].append((waits, fn, (S.lanes[lane], 16)))
        S._commit(tok, reads, writes)
        return tok

    def coll(self, kind, src, dst, reads, writes):
        return self._lane_op("pool", lambda e: e.collective_compute(kind, ALU.bypass, replica_groups=[list(range(8))],
                                                                    ins=[src], outs=[dst]), reads, writes)

    def gather(self, out, in_, idx, reads, writes):
        return self._lane_op("pool", lambda e: e.indirect_dma_start(out=out, out_offset=None, in_=in_,
                                                                    in_offset=bass.IndirectOffsetOnAxis(ap=idx, axis=0)), reads, writes)

    def ident(self, n=128):
        ones = self.sb("c_ones", [128, 512], F32)
        ident = self.sb("c_ident", [128, 128], F32)
        self.memset(ones[:], 1.0, ["c_ones"])
        self.S.op("pool", lambda e: e.affine_select(out=ident[:], in_=ones[:, 0:128], pattern=[[1, 128]],
                                                    compare_op=ALU.is_equal, fill=0.0, base=0, channel_multiplier=-1),
                  ["c_ones"], ["c_ident"])
        return ones, ident


def xT_tile_ap(xT, t0, n):
    return xT[:, t0:t0 + n].rearrange("(k p) t -> p k t", p=128)


def x_tile(xt, i):
    return xt[i].rearrange("p (k t) -> p k t", k=KCH)


def tile_xT(xT):
    n = xT.shape[1] // 512
    return np.ascontiguousarray(xT.reshape(KCH, 128, n, 512).transpose(2, 1, 0, 3).reshape(n, 128, KCH * 512))


def w_ap(w):
    return w.rearrange("(k p) c -> p k c", p=128)


def emit_p1a(b, st, xT, wa_d, pa_d, gw_d, yg_d, T=SEQ, SEG=512):
    S = b.S
    b.pfx = "a_"
    nseg = T // SEG

    wa = b.sb("wa", [128, KCH, 128], BF16, st=st)
    pa = b.sb("pa", [64, 16], st=st)
    gw = b.sb("gw", [64, 128], st=st)
    c1 = b.sb("c1", [64, 4], st=st)
    SUP = 4
    xb = [b.sb("xb%d" % i, [128, KCH, SUP * SEG], BF16, st=st) for i in range(2)]
    sgs = b.sb("sgs", [64, SUP * SEG], st=st)
    axb = b.sb("axb", [64, 4 * SEG + 3], st=st)
    u = [b.sb("u%d" % i, [64, SEG], st=st) for i in range(2)]
    gr = [b.sb("gr%d" % i, [64, SEG], st=st) for i in range(2)]
    gi = [b.sb("gi%d" % i, [64, SEG], st=st) for i in range(2)]
    at = [b.sb("at%d" % i, [64, SEG], st=st) for i in range(2)]
    a2 = [b.sb("a2%d" % i, [64, SEG], st=st) for i in range(2)]
    bt = [b.sb("bt%d" % i, [64, SEG], st=st) for i in range(2)]
    hb = [b.sb("hb%d" % i, [64, SEG], st=st) for i in range(2)]
    yo = [b.sb("yo%d" % i, [64, SEG], st=st) for i in range(2)]

    S.dma("pool", wa[:], w_ap(wa_d), writes=["wa"])
    b.load(pa[:], pa_d, "pa")
    b.load(gw[:], gw_d, "gw")
    b.memset(axb[:, 0:3], 0.0, ["axb"])
    b.act(c1[:, 2:3], pa[:, 9:10], AF.Exp, ["pa"], ["c1"], scale=-1.0)
    b.act(c1[:, 3:4], c1[:, 2:3], AF.Ln, ["c1"], ["c1"], bias=1.0)
    b.ts(c1[:, 0:1], c1[:, 3:4], -8.0, None, ALU.mult, None, ["c1"], ["c1"])
    b.ts(c1[:, 1:2], c1[:, 3:4], -16.0, None, ALU.mult, None, ["c1"], ["c1"])

    NWA = 2

    def seg_steps(s, p):
        sub = s % SUP
        o0 = sub * SEG
        u_, gr_, gi_, at_, a2_, bt_ = u[p], gr[p], gi[p], at[p], a2[p], bt[p]
        ku, kgr, kgi, kat, ka2, kbt = [(n, p) for n in ("u", "gr", "gi", "at", "a2", "bt")]
        b.ts(u_[:], axb[:, o0:o0 + SEG], pa[:, 3:4], pa[:, 2:3], ALU.mult, ALU.add, ["axb", "pa"], [ku])
        for j in range(1, 4):
            b.stt(u_[:], axb[:, o0 + j:o0 + j + SEG], pa[:, 3 + j:4 + j], u_[:], ALU.mult, ALU.add, ["axb", "pa", ku], [ku])
        if sub == SUP - 1 or s == nseg - 1:
            b.cp(axb[:, 0:3], axb[:, (sub + 1) * SEG:(sub + 1) * SEG + 3], ["axb", ku], ["axb"], eng="pool")
        yield
        ps3, pk3 = b.psum()
        b.mm(ps3[0:64, :SEG], gw[:, 0:64], u_[:], True, True, ["gw", ku], [pk3])
        b.act(gr_[:], ps3[0:64, :SEG], AF.Sigmoid, [pk3, "pa"], [kgr], bias=pa[:, 7:8])
        ps4, pk4 = b.psum()
        b.mm(ps4[0:64, :SEG], gw[:, 64:128], u_[:], True, True, ["gw", ku], [pk4])
        b.act(gi_[:], ps4[0:64, :SEG], AF.Sigmoid, [pk4, "pa"], [kgi], bias=pa[:, 8:9])
        yield
        b.act(at_[:], gr_[:], AF.Exp, [kgr, "c1"], [kat], scale=c1[:, 0:1])
        b.act(a2_[:], gr_[:], AF.Exp, [kgr, "c1"], [ka2], scale=c1[:, 1:2])
        yield
        b.ts(a2_[:], a2_[:], -1.0, 1.0, ALU.mult, ALU.add, [ka2], [ka2])
        b.act(a2_[:], a2_[:], AF.Sqrt, [ka2], [ka2])
        b.tt(bt_[:], gi_[:], u_[:], ALU.mult, [kgi, ku], [kbt])
        yield
        b.tt(bt_[:], bt_[:], a2_[:], ALU.mult, [kbt, ka2], [kbt])
        hcur = hb[s % 2]
        hprev = hb[(s + 1) % 2]
        init = 0.0 if s == 0 else hprev[:, SEG - 1:SEG]
        S.op("dve", lambda e, hcur=hcur, init=init: e.tensor_tensor_scan(out=hcur[:], data0=at_[:], data1=bt_[:], initial=init,
                                                                        op0=ALU.mult, op1=ALU.add),
             [kat, kbt, ("hb", (s + 1) % 2)], [("hb", s % 2)])
        yield
        yt = yo[s % 2]
        b.tt(yt[:], hcur[:], sgs[:, sub * SEG:(sub + 1) * SEG], ALU.mult, [("hb", s % 2), ("sgs", sub)], [("yo", s % 2)], eng="pool")
        S.dma("sp", yg_d[:, s * SEG:(s + 1) * SEG], yt[:], reads=[("yo", s % 2)], writes=[("ygd", s)])
        yield

    import itertools
    for s0 in range(0, nseg, SUP):
        sup = s0 // SUP
        xs = xb[sup % 2]
        xk = ("xb", sup % 2)
        nsub = min(SUP, nseg - s0)
        for q in range(nsub):
            S.dma("pool", xs[:, :, q * SEG:(q + 1) * SEG], x_tile(xT, s0 + q), writes=[(xk, q)])
        pa_ps = [b.psum() for _ in range(nsub)]
        for k in range(KCH):
            for q in range(nsub):
                b.mm(pa_ps[q][0][0:64, :SEG], wa[:, k, 0:64], xs[:, k, q * SEG:(q + 1) * SEG], k == 0, k == KCH - 1, ["wa", (xk, q)], [pa_ps[q][1]])
        for q in range(nsub):
            b.act(axb[:, 3 + q * SEG:3 + (q + 1) * SEG], pa_ps[q][0][0:64, :SEG], AF.Identity, [pa_ps[q][1], "pa"], ["axb"], bias=pa[:, 0:1])
        pg_ps = [b.psum() for _ in range(nsub)]
        for k in range(KCH):
            for q in range(nsub):
                b.mm(pg_ps[q][0][0:64, :SEG], wa[:, k, 64:128], xs[:, k, q * SEG:(q + 1) * SEG], k == 0, k == KCH - 1, ["wa", (xk, q)], [pg_ps[q][1]])
        for q in range(nsub):
            b.act(sgs[:, q * SEG:(q + 1) * SEG], pg_ps[q][0][0:64, :SEG], AF.Silu, [pg_ps[q][1], "pa"], [("sgs", q)], bias=pa[:, 1:2])
        for q0 in range(0, nsub, NWA):
            gens = [seg_steps(s0 + q, q % NWA) for q in range(q0, min(q0 + NWA, nsub))]
            for _ in itertools.zip_longest(*gens):
                pass
    return [("ygd", s) for s in range(nseg)]


def build_p1a(T=SEQ, SEG=512):
    b = B()
    st = contextlib.ExitStack()
    keys = emit_p1a(b, st, b.din("xTt", [T // 512, 128, KCH * 512]), b.din("wa", [D_MODEL, 128]), b.din("pa", [64, 16]),
                    b.din("gw", [64, 128]), b.dout("ygA", [64, T]), T, SEG)
    return b.finish(keys, extra=[st]), b.outs


def host_p1a(inp, l, xT):
    sp = np.cumsum([0, 512, 512])
    w_in = inp["w_in"][l]
    b_in = inp["b_in"][l]
    maps = []
    for c in range(8):
        cs = slice(64 * c, 64 * c + 64)
        wa = np.concatenate([w_in[:, 0:512][:, cs], w_in[:, 512:1024][:, cs]], axis=1)
        pa = np.zeros((64, 16), np.float32)
        pa[:, 0] = b_in[0:512][cs]
        pa[:, 1] = b_in[512:1024][cs]
        pa[:, 2] = inp["lru_conv_b"][l][cs]
        pa[:, 3:7] = inp["lru_conv_w"][l][:, cs].T
        pa[:, 7] = inp["lru_gate_a_b"][l][cs]
        pa[:, 8] = inp["lru_gate_x_b"][l][cs]
        pa[:, 9] = inp["lru_lambda"][l][cs]
        gw = np.concatenate([inp["lru_gate_a_w"][l][c], inp["lru_gate_x_w"][l][c]], axis=1)
        maps.append({"xTt": xT, "wa": np.ascontiguousarray(wa), "pa": pa, "gw": np.ascontiguousarray(gw)})
    return maps


NQ = 4096
NKV = 6144
HAL = 2048
DILS = (1, 4, 16)
N_BUCKETS = 32
MAX_DISTANCE = 2048


def emit_p1b(b, st, ones, ident, xTh, wq_d, wk_d, wv_d, wg_d, pb_d, bias_d, yg_d):
    S = b.S
    b.pfx = "b_"
    onesb = b.sb("onesb", [128, 128], BF16, st=st)
    b.cp(onesb[:], ones[:, 0:128], ["c_ones"], ["onesb"])
    pb = b.sb("pb", [128, 16], st=st); b.load(pb[:], pb_d, "pb")
    eb = b.sb("eb", [128, 3, 2, 128], st=st); ebf = b.sb("ebf", [128, 3, 2, 128], st=st)
    b.load(eb[:], bias_d.rearrange("p (g k q) -> p g k q", g=3, k=2), "eb")
    b.act(eb[:], eb[:], AF.Exp, ["eb"], ["eb"])
    b.cp(ebf[:], eb[:], ["eb"], ["ebf"])
    for g in range(3):
        b.ts(ebf[:, g, 0, :], ebf[:, g, 0, :], pb[:, 10:11], None, ALU.mult, None, ["ebf", "pb"], ["ebf"])
    wq = b.sb("wq", [128, KCH, 384], BF16, st=st); wk = b.sb("wk", [128, KCH, 384], BF16, st=st); wv = b.sb("wv", [128, KCH, 384], BF16, st=st)
    wg = b.sb("wg", [128, KCH, 128], BF16, st=st)
    S.dma("pool", wk[:], w_ap(wk_d), writes=["wk"]); S.dma("pool", wv[:], w_ap(wv_d), writes=["wv"])
    S.dma("pool", wq[:], w_ap(wq_d), writes=["wq"]); S.dma("pool", wg[:], w_ap(wg_d), writes=["wg"])
    xb = [b.sb("xb%d" % i, [128, KCH, 512], BF16, st=st) for i in range(2)]
    KT = b.sb("KT", [128, NKV], BF16, st=st)
    VT = b.sb("VT", [128, NKV], F32, st=st)
    QT = b.sb("QT", [128, NQ], BF16, st=st)
    bg = b.sb("bg", [128, NQ], F32, st=st)
    ND = b.sb("ND", [128, 2, NQ], F32, st=st)
    esb = [b.sb("esb%d" % i, [128, 256], F32, st=st) for i in range(2)]
    PT = [b.sb("PT%d" % i, [128, 2, 128], BF16, st=st) for i in range(2)]
    Vt = [b.sb("Vt%d" % i, [128, 2, 128], BF16, st=st) for i in range(2)]
    scale = 128.0 ** -0.5
    xi = 0
    bi = 0
    for g, dil in enumerate(DILS):
        start = HAL - max(512, 128 * dil)
        tl_all = list(range(start, NKV, 512))
        for pi in range(0, len(tl_all), 1):
            pair = tl_all[pi:pi + 1]
            xss = []
            for t0 in pair:
                xs = xb[xi % 2]; xk = ("xb", xi % 2); xi += 1
                S.dma("pool", xs[:], x_tile(xTh, t0 // 512), writes=[xk])
                xss.append((xs, xk, t0))
            secs = [("k", wk, "wk"), ("v", wv, "wv"), ("q", wq, "wq")] + ([("g", wg, "wg")] if g == 0 else [])
            for nm, wt, wkey in secs:
                xsel = xss if nm in ("k", "v") else [t for t in xss if t[2] >= HAL]
                if not xsel:
                    continue
                pss = [b.psum() for _ in xsel]
                for k in range(KCH):
                    for (ps_, pk_), (xs, xk, t0) in zip(pss, xsel):
                        lhs = wt[:, k, :] if nm == "g" else wt[:, k, g * 128:(g + 1) * 128]
                        b.mm(ps_[:, :], lhs, xs[:, k, :], k == 0, k == KCH - 1, [wkey, xk], [pk_])
                for (ps_, pk_), (xs, xk, t0) in zip(pss, xsel):
                    if nm == "k":
                        b.act(KT[:, t0:t0 + 512], ps_[:, :], AF.Identity, [pk_, "pb"], [("KT", t0)], bias=pb[:, 3 + g:4 + g])
                    elif nm == "v":
                        b.ts(VT[:, t0:t0 + 512], ps_[:, :], pb[:, 6 + g:7 + g], None, ALU.add, None, [pk_, "pb"], [("VT", t0)])
                    elif nm == "q":
                        b.act(QT[:, t0 - HAL:t0 - HAL + 512], ps_[:, :], AF.Identity, [pk_, "pb"], [("QT", t0 - HAL)], bias=pb[:, g:g + 1])
                    else:
                        b.act(bg[:, t0 - HAL:t0 - HAL + 512], ps_[:, :], AF.Silu, [pk_, "pb"], ["bg"], bias=pb[:, 9:10])
        span = 128 * dil
        allK = [("KT", t) for t in range(start, NKV, 512)]
        allV = [("VT", t) for t in range(start, NKV, 512)]
        allQ = [("QT", t) for t in range(0, NQ, 512)]

        def tiles_of(keys, base, lo, hi):
            return [(keys, t) for t in range(base, 99999, 512) if t < hi and t + 512 > lo]
        for r in range(dil):
            for nl in range(32 // dil):
                qs = r + span * nl
                ssl = lambda a: slice(a, a + 127 * dil + 1, dil)
                qsl = ssl(qs)
                kc_ = ssl(HAL + qs)
                kp_ = ssl(HAL + qs - span)
                qk = [("QT", t) for t in range(0, NQ, 512) if t < qs + span and t + 512 > qs]
                kk_ = [("KT", t) for t in range(start, NKV, 512) if t < HAL + qs + span and t + 512 > HAL + qs - span]
                vk_ = [("VT", t) for t in range(start, NKV, 512) if t < HAL + qs + span and t + 512 > HAL + qs - span]
                psl, kl = b.psum()
                b.mm(psl[:, 0:128], KT[:, kp_], QT[:, qsl], True, True, kk_ + qk, [kl])
                b.mm(psl[:, 128:256], KT[:, kc_], QT[:, qsl], True, True, kk_ + qk, [kl])
                e_ = esb[bi % 2]; ek = ("esb", bi % 2)
                b.act(e_[:], psl[:, 0:256], AF.Exp, [kl], [ek], scale=scale)
                p_ = PT[bi % 2]; pk_ = ("PT", bi % 2)
                ebsel = ebf if nl == 0 else eb
                b.tt(p_[:].rearrange("p a q -> p (a q)"), e_[:], ebsel[:, g].rearrange("p a q -> p (a q)"), ALU.mult, [ek, "eb", "ebf"], [pk_])
                psv, kv = b.psum()
                b.tr(psv[:, 0:128], VT[:, kp_], ident[:], vk_ + ["c_ident"], [kv])
                b.tr(psv[:, 128:256], VT[:, kc_], ident[:], vk_ + ["c_ident"], [kv])
                v_ = Vt[bi % 2]; vk2 = ("Vt", bi % 2)
                b.cp(v_[:].rearrange("p a q -> p (a q)"), psv[:, 0:256], [kv], [vk2], eng="act")
                pso, ko = b.psum()
                b.mm(pso[:, 0:128], v_[:, 0, :], p_[:, 0, :], True, False, [vk2, pk_], [ko])
                b.mm(pso[:, 0:128], v_[:, 1, :], p_[:, 1, :], False, True, [vk2, pk_], [ko])
                b.mm(pso[:, 128:256], onesb[:], p_[:, 0, :], True, False, ["onesb", pk_], [ko])
                b.mm(pso[:, 128:256], onesb[:], p_[:, 1, :], False, True, ["onesb", pk_], [ko])
                src = pso[:, 0:256].rearrange("p (a q) -> p a q", a=2)
                if g == 0:
                    b.cp(ND[:, :, qsl], src, [ko], ["ND"], eng="act")
                else:
                    b.tt(ND[:, :, qsl], ND[:, :, qsl], src, ALU.add, ["ND", ko], ["ND"])
                bi += 1
    S.op("dve", lambda e: e.reciprocal(out=ND[:, 1, :], in_=ND[:, 1, :]), ["ND"], ["ND"])
    b.tt(ND[:, 0, :], ND[:, 0, :], ND[:, 1, :], ALU.mult, ["ND"], ["ND"])
    b.tt(ND[:, 0, :], ND[:, 0, :], bg[:], ALU.mult, ["ND", "bg"], ["ND"], eng="pool")
    S.dma("sp", yg_d, ND[:, 0, :], reads=["ND"], writes=["outB"])
    return ["outB"]


def p1b_dram(b):
    return (b.din("xTht", [NKV // 512, 128, KCH * 512]), b.din("wq", [D_MODEL, 384]), b.din("wk", [D_MODEL, 384]), b.din("wv", [D_MODEL, 384]),
            b.din("wg", [D_MODEL, 128]), b.din("pb", [128, 16]), b.din("biasT", [128, 3 * 2 * 128]), b.dout("ygB", [128, NQ]))


def build_p1b():
    b = B()
    ones, ident = b.ident()
    st = contextlib.ExitStack()
    keys = emit_p1b(b, st, ones, ident, *p1b_dram(b))
    return b.finish(keys, extra=[st]), b.outs


def t5_bucket(dist):
    import math
    max_exact = N_BUCKETS // 2
    large = max_exact + (np.log(np.maximum(dist, 1) / max_exact) / math.log(MAX_DISTANCE / max_exact)
                         * (N_BUCKETS - max_exact)).astype(np.int32)
    large = np.minimum(large, N_BUCKETS - 1)
    return np.where(dist < max_exact, dist, large).astype(np.int32)


def host_p1b(inp, l, xT):
    w_in = inp["w_in"][l]; b_in = inp["b_in"][l]
    Q0 = 1024; K0 = Q0 + 1536; V0 = K0 + 1536; G0 = V0 + 1536
    table = inp["att_rel_bias"]
    ki = np.arange(128)[:, None, None]; kb = np.arange(2)[None, :, None]; qi = np.arange(128)[None, None, :]
    dist = qi + 128 - (kb * 128 + ki)
    valid = (dist >= 0) & (dist <= 128)
    maps = []
    for c in range(8):
        hm, half = c // 2, c % 2
        cols = lambda base: np.concatenate([w_in[:, base + (g * 4 + hm) * 128: base + (g * 4 + hm + 1) * 128] for g in range(3)], axis=1)
        pb = np.zeros((128, 16), np.float32)
        for g in range(3):
            hh = (g * 4 + hm) * 128
            pb[:, g] = b_in[Q0 + hh:Q0 + hh + 128]
            pb[:, 3 + g] = b_in[K0 + hh:K0 + hh + 128]
            pb[:, 6 + g] = b_in[V0 + hh:V0 + hh + 128]
        pb[:, 9] = b_in[G0 + hm * 128:G0 + hm * 128 + 128]
        pb[:, 10] = float(half)
        bT = np.zeros((128, 3, 2, 128), np.float32)
        for g, dil in enumerate(DILS):
            bucket = t5_bucket(np.clip(dist, 0, 128) * dil)
            bT[:, g] = np.where(valid, table[bucket, g * 4 + hm], np.float32(-100.0))
        xh = np.zeros((2048, NKV), np.float32)
        base = half * NQ
        if half == 0:
            xh[:, HAL:] = xT[:, 0:NQ]
        else:
            xh[:] = xT[:, base - HAL:base + NQ]
        maps.append({"xTht": tile_xT(xh), "wq": np.ascontiguousarray(cols(Q0)), "wk": np.ascontiguousarray(cols(K0)),
                     "wv": np.ascontiguousarray(cols(V0)), "wg": np.ascontiguousarray(w_in[:, G0 + hm * 128:G0 + (hm + 1) * 128]),
                     "pb": pb, "biasT": bT.reshape(128, 768)})
    return maps


def gather_p1b(results):
    yg = np.zeros((512, 8192), np.float32)
    for c, r in enumerate(results):
        hm, half = c // 2, c % 2
        yg[hm * 128:(hm + 1) * 128, half * NQ:(half + 1) * NQ] = r["ygB"]
    return yg


SEGC = 1024
NW = 4
GN_EPS = 64e-5
DEC_C = float(np.exp(-0.5))


def emit_p1c(b, st, ones, ident, xT, wc_d, pc_d, wup_d, gnb_d, yg_d, T=SEQ):
    S = b.S
    b.pfx = "c_"
    nseg = T // SEGC
    NCH = SEGC // 128
    pc = b.sb("pc", [64, 32], st=st); b.load(pc[:], pc_d, "pc")
    wup = b.sb("wup", [64, 128], st=st); b.load(wup[:], wup_d, "wup")
    gnb = b.sb("gnb", [128, 192], st=st); b.load(gnb[:], gnb_d, "gnb")
    wc = b.sb("wc", [128, KCH, 384], BF16, st=st)
    S.dma("pool", wc[:], w_ap(wc_d), writes=["wc"])
    pcx = b.sb("pcx", [64, 4], st=st)
    b.ts(pcx[:, 0:1], pc[:, 13:14], -1.0, 1.0, ALU.mult, ALU.add, ["pc"], ["pcx"])
    mask2 = b.sb("mask2", [128, 2, 128], st=st); maskL = b.sb("maskL", [128, 128], st=st); maskc = b.sb("maskc", [64, SEGC], st=st)
    S.op("pool", lambda e: e.affine_select(out=mask2[:, 0, :], in_=ones[:, 0:128], pattern=[[1, 128]], compare_op=ALU.is_gt,
                                           fill=0.0, base=0, channel_multiplier=-1), ["c_ones"], ["mask2"])
    S.op("pool", lambda e: e.affine_select(out=mask2[:, 1, :], in_=ones[:, 0:128], pattern=[[1, 128]], compare_op=ALU.is_ge,
                                           fill=0.0, base=0, channel_multiplier=-1), ["c_ones"], ["mask2"])
    S.op("pool", lambda e: e.affine_select(out=maskL[:], in_=ones[:, 0:128], pattern=[[-1, 128]], compare_op=ALU.is_gt,
                                           fill=0.0, base=0, channel_multiplier=1), ["c_ones"], ["maskL"])
    b.memset(maskc[:], 1.0, ["maskc"])
    b.memset(maskc[:, 0:SEGC - 127:128], 0.0, ["maskc"])

    xb = [b.sb("xb%d" % i, [128, KCH, 512], BF16, st=st) for i in range(2)]
    raw = [b.sb("raw%d" % i, [64, SEGC + 1], st=st) for i in range(5)]
    sh = [b.sb("sh%d" % i, [64, SEGC], st=st) for i in range(5)]
    names = ["lw", "aicl", "kk", "kc", "bb", "cum", "tmp1", "tmp2", "bt", "kt", "bh", "kh", "rkr"]
    A = {n: b.sb(n, [64, SEGC], st=st) for n in names}
    atrt = b.sb("atrt", [64, 2, SEGC], st=st)
    wl = b.sb("wl", [64, NCH], st=st)
    sgc = b.sb("sgc", [64, SEGC], st=st)
    obuf = [b.sb("obuf%d" % i, [64, SEGC], st=st) for i in range(2)]
    ST = [b.sb("ST%d" % i, [64, 64], st=st) for i in range(2)]
    tok4 = [b.sb("tok4_%d" % i, [128, 5, 64], st=st) for i in range(NW)]
    AB = [b.sb("AB%d" % i, [128, 2, 128], st=st) for i in range(NW)]
    AK = [b.sb("AK%d" % i, [128, 2, 128], st=st) for i in range(NW)]
    Mb = [[b.sb("Mb%d_%d" % (q, i), [128, 128], st=st) for i in range(2)] for q in range(NW)]
    MTb = [[b.sb("MTb%d_%d" % (q, i), [128, 128], st=st) for i in range(2)] for q in range(NW)]
    TTb = [[b.sb("TTb%d_%d" % (q, i), [128, 128], st=st) for i in range(2)] for q in range(NW)]
    Xsb = [b.sb("Xsb%d" % q, [128, 64], st=st) for q in range(NW)]
    PQ = [b.sb("PQ%d" % i, [128, 2, 64], st=st) for i in range(NW)]
    GT = [b.sb("GT%d" % q, [64, 64], st=st) for q in range(NW)]; Hs = [b.sb("Hs%d" % q, [64, 64], st=st) for q in range(NW)]
    R2T = [b.sb("R2T%d" % q, [64, 128], st=st) for q in range(NW)]
    bst = [b.sb("bst%d" % q, [128, 6], st=st) for q in range(NW)]; mv = [b.sb("mv%d" % q, [128, 4], st=st) for q in range(NW)]
    bsc = [b.sb("bsc%d" % q, [128, 1], st=st) for q in range(NW)]
    yn = [b.sb("yn%d" % i, [128, 64], st=st) for i in range(NW)]

    b.memset(ST[0][:], 0.0, [("ST", 0)])
    for g in range(5):
        b.memset(raw[g][:, 0:1], 0.0, [("raw", g)])
    xi = 0
    stc = 0
    outk = []
    for s in range(nseg):
        NTL = SEGC // 512

        def load_x(seg):
            nonlocal xi
            out = []
            for tl in range(NTL):
                xs = xb[xi % 2]; xk = ("xb", xi % 2); xi += 1
                S.dma("pool", xs[:], x_tile(xT, seg * NTL + tl), writes=[xk])
                out.append((xs, xk))
            return out
        xss = x_next if s > 0 else load_x(0)
        for g in range(6):
            pss = [b.psum() for _ in range(NTL)]
            for k in range(KCH):
                for tl in range(NTL):
                    b.mm(pss[tl][0][0:64, :], wc[:, k, g * 64:(g + 1) * 64], xss[tl][0][:, k, :], k == 0, k == KCH - 1, ["wc", xss[tl][1]], [pss[tl][1]])
            for tl in range(NTL):
                ps, pk = pss[tl]
                if g < 5:
                    b.act(raw[g][:, 1 + tl * 512:1 + (tl + 1) * 512], ps[0:64, :], AF.Identity, [pk, "pc"], [("raw", g)], bias=pc[:, g:g + 1])
                else:
                    b.act(sgc[:, tl * 512:(tl + 1) * 512], ps[0:64, :], AF.Silu, [pk, "pc"], ["sgc"], bias=pc[:, 16:17])
        x_next = load_x(s + 1) if s + 1 < nseg else None
        for g in range(5):
            b.tt(sh[g][:], raw[g][:, 0:SEGC], raw[g][:, 1:SEGC + 1], ALU.subtract, [("raw", g)], [("sh", g)])
            b.stt(sh[g][:], sh[g][:], pc[:, 5 + g:6 + g], raw[g][:, 1:SEGC + 1], ALU.mult, ALU.add, [("sh", g), "pc", ("raw", g)], [("sh", g)])
            b.cp(raw[g][:, 0:1], raw[g][:, SEGC:SEGC + 1], [("raw", g)], [("raw", g)], eng="pool")
        s_r, s_k, s_v, s_w, s_a = sh
        kr, kk_, kv_, kw_, ka_ = [("sh", g) for g in range(5)]
        b.act(s_w[:], s_w[:], AF.Tanh, [kw_], [kw_])
        for tl in range(2):
            sl = slice(tl * 512, (tl + 1) * 512)
            ps, pk = b.psum()
            b.mm(ps[0:64, :], wup[:, 0:64], s_w[:, sl], True, True, ["wup", kw_], [pk])
            b.act(A["lw"][:, sl], ps[0:64, :], AF.Sigmoid, [pk, "pc"], ["lw"], bias=pc[:, 10:11])
            ps, pk = b.psum()
            b.mm(ps[0:64, :], wup[:, 64:128], s_a[:, sl], True, True, ["wup", ka_], [pk])
            b.act(A["aicl"][:, sl], ps[0:64, :], AF.Sigmoid, [pk, "pc"], ["aicl"], bias=pc[:, 11:12])
        b.ts(A["lw"][:], A["lw"][:], -DEC_C, None, ALU.mult, None, ["lw"], ["lw"])
        b.ts(A["kk"][:], s_k[:], pc[:, 12:13], None, ALU.mult, None, [kk_, "pc"], ["kk"])
        b.tt(A["tmp1"][:], A["kk"][:], A["kk"][:], ALU.mult, ["kk"], ["tmp1"])
        for tl in range(2):
            sl = slice(tl * 512, (tl + 1) * 512)
            ps, pk = b.psum()
            b.mm(ps[0:64, :], ones[0:64, 0:64], A["tmp1"][:, sl], True, True, ["c_ones", "tmp1"], [pk])
            b.act(A["tmp2"][:, sl], ps[0:64, :], AF.Sqrt, [pk], ["tmp2"])
        b.ts(A["tmp2"][:], A["tmp2"][:], 1e-12, None, ALU.max, None, ["tmp2"], ["tmp2"])
        S.op("dve", lambda e: e.reciprocal(out=A["tmp2"][:], in_=A["tmp2"][:]), ["tmp2"], ["tmp2"])
        b.tt(A["kk"][:], A["kk"][:], A["tmp2"][:], ALU.mult, ["kk", "tmp2"], ["kk"])
        b.ts(A["tmp1"][:], A["aicl"][:], pc[:, 13:14], pcx[:, 0:1], ALU.mult, ALU.add, ["aicl", "pc", "pcx"], ["tmp1"])
        b.tt(A["kc"][:], s_k[:], A["tmp1"][:], ALU.mult, [kk_, "tmp1"], ["kc"])
        b.tt(A["bb"][:], A["kk"][:], A["aicl"][:], ALU.mult, ["kk", "aicl"], ["bb"], eng="pool")
        S.op("dve", lambda e: e.tensor_tensor_scan(out=A["cum"][:], data0=maskc[:], data1=A["lw"][:], initial=0.0, op0=ALU.mult, op1=ALU.add),
             ["maskc", "lw"], ["cum"])
        b.tt(A["tmp1"][:], A["cum"][:], A["lw"][:], ALU.subtract, ["cum", "lw"], ["tmp1"])
        b.act(A["tmp1"][:], A["tmp1"][:], AF.Exp, ["tmp1"], ["tmp1"])
        b.stt(atrt[:, 0, :], A["kk"][:], -1.0, A["tmp1"][:], ALU.mult, ALU.mult, ["kk", "tmp1"], ["atrt"])
        b.act(A["tmp2"][:], A["cum"][:], AF.Exp, ["cum"], ["tmp2"])
        b.tt(atrt[:, 1, :], s_r[:], A["tmp2"][:], ALU.mult, [kr, "tmp2"], ["atrt"])
        b.act(A["tmp2"][:], A["cum"][:], AF.Exp, ["cum", "atrt"], ["tmp2"], scale=-1.0)
        b.tt(A["bt"][:], A["bb"][:], A["tmp2"][:], ALU.mult, ["bb", "tmp2"], ["bt"])
        b.tt(A["kt"][:], A["kc"][:], A["tmp2"][:], ALU.mult, ["kc", "tmp2"], ["kt"], eng="pool")
        for c in range(NCH):
            cs = slice(c * 128, (c + 1) * 128)
            b.act(A["tmp1"][:, cs], A["cum"][:, cs], AF.Exp, ["cum", "atrt"], ["tmp1"], scale=-1.0, bias=A["cum"][:, c * 128 + 127:c * 128 + 128])
        b.tt(A["bh"][:], A["bb"][:], A["tmp1"][:], ALU.mult, ["bb", "tmp1"], ["bh"])
        b.tt(A["kh"][:], A["kc"][:], A["tmp1"][:], ALU.mult, ["kc", "tmp1"], ["kh"], eng="pool")
        b.act(wl[:], A["cum"][:, 127:SEGC:128], AF.Exp, ["cum"], ["wl"])
        b.stt(A["rkr"][:], s_r[:], pc[:, 15:16], A["kc"][:], ALU.mult, ALU.mult, [kr, "pc", "kc"], ["rkr"])

        ob = obuf[s % 2]; obk = ("obuf", s % 2)
        def chunk_steps(c, p):
            nonlocal stc
            cs = slice(c * 128, (c + 1) * 128)
            t4 = tok4[p]; t4k = ("tok4", p)
            psT, kT = b.psum()
            b.tr(psT[:, 0:64], atrt[:, 0, cs], ident[0:64, 0:64], ["atrt", "c_ident"], [kT])
            b.tr(psT[:, 64:128], s_v[:, cs], ident[0:64, 0:64], [kv_, "c_ident"], [kT])
            b.tr(psT[:, 128:192], A["bh"][:, cs], ident[0:64, 0:64], ["bh", "c_ident"], [kT])
            b.tr(psT[:, 192:256], A["kh"][:, cs], ident[0:64, 0:64], ["kh", "c_ident"], [kT])
            b.tr(psT[:, 256:320], sgc[:, cs], ident[0:64, 0:64], ["sgc", "c_ident"], [kT])
            b.cp(t4[:].rearrange("p a q -> p (a q)"), psT[:, 0:320], [kT], [t4k], eng="act")
            yield
            ab = AB[p]; abk = ("AB", p); ak = AK[p]; akk = ("AK", p)
            ps1, k1 = b.psum()
            b.mm(ps1[:, 0:256], A["bt"][:, cs], atrt[:, :, cs], True, True, ["bt", "atrt"], [k1])
            b.tt(ab[:].rearrange("p a q -> p (a q)"), ps1[:, 0:256], mask2[:].rearrange("p a q -> p (a q)"), ALU.mult, [k1, "mask2"], [abk])
            yield
            ps2, k2 = b.psum()
            b.mm(ps2[:, 0:256], A["kt"][:, cs], atrt[:, :, cs], True, True, ["kt", "atrt"], [k2])
            b.tt(ak[:].rearrange("p a q -> p (a q)"), ps2[:, 0:256], mask2[:].rearrange("p a q -> p (a q)"), ALU.mult, [k2, "mask2"], [akk])
            yield
            ps3, k3 = b.psum()
            b.mm(ps3[:, 0:128], atrt[:, 0, cs], A["bt"][:, cs], True, True, ["bt", "atrt"], [k3])
            b.tt(Mb[p][0][:], ps3[:, 0:128], maskL[:], ALU.mult, [k3, "maskL"], [("Mb", p, 0)])
            b.tt(TTb[p][0][:], ab[:, 0, :], ident[:], ALU.add, [abk, "c_ident"], [("TTb", p, 0)], eng="pool")
            yield
            Mc, Mck = Mb[p][0], ("Mb", p, 0)
            MTc, MTck = ab[:, 0, :], abk
            for lev in range(1, 7):
                Mn, Mnk = Mb[p][lev % 2], ("Mb", p, lev % 2)
                psm, km = b.psum()
                b.mm(psm[:, 0:128], MTc, Mc[:], True, True, [MTck, Mck], [km])
                if lev < 6:
                    MTn, MTnk = MTb[p][lev % 2], ("MTb", p, lev % 2)
                    psn, kn = b.psum()
                    b.mm(psn[:, 0:128], Mc[:], MTc, True, True, [MTck, Mck], [kn])
                b.cp(Mn[:], psm[:, 0:128], [km], [Mnk], eng="act")
                if lev < 6:
                    b.cp(MTn[:], psn[:, 0:128], [kn], [MTnk], eng="dve")
                yield
                pst, kt_ = b.psum()
                b.mm(pst[:, 0:128], Mn[:], TTb[p][(lev - 1) % 2][:], True, True, [Mnk, ("TTb", p, (lev - 1) % 2)], [kt_])
                b.tt(TTb[p][lev % 2][:], TTb[p][(lev - 1) % 2][:], pst[:, 0:128], ALU.add, [("TTb", p, (lev - 1) % 2), kt_], [("TTb", p, lev % 2)])
                yield
                Mc, Mck = Mn, Mnk
                if lev < 6:
                    MTc, MTck = MTn[:], MTnk
            TT, TTk = TTb[p][0], ("TTb", p, 0)
            pq = PQ[p]; pqk = ("PQ", p)
            psP, kP = b.psum()
            b.mm(psP[:, 0:64], TT[:], t4[:, 0, :], True, True, [TTk, t4k], [kP])
            psX, kX = b.psum()
            b.mm(psX[:, 0:64], ak[:, 0, :], t4[:, 1, :], True, True, [akk, t4k], [kX])
            b.cp(Xsb[p][:], psX[:, 0:64], [kX], [("Xsb", p)], eng="act")
            yield
            b.mm(psP[:, 64:128], TT[:], Xsb[p][:], True, True, [TTk, ("Xsb", p)], [kP])
            b.cp(pq[:].rearrange("p a q -> p (a q)"), psP[:, 0:128], [kP], [pqk], eng="dve")
            yield
            psg, kg = b.psum()
            b.mm(psg[0:64, 0:64], pq[:, 0, :], t4[:, 2, :], True, True, [pqk, t4k], [kg])
            b.stt(GT[p][:], ident[0:64, 0:64], wl[:, c:c + 1], psg[0:64, 0:64], ALU.mult, ALU.add, ["c_ident", "wl", kg], [("GT", p)])
            psh, kh_ = b.psum()
            b.mm(psh[0:64, 0:64], t4[:, 2, :], pq[:, 1, :], True, False, [pqk, t4k], [kh_])
            b.mm(psh[0:64, 0:64], t4[:, 3, :], t4[:, 1, :], False, True, [t4k], [kh_])
            b.cp(Hs[p][:], psh[0:64, 0:64], [kh_], [("Hs", p)], eng="act")
            yield
            psr, kr_ = b.psum()
            b.mm(psr[0:64, 0:128], pq[:, 0, :], ab[:, 1, :], True, True, [pqk, abk], [kr_])
            b.tt(R2T[p][:], psr[0:64, 0:128], atrt[:, 1, cs], ALU.add, [kr_, "atrt"], [("R2T", p)])
            psb, kb = b.psum()
            b.mm(psb[:, 0:1], A["rkr"][:, cs], ones[0:64, 0:1], True, True, ["rkr", "c_ones"], [kb])
            b.cp(bsc[p][:], psb[:, 0:1], [kb], [("bsc", p)], eng="act")
            yield
            stcur, stk = ST[stc % 2], ("ST", stc % 2)
            stnew, stnk = ST[(stc + 1) % 2], ("ST", (stc + 1) % 2)
            stc += 1
            psy, ky = b.psum()
            b.mm(psy[:, 0:64], R2T[p][:], stcur[:], True, False, [("R2T", p), stk], [ky])
            b.mm(psy[:, 0:64], ab[:, 1, :], pq[:, 1, :], False, False, [abk, pqk], [ky])
            b.mm(psy[:, 0:64], ak[:, 1, :], t4[:, 1, :], False, True, [akk, t4k], [ky])
            pss, ks = b.psum()
            b.mm(pss[0:64, 0:64], GT[p][:], stcur[:], True, True, [("GT", p), stk], [ks])
            b.tt(stnew[:], pss[0:64, 0:64], Hs[p][:], ALU.add, [ks, ("Hs", p)], [stnk])
            S.op("dve", lambda e, psy=psy: e.bn_stats(out=bst[p][:], in_=psy[:, 0:64]), [ky], [("bst", p)])
            S.op("dve", lambda e: e.bn_aggr(out=mv[p][:, 0:2], in_=bst[p][:]), [("bst", p)], [("mv", p)])
            yield
            b.act(mv[p][:, 2:3], mv[p][:, 1:2], AF.Sqrt, [("mv", p)], [("mv2", p)], bias=GN_EPS)
            S.op("dve", lambda e: e.reciprocal(out=mv[p][:, 2:3], in_=mv[p][:, 2:3]), [("mv2", p)], [("mv2", p)])
            b.stt(mv[p][:, 3:4], mv[p][:, 0:1], -1.0, mv[p][:, 2:3], ALU.mult, ALU.mult, [("mv", p), ("mv2", p)], [("mv3", p)])
            y_ = yn[p]; ynk = ("yn", p)
            b.act(y_[:], psy[:, 0:64], AF.Identity, [ky, ("mv2", p), ("mv3", p)], [ynk], bias=mv[p][:, 3:4], scale=mv[p][:, 2:3])
            yield
            b.tt(y_[:], y_[:], gnb[:, 0:64], ALU.mult, [ynk, "gnb"], [ynk])
            b.tt(y_[:], y_[:], gnb[:, 64:128], ALU.add, [ynk, "gnb"], [ynk], eng="pool")
            b.stt(y_[:], t4[:, 1, :], bsc[p][:, 0:1], y_[:], ALU.mult, ALU.add, [t4k, ("bsc", p), ynk], [ynk])
            b.tt(y_[:], y_[:], t4[:, 4, :], ALU.mult, [ynk, t4k], [ynk], eng="pool")
            pso, ko = b.psum()
            b.tr(pso[0:64, 0:128], y_[:], ident[:], [ynk, "c_ident"], [ko])
            b.cp(ob[:, cs], pso[0:64, 0:128], [ko], [obk], eng="act")
            yield

        import itertools
        for c0 in range(0, NCH, NW):
            gens = [chunk_steps(c0 + q, q) for q in range(min(NW, NCH - c0))]
            for _ in itertools.zip_longest(*gens):
                pass
        S.dma("sp", yg_d[:, s * SEGC:(s + 1) * SEGC], ob[:], reads=[obk], writes=[("outC", s)])
        outk.append(("outC", s))
    return outk


def p1c_dram(b, T=SEQ):
    return (b.din("wc", [D_MODEL, 384]), b.din("pc", [64, 32]), b.din("wup", [64, 128]), b.din("gnb", [128, 192]), b.dout("ygC", [64, T]))


def build_p1c(T=SEQ):
    b = B()
    ones, ident = b.ident()
    st = contextlib.ExitStack()
    keys = emit_p1c(b, st, ones, ident, b.din("xTt", [T // 512, 128, KCH * 512]), *p1c_dram(b, T), T=T)
    return b.finish(keys, extra=[st]), b.outs


def host_p1c(inp, l, xT):
    w_in = inp["w_in"][l]; b_in = inp["b_in"][l]
    C0 = 1024 + 5120
    mu = inp["rwkv_mu"][l]
    maps = []
    for c in range(8):
        hs = slice(64 * c, 64 * c + 64)
        secs = [(C0, hs), (C0 + 512, hs), (C0 + 1024, hs), (C0 + 1536, slice(0, 64)), (C0 + 1600, slice(0, 64)), (C0 + 1664, hs)]
        wc = np.concatenate([w_in[:, o:o + 512][:, sl_] if sl_ is hs else w_in[:, o:o + 64] for o, sl_ in secs], axis=1)
        pc = np.zeros((64, 32), np.float32)
        pc[:, 0] = b_in[C0:C0 + 512][hs]; pc[:, 1] = b_in[C0 + 512:C0 + 1024][hs]; pc[:, 2] = b_in[C0 + 1024:C0 + 1536][hs]
        pc[:, 3] = b_in[C0 + 1536:C0 + 1600]; pc[:, 4] = b_in[C0 + 1600:C0 + 1664]
        pc[:, 5] = mu[0:512][hs]; pc[:, 6] = mu[512:1024][hs]; pc[:, 7] = mu[1024:1536][hs]
        pc[:, 8] = mu[1536:1600]; pc[:, 9] = mu[1600:1664]
        pc[:, 10] = inp["rwkv_w0"][l][hs]; pc[:, 11] = inp["rwkv_a0"][l][hs]
        pc[:, 12] = inp["rwkv_k_k"][l][hs]; pc[:, 13] = inp["rwkv_k_a"][l][hs]
        pc[:, 15] = inp["rwkv_r_k"][l][c]
        pc[:, 16] = b_in[C0 + 1664:C0 + 2176][hs]
        wup = np.concatenate([inp["rwkv_w_up"][l][:, hs], inp["rwkv_a_up"][l][:, hs]], axis=1)
        gn = np.concatenate([inp["rwkv_gn_g"][l][hs], inp["rwkv_gn_b"][l][hs], b_in[C0 + 1664:C0 + 2176][hs]])
        gnb = np.ascontiguousarray(np.broadcast_to(gn[None, :], (128, 192)))
        maps.append({"xTt": xT, "wc": np.ascontiguousarray(wc), "pc": pc, "wup": np.ascontiguousarray(wup), "gnb": gnb})
    return maps


def build_p1():
    b = B()
    S = b.S
    ones, ident = b.ident()
    xT = b.din("xTt", [SEQ // 512, 128, KCH * 512])
    a_dram = (b.din("wa", [D_MODEL, 128]), b.din("pa", [64, 16]), b.din("gw", [64, 128]), b.dout("ygA", [64, SEQ]))
    b_dram = p1b_dram(b)
    c_dram = p1c_dram(b)
    keys = []
    st = contextlib.ExitStack()
    keys += emit_p1a(b, st, xT, *a_dram)
    S.barrier(); st.close()
    st = contextlib.ExitStack()
    keys += emit_p1b(b, st, ones, ident, *b_dram)
    S.barrier(); st.close()
    st = contextlib.ExitStack()
    keys += emit_p1c(b, st, ones, ident, xT, *c_dram)
    return b.finish(keys, extra=[st]), b.outs


def host_p1(inp, l, xT):
    xTt = tile_xT(xT)
    ma, mb, mc = host_p1a(inp, l, xTt), host_p1b(inp, l, xT), host_p1c(inp, l, xTt)
    maps = []
    for c in range(8):
        m = dict(ma[c]); m.update(mb[c]); m.update(mc[c])
        maps.append(m)
    return maps


ALPHA = (2.0 * 2) ** 0.25
LN_EPS = 1e-5
NT = 1024
HALO = 32


def build_p2():
    b = B()
    S = b.S
    xTh = b.din("xTh", [D_MODEL, NT + HALO])
    xown = b.din("xown", [NT, D_MODEL])
    ygT = b.din("ygT", [1536, NT])
    wm_d = b.din("wm", [D_MODEL, 8192])
    wd_d = b.din("wd", [D_MODEL, 1536])
    wbr_d = b.din("wbr", [2048, D_MODEL])
    wout_d = b.din("wout", [D_MODEL, D_MODEL])
    pm_d = b.din("pm", [128, 64])
    pd_d = b.din("pd", [128, 64])
    dww_d = b.din("dww", [128, 4 * 31])
    lng_d = b.din("lng", [128, D_MODEL])
    lnb_d = b.din("lnb", [128, D_MODEL])
    xo_d = b.dout("xo", [NT, D_MODEL])

    ones, ident = b.ident()
    pm = b.sb("pm", [128, 64]); pd = b.sb("pd", [128, 64]); dww = b.sb("dww", [128, 4, 31])
    b.load(pm[:], pm_d, "pm"); b.load(pd[:], pd_d, "pd")
    b.load(dww[:], dww_d.rearrange("p (c j) -> p c j", j=31), "dww")
    mixT = b.sb("mixT", [128, KCH, NT], BF16)

    stDM = contextlib.ExitStack()
    xb = b.sb("xb", [128, KCH, NT + HALO], BF16, st=stDM)
    ygd = b.sb("ygd", [128, 4, NT], BF16, st=stDM)
    S.dma("pool", xb[:], xT_tile_ap(xTh, 0, NT + HALO), writes=["xb"])

    stD = contextlib.ExitStack()
    wdb = [b.sb("wdb%d" % i, [128, KCH, 3, 128], BF16, st=stD) for i in range(2)]
    cu = b.sb("cu", [128, NT + HALO], F32, st=stD)
    t1 = b.sb("t1", [128, NT + HALO], F32, st=stD)
    t2 = b.sb("t2", [128, NT + HALO], F32, st=stD)
    cv = b.sb("cv", [128, 4, NT], F32, st=stD)
    dg = b.sb("dg", [128, 4, NT], F32, st=stD)
    sq = b.sb("sq", [128, NT], F32, st=stD)
    mean = b.sb("mean", [128, NT], F32, st=stD)
    rstd = b.sb("rstd", [128, NT], F32, st=stD)
    tiles = [(0, HALO), (HALO, 512), (HALO + 512, 512)]
    for cc in range(4):
        w = wdb[cc % 2]; wk = ("wdb", cc % 2)
        for j in range(3):
            S.dma("pool", w[:, :, j, :], w_ap(wd_d[:, j * 512 + cc * 128: j * 512 + cc * 128 + 128]), writes=[wk])
        groups = [[tiles[0]], [tiles[1], tiles[2]]]
        for grp in groups:
            for sec in range(3):
                if sec == 2 and grp[0][0] < HALO:
                    continue
                pss = [b.psum() for _ in grp]
                for k in range(KCH):
                    for (ps_, pk_), (t0, n) in zip(pss, grp):
                        b.mm(ps_[:, :n], w[:, k, sec, :], xb[:, k, t0:t0 + n], k == 0, k == KCH - 1, [wk, "xb"], [pk_])
                for (ps_, pk_), (t0, n) in zip(pss, grp):
                    if sec == 0:
                        b.act(t1[:, t0:t0 + n], ps_[:, :n], AF.Identity, [pk_, "pd"], [("t1", t0)], bias=pd[:, cc:cc + 1])
                    elif sec == 1:
                        b.act(t2[:, t0:t0 + n], ps_[:, :n], AF.Sigmoid, [pk_, "pd"], [("t2", t0)], bias=pd[:, 4 + cc:5 + cc])
                        b.tt(cu[:, t0:t0 + n], t1[:, t0:t0 + n], t2[:, t0:t0 + n], ALU.mult, [("t1", t0), ("t2", t0)], ["cu"])
                    else:
                        b.act(dg[:, cc, t0 - HALO:t0 - HALO + n], ps_[:, :n], AF.Silu, [pk_, "pd"], [("dg", cc)], bias=pd[:, 8 + cc:9 + cc])
        b.ts(cu[:, 0:HALO], cu[:, 0:HALO], pd[:, 24:25], None, ALU.mult, None, ["cu", "pd"], ["cu"])
        ck = ("cv", cc)
        b.ts(cv[:, cc, :], cu[:, 2:2 + NT], dww[:, cc, 0:1], pd[:, 12 + cc:13 + cc], ALU.mult, ALU.add, ["cu", "dww", "pd"], [ck])
        for j in range(1, 31):
            b.stt(cv[:, cc, :], cu[:, 2 + j:2 + j + NT], dww[:, cc, j:j + 1], cv[:, cc, :], ALU.mult, ALU.add, ["cu", "dww", ck], [ck])
    for tt_ in range(2):
        sl = slice(tt_ * 512, (tt_ + 1) * 512)
        ps1, k1 = b.psum()
        for cc in range(4):
            b.mm(ps1[:, :], ones[:, 0:128], cv[:, cc, sl], cc == 0, cc == 3, ["c_ones", ("cv", cc)], [k1])
        b.act(mean[:, sl], ps1[:, :], AF.Copy, [k1], ["mean"], scale=1.0 / 512)
        ps2, k2 = b.psum()
        for cc in range(4):
            b.act(sq[:, sl], cv[:, cc, sl], AF.Square, [("cv", cc)], ["sq"])
            b.mm(ps2[:, :], ones[:, 0:128], sq[:, sl], cc == 0, cc == 3, ["c_ones", "sq"], [k2])
        b.act(rstd[:, sl], ps2[:, :], AF.Copy, [k2], ["rstd"], scale=1.0 / 512)
    b.tt(sq[:], mean[:], mean[:], ALU.mult, ["mean"], ["sq"])
    b.tt(rstd[:], rstd[:], sq[:], ALU.subtract, ["rstd", "sq"], ["rstd"])
    b.act(rstd[:], rstd[:], AF.Sqrt, ["rstd"], ["rstd"], bias=LN_EPS)
    S.op("dve", lambda e: e.reciprocal(out=rstd[:], in_=rstd[:]), ["rstd"], ["rstd"])
    for cc in range(4):
        b.tt(sq[:], cv[:, cc, :], mean[:], ALU.subtract, [("cv", cc), "mean"], ["sq"])
        b.tt(sq[:], sq[:], rstd[:], ALU.mult, ["sq", "rstd"], ["sq"], eng="pool")
        b.act(sq[:], sq[:], AF.Silu, ["sq", "pd"], ["sq"], bias=pd[:, 20 + cc:21 + cc], scale=pd[:, 16 + cc:17 + cc])
        b.tt(ygd[:, cc, :], sq[:], dg[:, cc, :], ALU.mult, ["sq", ("dg", cc)], ["ygd"])
    S.barrier()
    stD.close()

    stM = contextlib.ExitStack()
    ygb = b.sb("ygb", [128, 12, NT], BF16, st=stM)
    S.dma("pool", ygb[:], ygT.rearrange("(c p) t -> p c t", p=128), writes=["ygb"])
    wmb = [b.sb("wmb%d" % i, [128, KCH, 512], BF16, st=stM) for i in range(3)]
    wbb = [b.sb("wbb%d" % i, [128, 4, 512], BF16, st=stM) for i in range(3)]
    macc = b.sb("macc", [128, 4, NT], F32, st=stM)
    mg = [b.sb("mg%d" % i, [128, 512], F32, st=stM) for i in range(2)]
    tmp = [b.sb("tmp%d" % i, [128, 512], F32, st=stM) for i in range(2)]
    it = 0
    for dcg in range(4):
        for n in range(4):
            wi = (dcg * 4 + n) % 3
            wm_, wmk = wmb[wi], ("wmb", wi)
            wb_, wbk = wbb[wi], ("wbb", wi)
            c0 = n * 2048 + dcg * 512
            S.dma("pool", wm_[:], w_ap(wm_d[:, c0:c0 + 512]), writes=[wmk])
            S.dma("pool", wb_[:], wbr_d[n * 512:(n + 1) * 512, dcg * 512:(dcg + 1) * 512].rearrange("(c p) d -> p c d", p=128), writes=[wbk])
            for j in range(4):
                psm2 = [b.psum() for _ in range(2)]
                for k in range(KCH):
                    for tt_ in range(2):
                        b.mm(psm2[tt_][0][:, :], wm_[:, k, j * 128:(j + 1) * 128], xb[:, k, HALO + tt_ * 512:HALO + (tt_ + 1) * 512],
                             k == 0, k == KCH - 1, [wmk, "xb"], [psm2[tt_][1]])
                psb2 = [b.psum() for _ in range(2)]
                for c4 in range(4):
                    for tt_ in range(2):
                        sl = slice(tt_ * 512, (tt_ + 1) * 512)
                        rhs = ygb[:, n * 4 + c4, sl] if n < 3 else ygd[:, c4, sl]
                        b.mm(psb2[tt_][0][:, :], wb_[:, c4, j * 128:(j + 1) * 128], rhs, c4 == 0, c4 == 3, [wbk, "ygb", "ygd"], [psb2[tt_][1]])
                for tt_ in range(2):
                    sl = slice(tt_ * 512, (tt_ + 1) * 512)
                    psm, km = psm2[tt_]
                    psb, kb = psb2[tt_]
                    m_ = mg[it % 2]; mk = ("mg", it % 2)
                    col = n * 16 + dcg * 4 + j
                    b.act(m_[:], psm[:, :], AF.Sigmoid, [km, "pm"], [mk], bias=pm[:, col:col + 1])
                    ak = ("macc", j, tt_)
                    if n == 0:
                        b.tt(macc[:, j, sl], m_[:], psb[:, :], ALU.mult, [mk, kb], [ak])
                    else:
                        t_ = tmp[it % 2]; tk = ("tmp", it % 2)
                        b.tt(t_[:], m_[:], psb[:, :], ALU.mult, [mk, kb], [tk])
                        b.tt(macc[:, j, sl], macc[:, j, sl], t_[:], ALU.add, [ak, tk], [ak])
                    it += 1
                    if n == 3:
                        b.cp(mixT[:, dcg * 4 + j, sl], macc[:, j, sl], [ak], ["mixT"], eng="act")
    S.barrier()
    stM.close()
    stDM.close()

    stO = contextlib.ExitStack()
    wob = b.sb("wob", [128, KCH, D_MODEL], BF16, st=stO)
    for eg in range(4):
        S.dma("pool", wob[:, :, eg * 512:(eg + 1) * 512], w_ap(wout_d[:, eg * 512:(eg + 1) * 512]), writes=[("wob", eg)])
    lng = b.sb("lng", [128, D_MODEL], F32, st=stO); lnb = b.sb("lnb", [128, D_MODEL], F32, st=stO)
    b.load(lng[:], lng_d, "lng"); b.load(lnb[:], lnb_d, "lnb")
    xt = [b.sb("xt%d" % i, [128, D_MODEL], F32, st=stO) for i in range(2)]
    z = [b.sb("z%d" % i, [128, D_MODEL], F32, st=stO) for i in range(2)]
    bst = b.sb("bst", [128, 4, 6], F32, st=stO)
    mv = b.sb("mv", [128, 4], F32, st=stO)
    outk = []
    for tt_ in range(NT // 128):
        x_ = xt[tt_ % 2]; xk = ("xt", tt_ % 2)
        z_ = z[tt_ % 2]; zk = ("z", tt_ % 2)
        b.load(x_[:], xown[tt_ * 128:(tt_ + 1) * 128, :], xk)
        pso = [b.psum() for _ in range(4)]
        for dc in range(KCH):
            for eg in range(4):
                b.mm(pso[eg][0][:, :], mixT[:, dc, tt_ * 128:(tt_ + 1) * 128], wob[:, dc, eg * 512:(eg + 1) * 512], dc == 0, dc == KCH - 1,
                     ["mixT", ("wob", eg)], [pso[eg][1]])
        for eg in range(4):
            es = slice(eg * 512, (eg + 1) * 512)
            ps, pk = pso[eg]
            b.stt(z_[:, es], x_[:, es], ALPHA, ps[:, :], ALU.mult, ALU.add, [xk, pk], [(zk, eg)])
            S.op("dve", lambda e, eg=eg, z_=z_, es=es: e.bn_stats(out=bst[:, eg, :], in_=z_[:, es]), [(zk, eg)], [("bst", eg)])
        S.op("dve", lambda e: e.bn_aggr(out=mv[:, 0:2], in_=bst[:].rearrange("p a b -> p (a b)")), [("bst", eg) for eg in range(4)], ["mv"])
        b.act(mv[:, 2:3], mv[:, 1:2], AF.Sqrt, ["mv"], ["mv2"], bias=LN_EPS)
        S.op("dve", lambda e: e.reciprocal(out=mv[:, 2:3], in_=mv[:, 2:3]), ["mv2"], ["mv2"])
        b.stt(mv[:, 3:4], mv[:, 0:1], -1.0, mv[:, 2:3], ALU.mult, ALU.mult, ["mv", "mv2"], ["mv3"])
        b.act(z_[:], z_[:], AF.Identity, [(zk, eg) for eg in range(4)] + ["mv2", "mv3"], [zk], bias=mv[:, 3:4], scale=mv[:, 2:3])
        b.tt(z_[:], z_[:], lng[:], ALU.mult, [zk, "lng"], [zk])
        b.tt(z_[:], z_[:], lnb[:], ALU.add, [zk, "lnb"], [zk] + [(zk, eg) for eg in range(4)], eng="pool")
        S.dma("sp", xo_d[tt_ * 128:(tt_ + 1) * 128, :], z_[:], reads=[zk], writes=[zk, ("xo", tt_)] + [(zk, eg) for eg in range(4)])
        outk.append(("xo", tt_))
    nc = b.finish(outk, extra=[stO])
    return nc, b.outs


def host_p2(inp, l, x_tok, ygT_full):
    w_in = inp["w_in"][l]; b_in = inp["b_in"][l]
    D0 = 1024 + 5120 + 2176
    M0 = D0 + 1536
    wm = np.ascontiguousarray(w_in[:, M0:M0 + 8192])
    wd = np.ascontiguousarray(w_in[:, D0:D0 + 1536])
    wbr = np.ascontiguousarray(inp["w_br"][l].reshape(2048, 2048))
    wout = np.ascontiguousarray(inp["w_out"][l])
    pm = np.ascontiguousarray(b_in[M0:M0 + 8192].reshape(64, 128).T)
    lng = np.ascontiguousarray(np.broadcast_to(inp["ln_g"][l][None, :], (128, 2048)))
    lnb = np.ascontiguousarray(np.broadcast_to(inp["ln_b"][l][None, :], (128, 2048)))
    dww = np.ascontiguousarray(inp["conf_dw_w"][l].T.reshape(4, 128, 31).transpose(1, 0, 2).reshape(128, 124))
    xT = x_tok.T
    maps = []
    for c in range(8):
        pd = np.zeros((128, 64), np.float32)
        bd = b_in[D0:D0 + 1536]
        for cc in range(4):
            pd[:, cc] = bd[cc * 128:(cc + 1) * 128]
            pd[:, 4 + cc] = bd[512 + cc * 128:512 + (cc + 1) * 128]
            pd[:, 8 + cc] = bd[1024 + cc * 128:1024 + (cc + 1) * 128]
            pd[:, 12 + cc] = inp["conf_dw_b"][l][cc * 128:(cc + 1) * 128]
            pd[:, 16 + cc] = inp["conf_ln_g"][l][cc * 128:(cc + 1) * 128]
            pd[:, 20 + cc] = inp["conf_ln_b"][l][cc * 128:(cc + 1) * 128]
        pd[:, 24] = 0.0 if c == 0 else 1.0
        t0 = c * NT
        xTh = np.zeros((2048, NT + HALO), np.float32)
        if c == 0:
            xTh[:, HALO:] = xT[:, 0:NT]
        else:
            xTh[:] = xT[:, t0 - HALO:t0 + NT]
        maps.append({"xTh": xTh, "xown": np.ascontiguousarray(x_tok[t0:t0 + NT]), "ygT": np.ascontiguousarray(ygT_full[:, t0:t0 + NT]),
                     "wm": wm, "wd": wd, "wbr": wbr, "wout": wout, "pm": pm, "pd": pd, "dww": dww, "lng": lng, "lnb": lnb})
    return maps


CORES = list(range(8))


def kernel(**inputs):
    inp = {k: np.asarray(v) for k, v in inputs.items()}
    x = np.ascontiguousarray(inp["x"][0], dtype=np.float32)
    for l in range(2):
        xT = np.ascontiguousarray(x.T)
        nc, _ = build_p1()
        r1 = run_bass_kernel_spmd(nc, host_p1(inp, l, xT), core_ids=CORES)
        ygA = np.concatenate([r["ygA"] for r in r1.results], axis=0)
        ygB = gather_p1b(r1.results)
        ygC = np.concatenate([r["ygC"] for r in r1.results], axis=0)
        ygT = np.ascontiguousarray(np.concatenate([ygA, ygB, ygC], axis=0))
        nc, _ = build_p2()
        r2 = run_bass_kernel_spmd(nc, host_p2(inp, l, x, ygT), core_ids=CORES)
        x = np.ascontiguousarray(np.concatenate([r["xo"] for r in r2.results], axis=0))
    return x[None].astype(np.float32)
```
